# Optimizing a Trainium2 kernel written in Bass

```python
import math
import jax, jax.numpy as jnp
from jax import lax
import numpy as np

D_MODEL = 2048
BATCH = 2
SEQ = 4096
DEPTH = 1
DEC_BATCH = 128
DEC_SEQ = 8
PAST_LEN = 8192
PAGE_SIZE = 128

HEAD_DIM = 64
N_Q_HEADS = 16
N_KV_HEADS = 4
GQA = N_Q_HEADS // N_KV_HEADS
ATTN_WIDTH = N_Q_HEADS * HEAD_DIM
KV_WIDTH = N_KV_HEADS * HEAD_DIM
WINDOW = 128
ATTN_BLOCK = 128
ROPE_THETA = 10000.0
LRU_WIDTH = D_MODEL - ATTN_WIDTH
LRU_BLOCKS = 16
LRU_BLOCK = LRU_WIDTH // LRU_BLOCKS
CONV_W = 4
LRU_C = 8.0
IN_WIDTH = ATTN_WIDTH + 2 * KV_WIDTH + 2 * LRU_WIDTH
N_KEYS = 128
N_EXPERTS = N_KEYS * N_KEYS
PEER_HEADS = 8
PEER_TOPK = 16
D_KEY = 256
PEER_CHUNK = 128
EPS = 1e-6
NEG_INF = -1e30

kernel_name = "hymba_swa_sink_rglru_peer_step"


def _rmsnorm(x, g):
    xf = x.astype(jnp.float32)
    y = xf * lax.rsqrt(jnp.mean(xf * xf, axis=-1, keepdims=True) + EPS)
    return (y * g.astype(jnp.float32)).astype(x.dtype)


def _rope(x, pos):
    half = HEAD_DIM // 2
    inv = jnp.exp(-math.log(ROPE_THETA) * jnp.arange(half, dtype=jnp.float32) * (2.0 / HEAD_DIM))
    ang = pos[:, None] * inv[None, :]
    cos = jnp.cos(ang)[:, None, :]
    sin = jnp.sin(ang)[:, None, :]
    xf = x.astype(jnp.float32)
    x1, x2 = xf[..., :half], xf[..., half:]
    return jnp.concatenate([x1 * cos - x2 * sin, x2 * cos + x1 * sin], axis=-1).astype(x.dtype)


def _split_in(z):
    cuts = np.cumsum([ATTN_WIDTH, KV_WIDTH, KV_WIDTH, LRU_WIDTH]).tolist()
    return jnp.split(z, cuts, axis=-1)


def _qk_prep(q, k, v, pos, g_q, g_k):
    b, t = q.shape[:2]
    q = _rope(_rmsnorm(q.reshape(b, t, N_Q_HEADS, HEAD_DIM), g_q), pos)
    k = _rope(_rmsnorm(k.reshape(b, t, N_KV_HEADS, HEAD_DIM), g_k), pos)
    return q, k, v.reshape(b, t, N_KV_HEADS, HEAD_DIM)


def _sink_softmax(logits, mask, sink):
    logits = jnp.where(mask, logits, NEG_INF)
    m = jnp.maximum(jnp.max(logits, axis=-1, keepdims=True), sink)
    e = jnp.exp(logits - m)
    return e / (jnp.sum(e, axis=-1, keepdims=True) + jnp.exp(sink - m))


def _attn_banded(q, k, v, sink):
    b, t = q.shape[:2]
    nb = t // ATTN_BLOCK
    qb = q.reshape(b, nb, ATTN_BLOCK, N_KV_HEADS, GQA, HEAD_DIM)
    kb = k.reshape(b, nb, ATTN_BLOCK, N_KV_HEADS, HEAD_DIM)
    vb = v.reshape(b, nb, ATTN_BLOCK, N_KV_HEADS, HEAD_DIM)
    shift = lambda a: jnp.concatenate([jnp.zeros_like(a[:, :1]), a[:, :-1]], axis=1)
    kc = jnp.concatenate([shift(kb), kb], axis=2)
    vc = jnp.concatenate([shift(vb), vb], axis=2)
    logits = jnp.einsum('bnqkgd,bnskd->bnkgqs', qb, kc).astype(jnp.float32) * (HEAD_DIM ** -0.5)
    i = jnp.arange(ATTN_BLOCK)[:, None]
    j = jnp.arange(2 * ATTN_BLOCK)[None, :]
    diff = i + ATTN_BLOCK - j
    band = (diff >= 0) & (diff < WINDOW)
    mask = band[None] & ((jnp.arange(nb)[:, None, None] > 0) | (j[None] >= ATTN_BLOCK))
    p = _sink_softmax(logits, mask[:, None, None], sink)
    o = jnp.einsum('bnkgqs,bnskd->bnqkgd', p.astype(v.dtype), vc)
    return o.reshape(b, t, ATTN_WIDTH)


def _attn_window_cache(q, k, v, ck, cv, sink):
    b, s = q.shape[:2]
    w = ck.shape[1]
    kall = jnp.concatenate([ck.astype(k.dtype), k], axis=1)
    vall = jnp.concatenate([cv.astype(v.dtype), v], axis=1)
    qpos = PAST_LEN + jnp.arange(s)
    kpos = PAST_LEN - w + jnp.arange(w + s)
    diff = qpos[:, None] - kpos[None, :]
    mask = (diff >= 0) & (diff < WINDOW)
    qg = q.reshape(b, s, N_KV_HEADS, GQA, HEAD_DIM)
    logits = jnp.einsum('bqkgd,bskd->bkgqs', qg, kall).astype(jnp.float32) * (HEAD_DIM ** -0.5)
    p = _sink_softmax(logits, mask, sink)
    o = jnp.einsum('bkgqs,bskd->bqkgd', p.astype(v.dtype), vall)
    return o.reshape(b, s, ATTN_WIDTH), kall[:, -w:], vall[:, -w:]


def _rg_lru(xr, yg, conv_prev, h0, conv_w, conv_b, w_a, b_a, w_x, b_x, lam):
    b, t, _ = xr.shape
    xp = jnp.concatenate([conv_prev.astype(xr.dtype), xr], axis=1)
    xc = sum(xp[:, j:j + t] * conv_w[j] for j in range(CONV_W)) + conv_b
    xb = xc.reshape(b, t, LRU_BLOCKS, LRU_BLOCK)
    r = jax.nn.sigmoid(jnp.einsum('btnd,nde->btne', xb, w_a) + b_a).reshape(b, t, LRU_WIDTH)
    ig = jax.nn.sigmoid(jnp.einsum('btnd,nde->btne', xb, w_x) + b_x).reshape(b, t, LRU_WIDTH)
    log_a = -LRU_C * r.astype(jnp.float32) * jax.nn.softplus(-lam.astype(jnp.float32))
    a = jnp.exp(log_a)
    u = jnp.sqrt(-jnp.expm1(2.0 * log_a)) * (ig * xc).astype(jnp.float32)
    u = u.at[:, 0].add(a[:, 0] * h0.astype(jnp.float32))
    comb = lambda l, rr: (l[0] * rr[0], rr[0] * l[1] + rr[1])
    _, h = lax.associative_scan(comb, (a, u), axis=1)
    y = h.astype(xr.dtype) * jax.nn.gelu(yg)
    return y, xp[:, -(CONV_W - 1):], h[:, -1].astype(xr.dtype)


def _peer(h, w_pq, sub_keys, peer_u, peer_v):
    n = h.shape[0]
    pad = (-n) % PEER_CHUNK
    hb = jnp.pad(h, ((0, pad), (0, 0))).reshape(-1, PEER_CHUNK, D_MODEL)

    def one(hc):
        c = hc.shape[0]
        q = (hc @ w_pq).reshape(c, PEER_HEADS, 2, D_KEY // 2)
        s = jnp.einsum('nhcd,hckd->nhck', q, sub_keys).astype(jnp.float32)
        sv, si = lax.top_k(s, PEER_TOPK)
        cand = (sv[:, :, 0, :, None] + sv[:, :, 1, None, :]).reshape(c, PEER_HEADS, -1)
        cid = (si[:, :, 0, :, None] * N_KEYS + si[:, :, 1, None, :]).reshape(c, PEER_HEADS, -1)
        best, pos = lax.top_k(cand, PEER_TOPK)
        eid = jnp.take_along_axis(cid, pos, axis=-1)
        g = jax.nn.softmax(best, axis=-1)
        act = jax.nn.gelu(jnp.einsum('nd,nhkd->nhk', hc, peer_u[eid]).astype(jnp.float32))
        return jnp.einsum('nhk,nhkd->nd', (g * act).astype(hc.dtype), peer_v[eid])

    return lax.map(one, hb).reshape(-1, D_MODEL)[:n]


def _layer(x, pos0, kv_cache, conv_prev, h0, lw):
    (g_mix, w_in, g_q, g_k, sinks, conv_w, conv_b, w_a, b_a, w_x, b_x, lam,
     g_ao, g_lo, w_out, g_ffn, w_pq, sub_keys, peer_u, peer_v) = lw
    b, t, _ = x.shape
    hn = _rmsnorm(x, g_mix)
    q, k, v, xr, yg = _split_in(hn @ w_in)
    pos = pos0 + jnp.arange(t, dtype=jnp.float32)
    q, k, v = _qk_prep(q, k, v, pos, g_q, g_k)
    sink = sinks.astype(jnp.float32).reshape(N_KV_HEADS, GQA, 1, 1)
    if kv_cache is None:
        ao = _attn_banded(q, k, v, sink)
        k_rows, v_rows = k[:, -WINDOW:], v[:, -WINDOW:]
    else:
        ao, k_rows, v_rows = _attn_window_cache(q, k, v, kv_cache[0], kv_cache[1], sink)
    lo, conv_new, h_new = _rg_lru(xr, yg, conv_prev, h0, conv_w, conv_b, w_a, b_a, w_x, b_x, lam)
    mixed = jnp.concatenate([_rmsnorm(ao, g_ao), _rmsnorm(lo, g_lo)], axis=-1) @ w_out
    x = x + mixed
    f = _peer(_rmsnorm(x, g_ffn).reshape(b * t, D_MODEL), w_pq, sub_keys, peer_u, peer_v)
    x = x + f.reshape(b, t, D_MODEL)
    return x, k_rows, v_rows, conv_new, h_new


def setup_inputs(seed: int = 0) -> dict:
    key = jax.random.key(seed)
    ks = jax.random.split(key, 32)
    nrm = lambda k, shape, s: jax.random.normal(k, shape, jnp.float32) * s
    gain = lambda k, shape: 1.0 + 0.02 * jax.random.normal(k, shape, jnp.float32)
    cache_len = min(WINDOW, PAST_LEN)
    u = jax.random.uniform(ks[12], (DEPTH, LRU_WIDTH), jnp.float32, 0.9, 0.999)
    a0 = u ** (1.0 / LRU_C)
    return {
        "x_prompt": nrm(ks[0], (BATCH, SEQ, D_MODEL), 1.0),
        "x_sample": nrm(ks[1], (DEC_BATCH, DEC_SEQ, D_MODEL), 1.0),
        "cache_k": nrm(ks[2], (DEPTH, DEC_BATCH, cache_len, N_KV_HEADS, HEAD_DIM), 1.0),
        "cache_v": nrm(ks[3], (DEPTH, DEC_BATCH, cache_len, N_KV_HEADS, HEAD_DIM), 1.0),
        "state_conv": nrm(ks[4], (DEPTH, DEC_BATCH, CONV_W - 1, LRU_WIDTH), 1.0),
        "state_h": nrm(ks[5], (DEPTH, DEC_BATCH, LRU_WIDTH), 0.5),
        "norm_mix_g": gain(ks[6], (DEPTH, D_MODEL)),
        "w_in": nrm(ks[7], (DEPTH, D_MODEL, IN_WIDTH), D_MODEL ** -0.5),
        "q_norm_g": gain(ks[8], (DEPTH, HEAD_DIM)),
        "k_norm_g": gain(ks[9], (DEPTH, HEAD_DIM)),
        "attn_sinks": nrm(ks[10], (DEPTH, N_Q_HEADS), 1.0),
        "conv_w": nrm(ks[11], (DEPTH, CONV_W, LRU_WIDTH), CONV_W ** -0.5),
        "conv_b": nrm(ks[13], (DEPTH, LRU_WIDTH), 0.01),
        "w_rec_gate": nrm(ks[14], (DEPTH, LRU_BLOCKS, LRU_BLOCK, LRU_BLOCK), LRU_BLOCK ** -0.5),
        "b_rec_gate": nrm(ks[15], (DEPTH, LRU_BLOCKS, LRU_BLOCK), 0.01),
        "w_in_gate": nrm(ks[16], (DEPTH, LRU_BLOCKS, LRU_BLOCK, LRU_BLOCK), LRU_BLOCK ** -0.5),
        "b_in_gate": nrm(ks[17], (DEPTH, LRU_BLOCKS, LRU_BLOCK), 0.01),
        "lru_lambda": jnp.log(a0) - jnp.log1p(-a0),
        "attn_out_g": gain(ks[18], (DEPTH, ATTN_WIDTH)),
        "lru_out_g": gain(ks[19], (DEPTH, LRU_WIDTH)),
        "w_out": nrm(ks[20], (DEPTH, D_MODEL, D_MODEL), D_MODEL ** -0.5),
        "norm_ffn_g": gain(ks[21], (DEPTH, D_MODEL)),
        "w_peer_q": nrm(ks[22], (DEPTH, D_MODEL, PEER_HEADS * D_KEY), D_MODEL ** -0.5),
        "peer_sub_keys": nrm(ks[23], (DEPTH, PEER_HEADS, 2, N_KEYS, D_KEY // 2), (D_KEY // 2) ** -0.5),
        "peer_u": nrm(ks[24], (DEPTH, N_EXPERTS, D_MODEL), D_MODEL ** -0.5),
        "peer_v": nrm(ks[25], (DEPTH, N_EXPERTS, D_MODEL), PEER_TOPK ** -0.5),
    }


def reference(x_prompt, x_sample, cache_k, cache_v, state_conv, state_h,
              norm_mix_g, w_in, q_norm_g, k_norm_g, attn_sinks, conv_w, conv_b,
              w_rec_gate, b_rec_gate, w_in_gate, b_in_gate, lru_lambda,
              attn_out_g, lru_out_g, w_out, norm_ffn_g, w_peer_q, peer_sub_keys,
              peer_u, peer_v):
    xp, xs = x_prompt, x_sample
    kp, vp, cp, hp, kss, vss, css, hss = [], [], [], [], [], [], [], []
    bp = xp.shape[0]
    for l in range(DEPTH):
        lw = (norm_mix_g[l], w_in[l], q_norm_g[l], k_norm_g[l], attn_sinks[l], conv_w[l], conv_b[l],
              w_rec_gate[l], b_rec_gate[l], w_in_gate[l], b_in_gate[l], lru_lambda[l],
              attn_out_g[l], lru_out_g[l], w_out[l], norm_ffn_g[l], w_peer_q[l], peer_sub_keys[l],
              peer_u[l], peer_v[l])
        xp, k1, v1, c1, h1 = _layer(
            xp, 0.0, None,
            jnp.zeros((bp, CONV_W - 1, LRU_WIDTH), xp.dtype),
            jnp.zeros((bp, LRU_WIDTH), xp.dtype), lw)
        xs, k2, v2, c2, h2 = _layer(
            xs, float(PAST_LEN), (cache_k[l], cache_v[l]), state_conv[l], state_h[l], lw)
        kp.append(k1); vp.append(v1); cp.append(c1); hp.append(h1)
        kss.append(k2); vss.append(v2); css.append(c2); hss.append(h2)
    return (xp, xs,
            jnp.stack(kp), jnp.stack(vp), jnp.stack(cp), jnp.stack(hp),
            jnp.stack(kss), jnp.stack(vss), jnp.stack(css), jnp.stack(hss))
```

```python
import math
import sys
from contextlib import ExitStack
import numpy as np
import concourse.bass as bass
import concourse.mybir as mybir
from concourse.bass_utils import run_bass_kernel_spmd

F32 = mybir.dt.float32
BF16 = mybir.dt.bfloat16
U32 = mybir.dt.uint32
AF = mybir.ActivationFunctionType
ALU = mybir.AluOpType
AX = mybir.AxisListType

NCORES = 8
D = 2048
DC = 16
EPS = 1e-6
NPRE = 24
NOWN = 8
THR_TOL = -4e-6


class Prog:
    ENGS = ("pe", "act", "dve", "pool", "sp")
    EPOCH = 50000
    NDMA_SEM = 8

    def __init__(self, nc):
        self.nc = nc
        self.ops = []
        self.last_w = {}
        self.readers = {}
        self.barrier_deps = set()
        self.last_on_eng = {}
        self.dma_rr = {q: 0 for q in ("sp", "act", "pool")}
        self.dma_last = {}
        self.dma_last_bar = {}
        self.dma_cnt = {}

    def _nm(self, ap):
        sp = str(ap.space)
        if sp == "SB":
            return self.sb.lookup(ap)
        if sp == "PSUM":
            return self.ps.lookup(ap)
        return "dram:" + ap.name

    def begin_stream(self):
        self.pending = []
        self.marks = []

    def mark_split(self):
        if getattr(self, "pending", None) is not None:
            self.marks.append(len(self.pending))

    def commit_prefix_pairs(self, streams, lag=2):
        pairs = [streams[i:i + 2] for i in range(0, len(streams), 2)]

        def front(pair):
            out = []
            for k in range(3):
                for ops, m in pair:
                    out.extend(ops[(0 if k == 0 else m[k - 1]):m[k]])
            return out

        def back(pair):
            out = []
            ba = pair[0][0][pair[0][1][2]:]
            bb = pair[1][0][pair[1][1][2]:] if len(pair) > 1 else []
            ia = ib = 0
            while ia < len(ba) or ib < len(bb):
                if ia < len(ba):
                    out.append(ba[ia])
                    ia += 1
                if ib < len(bb) and (ia >= ib + 1 + lag or ia >= len(ba)):
                    out.append(bb[ib])
                    ib += 1
            return out

        def run(ops):
            for o in ops:
                self._add(*o)
        run(front(pairs[0]))
        if len(pairs) > 1:
            run(front(pairs[1]))
        for p in range(len(pairs)):
            bl = back(pairs[p])
            fl = front(pairs[p + 2]) if p + 2 < len(pairs) else []
            nb, nf = len(bl), len(fl)
            jf = 0
            for ib_, o in enumerate(bl):
                self._add(*o)
                tgt = (ib_ + 1) * nf // nb if nb else nf
                while jf < tgt:
                    self._add(*fl[jf])
                    jf += 1
            while jf < nf:
                self._add(*fl[jf])
                jf += 1

    def end_stream(self):
        p, self.pending = self.pending, None
        return p, list(self.marks)

    def commit_spread(self, streams):
        def run(ops):
            for o in ops:
                self._add(*o)
        run(streams[0][0][:streams[0][1][0]])
        for j, (ops, m) in enumerate(streams):
            rl = ops[m[0]:]
            fl = streams[j + 1][0][:streams[j + 1][1][0]] if j + 1 < len(streams) else []
            nb, nf = len(rl), len(fl)
            jf = 0
            for ib_, o in enumerate(rl):
                self._add(*o)
                tgt = (ib_ + 1) * nf // nb if nb else nf
                while jf < tgt:
                    self._add(*fl[jf])
                    jf += 1
            while jf < nf:
                self._add(*fl[jf])
                jf += 1

    def _add(self, eng, fn, r, w, kind, extra=None):
        if getattr(self, "pending", None) is not None and extra != "commit":
            fr = sys._getframe(2)
            self.pending.append((eng, fn, r, w, kind, "commit"))
            return -1
        idx = len(self.ops)
        deps = set(self.barrier_deps)
        for k in r:
            if k in self.last_w:
                deps.add(self.last_w[k])
        for k in w:
            if k in self.last_w:
                deps.add(self.last_w[k])
            for rr in self.readers.get(k, ()):
                deps.add(rr)
        fr = sys._getframe(2)
        lines = []
        while fr is not None and len(lines) < 4:
            lines.append(fr.f_lineno)
            fr = fr.f_back
        self._nobar = (kind == "d_bg")
        if kind == "d_bg":
            kind = "d"
        op = dict(eng=eng, fn=fn, deps=deps, kind=kind, line=lines)
        if kind == "d":
            slot = self.dma_rr[eng]
            self.dma_rr[eng] = (slot + 1) % self.NDMA_SEM
            key = (eng, slot)
            if key in self.dma_last:
                deps.add(self.dma_last[key])
            self.dma_last[key] = idx
            if not getattr(self, "_nobar", False):
                self.dma_last_bar[key] = idx
            self.dma_cnt[key] = self.dma_cnt.get(key, 0) + 1
            op["dsem"] = key
            op["dval"] = 16 * self.dma_cnt[key]
        else:
            self.last_on_eng[eng] = idx
        deps.discard(idx)
        for k in r:
            self.readers.setdefault(k, []).append(idx)
        for k in w:
            self.last_w[k] = idx
            self.readers[k] = []
        self.ops.append(op)
        return idx

    def barrier(self):
        s = set(self.last_on_eng.values())
        s.update(self.dma_last_bar.values())
        self.barrier_deps = s

    def op(self, eng, fn, ins, outs, r=None, w=None):
        rr = list(r) if r is not None else [self._nm(a) for a in ins]
        ww = list(w) if w is not None else [self._nm(a) for a in outs]
        return self._add(eng, fn, rr, ww, "c")

    def dma(self, q, out, in_, r=None, w=None, background=False):
        rr = list(r) if r is not None else [self._nm(in_)]
        ww = list(w) if w is not None else [self._nm(out)]
        return self._add(q, lambda e: e.dma_start(out=out, in_=in_), rr, ww, "d_bg" if background else "d")

    def tt(self, eng, out, in0, in1, op, **kw):
        return self.op(eng, lambda e: e.tensor_tensor(out=out, in0=in0, in1=in1, op=op), [in0, in1], [out], **kw)

    def ts(self, eng, out, in0, s1, s2, op0, op1=None, accum=None, **kw):
        ins = [in0] + [s for s in (s1, s2) if not isinstance(s, (int, float)) and s is not None]
        outs = [out] + ([accum] if accum is not None else [])

        def fn(e):
            kws = {}
            if op1 is not None:
                kws["op1"] = op1
            if accum is not None:
                kws["accum_out"] = accum
            return e.tensor_scalar(out=out, in0=in0, scalar1=s1, scalar2=s2, op0=op0, **kws)
        return self.op(eng, fn, ins, outs, **kw)

    def stt(self, out, in0, scalar, in1, op0, op1, **kw):
        ins = [in0, in1] + ([scalar] if not isinstance(scalar, (int, float)) else [])
        return self.op("dve", lambda e: e.scalar_tensor_tensor(out=out, in0=in0, scalar=scalar, in1=in1, op0=op0, op1=op1),
                       ins, [out], **kw)

    def actf(self, out, in_, func, bias=None, scale=None, accum=None, **kw):
        ins = [in_] + [s for s in (bias, scale) if s is not None and not isinstance(s, (int, float))]
        outs = [out] + ([accum] if accum is not None else [])

        def fn(e):
            kws = {}
            if bias is not None:
                kws["bias"] = bias
            if scale is not None:
                kws["scale"] = scale
            if accum is not None:
                kws["accum_out"] = accum
            return e.activation(out=out, in_=in_, func=func, **kws)
        return self.op("act", fn, ins, outs, **kw)

    def copy(self, eng, out, in_, **kw):
        if eng == "act":
            return self.op("act", lambda e: e.copy(out=out, in_=in_), [in_], [out], **kw)
        return self.op(eng, lambda e: e.tensor_copy(out=out, in_=in_), [in_], [out], **kw)

    def mm(self, out, lhsT, rhs, start, stop, **kw):
        return self.op("pe", lambda e: e.matmul(out, lhsT=lhsT, rhs=rhs, start=start, stop=stop), [lhsT, rhs], [out], **kw)

    def tr(self, out, in_, ident, **kw):
        return self.op("pe", lambda e: e.transpose(out=out, in_=in_, identity=ident), [in_, ident], [out], **kw)

    def scan(self, out, d0, d1, init, **kw):
        ins = [d0, d1] + ([init] if not isinstance(init, (int, float)) else [])
        return self.op("dve", lambda e: e.tensor_tensor_scan(out=out, data0=d0, data1=d1, initial=init, op0=ALU.mult, op1=ALU.add),
                       ins, [out], **kw)

    def max8(self, out, in_, **kw):
        return self.op("dve", lambda e: e.max(out=out, in_=in_), [in_], [out], **kw)

    def maxidx(self, out, in_max, in_values, **kw):
        return self.op("dve", lambda e: e.max_index(out=out, in_max=in_max, in_values=in_values), [in_max, in_values], [out], **kw)

    def matchrep(self, out, in_to_replace, in_values, imm, **kw):
        return self.op("dve", lambda e: e.match_replace(out=out, in_to_replace=in_to_replace, in_values=in_values, imm_value=imm),
                       [in_to_replace, in_values], [out], **kw)

    def recip(self, out, in_, **kw):
        return self.op("dve", lambda e: e.reciprocal(out=out, in_=in_), [in_], [out], **kw)

    def memset(self, eng, ap, val, **kw):
        return self.op(eng, lambda e: e.memset(ap, val), [], [ap], **kw)

    def reduce(self, out, in_, op, **kw):
        return self.op("dve", lambda e: e.tensor_reduce(out=out, in_=in_, axis=AX.X, op=op), [in_], [out], **kw)

    def simulate(self):
        sems = {}
        pos = {e: 0 for e in self.ENGS}
        progress = True
        while progress:
            progress = False
            for e in self.ENGS:
                tr = self.trace[e]
                while pos[e] < len(tr):
                    ev = tr[pos[e]]
                    if ev[0] == "w":
                        if sems.get(ev[1], 0) >= ev[2]:
                            pos[e] += 1
                            progress = True
                        else:
                            break
                    else:
                        sems[ev[1]] = sems.get(ev[1], 0) + ev[2]
                        pos[e] += 1
                        progress = True
        stuck = {e: (pos[e], len(self.trace[e])) for e in self.ENGS if pos[e] < len(self.trace[e])}
        if stuck:
            for e in stuck:
                ev = self.trace[e][pos[e]]
                print(f"[simulate] DEADLOCK: {e} blocked at event {pos[e]}/{len(self.trace[e])}: {ev} cur={sems.get(ev[1], 0)}")
            raise RuntimeError("deadlock in emitted program")
        print("[simulate] no deadlock")

    def emit(self, es):
        nc = self.nc
        ops = self.ops
        needed = set()
        for o in ops:
            for d in o["deps"]:
                needed.add(d)
        cnt = {e: 0 for e in self.ENGS}
        for i, o in enumerate(ops):
            if o["kind"] == "c" and i in needed:
                cnt[o["eng"]] += 1
                o["ticket"] = cnt[o["eng"]]
        csem = {}
        for e in self.ENGS:
            n_ep = (cnt[e] + self.EPOCH - 1) // self.EPOCH
            csem[e] = [es.enter_context(nc.semaphore(f"c_{e}_{k}")) for k in range(max(1, n_ep))]
        dsem = {}
        for key in self.dma_cnt:
            dsem[key] = es.enter_context(nc.semaphore(f"d_{key[0]}_{key[1]}"))
        block = es.enter_context(nc.Block())
        EP = self.EPOCH
        self.trace = {e: [] for e in self.ENGS}

        def emit_engine(eng, e):
            waited = {}

            def wait(semkey, sem, val):
                if waited.get(semkey, 0) >= val:
                    return
                waited[semkey] = val
                e.wait_ge(sem, val)
                self.trace[eng].append(("w", semkey, val))

            for i, o in enumerate(ops):
                if o["eng"] != eng:
                    continue
                for d in sorted(o["deps"]):
                    od = ops[d]
                    if od["kind"] == "d":
                        wait(("d",) + od["dsem"], dsem[od["dsem"]], od["dval"])
                    else:
                        if od["eng"] == "pe" and eng == "pe" and o["kind"] == "c":
                            continue
                        t = od["ticket"]
                        ep = (t - 1) // EP
                        wait(("c", od["eng"], ep), csem[od["eng"]][ep], (t - 1) % EP + 1)
                try:
                    ins = o["fn"](e)
                except Exception:
                    print(f"[emit] failure in op recorded at lines {o['line']} on {eng}")
                    raise
                if o["kind"] == "d":
                    ins.then_inc(dsem[o["dsem"]], 16)
                    self.trace[eng].append(("i", ("d",) + o["dsem"], 16, i))
                elif "ticket" in o:
                    ep = (o["ticket"] - 1) // EP
                    ins.then_inc(csem[eng][ep], 1)
                    self.trace[eng].append(("i", ("c", eng, ep), 1, i))
            if eng == "sp":
                for key, c in self.dma_cnt.items():
                    e.wait_ge(dsem[key], 16 * c)

        @block.sync
        def _(e):
            emit_engine("sp", e)

        @block.scalar
        def _(e):
            emit_engine("act", e)

        @block.vector
        def _(e):
            emit_engine("dve", e)

        @block.gpsimd
        def _(e):
            emit_engine("pool", e)

        @block.tensor
        def _(e):
            emit_engine("pe", e)


class Arena:
    def __init__(self, prog, tensor, nbytes, tag):
        self.P = prog
        self.t = tensor
        self.n = nbytes
        self.tag = tag
        self.top = 0
        self.regions = []
        self.peak = 0
        self.gen = 0

    def alloc(self, name, free_shape, dtype, align=64):
        esz = {F32: 4, BF16: 2, U32: 4}[dtype]
        n = esz
        for s in free_shape:
            n *= s
        start = (self.top + align - 1) // align * align
        end = start + n
        if end > self.n:
            raise RuntimeError(f"{self.tag} arena overflow allocating {name}: {end} > {self.n}")
        self.top = end
        self.peak = max(self.peak, end)
        self.gen += 1
        self.regions.append((start, end, f"{self.tag}:{name}:{self.gen}"))
        ap = self.t[:, start // 4:(end + 3) // 4]
        if dtype != F32:
            ap = ap.bitcast(dtype)[:, 0:n // esz]
        if len(free_shape) > 1:
            names = " ".join(f"d{i}" for i in range(len(free_shape)))
            kw = {f"d{i}": free_shape[i] for i in range(1, len(free_shape))}
            ap = ap.rearrange(f"p ({names}) -> p {names}", **kw)
        return ap

    def mark(self):
        return (self.top, len(self.regions))

    def release(self, m):
        self.top, nreg = m
        del self.regions[nreg:]
        self.P.barrier()

    def lookup(self, ap):
        esz = {F32: 4, BF16: 2, U32: 4}[ap.dtype]
        pitch = ap.ap[0][0]
        off = (ap.offset % pitch) * esz
        for (s, e, nme) in reversed(self.regions):
            if s <= off < e:
                return nme
        raise RuntimeError(f"{self.tag} lookup failed for offset {off}")


def bc(ap, shape):
    return ap.broadcast_to(list(shape))


class WNames:
    def __init__(self, names, mode, blk=512):
        self.names, self.mode, self.blk = names, mode, blk

    def at(self, dc, col):
        return self.names[dc] if self.mode == "dc" else self.names[col // self.blk]


def build(stage="full"):
    nc = bass.Bass("TRN2", target_bir_lowering=False)

    def din(name, shape):
        return nc.dram_tensor(name, list(shape), F32, kind="ExternalInput").ap()

    def dout(name, shape):
        return nc.dram_tensor(name, list(shape), F32, kind="ExternalOutput").ap()

    I = {}
    for name, shape in [
        ("xp", (1024, D)), ("xs", (128, D)), ("xpre", (NPRE * 128, D)), ("pmask", (128, NPRE)),
        ("rope", (128, 10, 64)), ("amask", (128, 3, 128)), ("smask", (128, 17, 128)),
        ("ck", (16, 128, 256)), ("cv", (16, 128, 256)), ("sconv", (48, 1024)), ("sh", (16, 1024)),
        ("w_in", (D, 3584)), ("w_out", (D, D)), ("w_pq", (D, D)), ("uT", (128, 128, 16, 128)),
        ("pv", (16384, D)), ("skT", (128, 16, 128)),
        ("gmixT", (128, 16)), ("gffnT", (128, 16)), ("gaoT", (128, 8)), ("gloT", (128, 8)),
        ("gqk", (128, 1280)), ("sinks", (128, 16)), ("cw", (128, 8, 4)), ("cb", (128, 8)),
        ("ba", (128, 8)), ("bx", (128, 8)), ("lam", (128, 8)), ("bda", (128, 8, 128)), ("bdx", (128, 8, 128)),
        ("identf", (128, 128)), ("iota", (128, 128)),
    ]:
        I[name] = din(name, shape)
    O = {}
    for name, shape in [
        ("yp", (1024, D)), ("ys", (128, D)), ("kp", (128, 256)), ("vp", (128, 256)), ("cp", (3, 1024)),
        ("hp", (1, 1024)), ("ks", (16, 128, 256)), ("vs", (16, 128, 256)), ("cs", (48, 1024)), ("hs", (16, 1024)),
    ]:
        O[name] = dout(name, shape)

    ub16 = nc.dram_tensor("ub16", [128, 128, 2048], BF16, kind="Internal").ap()
    vb16 = nc.dram_tensor("vb16", [16384, 2048], BF16, kind="Internal").ap()
    P = Prog(nc)
    es = ExitStack()
    win16 = nc.dram_tensor("win16", [D, 3584], BF16, kind="Internal").ap()
    wout16 = nc.dram_tensor("wout16", [D, D], BF16, kind="Internal").ap()
    wpq16 = nc.dram_tensor("wpq16", [D, D], BF16, kind="Internal").ap()
    W16 = {"w_in": win16, "w_out": wout16, "w_pq": wpq16}
    conv_jobs = []
    for wn_ in ("w_in", "w_out", "w_pq"):
        for dc in range(DC):
            conv_jobs.append((W16[wn_][dc * 128:(dc + 1) * 128, :], I[wn_][dc * 128:(dc + 1) * 128, :], f"{wn_}16#{dc}"))
    for ic in range(128):
        conv_jobs.append((ub16[ic], I["uT"][ic].rearrange("p a b -> p (a b)"), f"ub16#{ic}"))
        conv_jobs.append((vb16[ic * 128:(ic + 1) * 128, :], I["pv"][ic * 128:(ic + 1) * 128, :], f"vb16#{ic}"))
    conv_jobs.reverse()

    def emit_conv(n):
        for _ in range(n):
            if conv_jobs:
                o_, i_, nm_ = conv_jobs.pop()
                P.dma("pool", o_, i_, r=[], w=[nm_], background=True)
    SB_BYTES = 212480
    sb_t = es.enter_context(nc.sbuf_tensor("sb_arena", [128, SB_BYTES // 4], F32))
    ps_t = es.enter_context(nc.psum_tensor("ps_arena", [128, 4096], F32))
    SB = Arena(P, sb_t, SB_BYTES, "sb")
    PS = Arena(P, ps_t, 16384, "ps")
    P.sb, P.ps = SB, PS

    identf = SB.alloc("identf", [128], F32)
    identb = SB.alloc("identb", [128], BF16)
    iota = SB.alloc("iota", [128], F32)
    iotab = SB.alloc("iotab", [128], BF16)
    gmixT = SB.alloc("gmixT", [16], F32)
    gffnT = SB.alloc("gffnT", [16], F32)
    gaoT = SB.alloc("gaoT", [8], F32)
    gloT = SB.alloc("gloT", [8], F32)
    esink = SB.alloc("esink", [16], F32)
    cw = SB.alloc("cw", [8, 4], F32)
    cb = SB.alloc("cb", [8], F32)
    ba = SB.alloc("ba", [8], F32)
    bx = SB.alloc("bx", [8], F32)
    nsp = SB.alloc("nsp", [8], F32)
    bda = SB.alloc("bda", [8, 128], BF16)
    bdx = SB.alloc("bdx", [8, 128], BF16)
    amask = SB.alloc("amask", [3, 128], BF16)
    rope = SB.alloc("rope", [10, 64], F32)
    pmask = SB.alloc("pmask", [NPRE], F32)
    onesb = SB.alloc("onesb", [1], BF16)
    epsc = SB.alloc("epsc", [1], F32)
    hist = SB.alloc("hist", [8, 3], F32)
    hstate = SB.alloc("hstate", [8], F32)
    KT = [SB.alloc(f"KT{i}", [2, 128], BF16) for i in range(2)]
    vaug = [SB.alloc(f"vaug{i}", [4, 65], BF16) for i in range(2)]

    for nm, t in [("identf", identf), ("iota", iota), ("gmixT", gmixT), ("gffnT", gffnT), ("gaoT", gaoT), ("gloT", gloT),
                  ("sinks", esink), ("cw", cw), ("cb", cb), ("ba", ba), ("bx", bx), ("lam", nsp),
                  ("rope", rope), ("pmask", pmask)]:
        P.dma("sp", t, I[nm])
    for nm, t in [("bda", bda), ("bdx", bdx), ("amask", amask), ("identf", identb)]:
        P.dma("pool", t, I[nm])
    P.memset("dve", onesb, 1.0)
    P.copy("dve", iotab, iota)
    P.memset("dve", epsc, EPS)
    P.memset("dve", hist, 0.0)
    P.memset("dve", hstate, 0.0)
    for i in range(2):
        P.memset("dve", vaug[i], 1.0)
        P.memset("dve", KT[i], 0.0)
    P.actf(esink, esink, AF.Exp)
    P.actf(nsp, nsp, AF.Exp, scale=-1.0)
    tmp8 = SB.alloc("tmp8", [8], F32)
    P.ts("dve", tmp8, nsp, 1.0, None, ALU.add)
    P.actf(nsp, tmp8, AF.Ln)
    P.ts("dve", nsp, nsp, -8.0, None, ALU.mult)

    state = dict(hslot=1, kvslot=1)

    def load_w(tile_ap, dram, r0, c0, c1, nchunks, tag):
        names = []
        for dc in range(nchunks):
            nm = f"{tag}#{dc}"
            P.dma("pool", tile_ap[:, dc, :], dram[r0 + dc * 128:r0 + (dc + 1) * 128, c0:c1], w=[nm])
            names.append(nm)
        return WNames(names, "dc")

    def load_w16(tile_ap, wname, c0, c1, tag):
        names = []
        src = W16[wname].rearrange("(dc p) n -> p dc n", p=128)
        for k, cb0 in enumerate(range(c0, c1, 512)):
            nm = f"{tag}#c{k}"
            P.dma("sp" if k % 2 == 0 else "act", tile_ap[:, :, cb0 - c0:cb0 - c0 + 512], src[:, :, cb0:cb0 + 512],
                  r=[f"{wname}16#{dc}" for dc in range(DC)], w=[nm])
            names.append(nm)
        return WNames(names, "col")

    def norm_T(xsrc, gT, hT_dst, scr, psT, sfx):
        P.actf(scr["junk"], xsrc, AF.Square, accum=scr["ssq"])
        P.actf(scr["rms"], scr["ssq"], AF.Sqrt, bias=epsc, scale=1.0 / D)
        P.recip(scr["rstd"], scr["rms"])
        P.ts("dve", scr["xn"], xsrc, scr["rstd"], None, ALU.mult)
        psTb = psT.bitcast(BF16).rearrange("p (c t) -> p c t", t=128)
        for dc in range(DC):
            P.tr(psTb[:, dc, :], scr["xn"][:, dc * 128:(dc + 1) * 128], identb)
        P.tt("dve", hT_dst, psTb[:, 0:DC, :], bc(gT.unsqueeze(2), [128, DC, 128]), ALU.mult)

    def alloc_norm_scr():
        xn = SB.alloc("xn", [D], BF16)
        return dict(junk=xn, ssq=SB.alloc("ssq", [1], F32), rms=SB.alloc("rms", [1], F32),
                    rstd=SB.alloc("rstd", [1], F32), xn=xn)

    def alloc_lru_scr():
        return dict(xpb=SB.alloc("xpb", [8, 176], F32), xc=SB.alloc("xc", [8, 128], F32), xcb=SB.alloc("xcb", [8, 128], BF16),
                    tmp=SB.alloc("ltmp", [8, 128], F32), r=SB.alloc("r", [8, 128], F32), ig=SB.alloc("ig", [8, 128], F32),
                    a=SB.alloc("a", [8, 128], F32), u=SB.alloc("u", [8, 128], F32), gy=SB.alloc("gy", [8, 128], F32),
                    lob=SB.alloc("lob", [8, 128], BF16), cst=SB.alloc("cst", [8, 48], F32), h0T=SB.alloc("h0T", [8, 16], F32),
                    ost=SB.alloc("ost", [1024], F32), h=SB.alloc("hcur", [8, 128], F32))

    def lru_tile(kind, hnT, win_lru, wnames, S, psX, psS, mixT_dst, ssq_dst, pm_col=None, outs=None, sample_in=None,
                 hprev=None, psG=None, psOut=None):
        npart = 16 if kind != "pre" else 8
        psXv = psX.rearrange("p (c t) -> p c t", t=128)
        for cc in range(npart):
            for dc in range(DC):
                P.mm(psXv[:, cc, :], win_lru[:, dc, cc * 128:(cc + 1) * 128], hnT[:, dc, :], dc == 0, dc == DC - 1,
                     r=[wnames.at(dc, cc * 128), P._nm(hnT)])
        P.mark_split()
        psGv = psXv if psG is None else psG.rearrange("p (c t) -> p c t", t=128)
        hcur = S["h"]
        if kind != "smp":
            xpv = S["xpb"][:, :, 0:131]
            P.copy("act", xpv[:, :, 3:131], psXv[:, 0:8, :])
            P.copy("dve", xpv[:, :, 0:3], hist)
            P.copy("dve", hist, xpv[:, :, 128:131])

            def sl(j):
                return xpv[:, :, j:j + 128]

            def cwb(j):
                return bc(cw[:, :, j:j + 1], [128, 8, 128])
            cbb = bc(cb.unsqueeze(2), [128, 8, 128])
            xcv, tmpv = S["xc"], S["tmp"]
        else:
            xp4 = S["xpb"].rearrange("p c (b l) -> p c b l", l=11)
            P.copy("act", xp4[:, :, :, 3:11], psXv[:, 0:8, :].rearrange("p c (b l) -> p c b l", l=8))
            sct, sht = sample_in
            psC = psS.rearrange("p (c n) -> p c n", n=64)
            for cc in range(8):
                P.tr(psC[:, cc, 0:48], sct[:, cc * 128:(cc + 1) * 128], identf[0:48, 0:48])
            P.copy("dve", xp4[:, :, :, 0:3], psC[:, :, 0:48].rearrange("p c (b j) -> p c b j", j=3))
            for cc in range(8):
                P.tr(psC[:, cc, 48:64], sht[:, cc * 128:(cc + 1) * 128], identf[0:16, 0:16])
            P.copy("dve", S["h0T"], psC[:, :, 48:64])

            def sl(j):
                return xp4[:, :, :, j:j + 8]

            def cwb(j):
                return bc(cw[:, :, j:j + 1].unsqueeze(3), [128, 8, 16, 8])
            cbb = bc(cb.unsqueeze(2).unsqueeze(3), [128, 8, 16, 8])
            xcv = S["xc"].rearrange("p c (b l) -> p c b l", l=8)
            tmpv = S["tmp"].rearrange("p c (b l) -> p c b l", l=8)
        if kind != "pre":
            P.actf(S["gy"], psXv[:, 8:16, :], AF.Gelu_apprx_tanh)
        P.mark_split()
        P.tt("dve", xcv, sl(0), cwb(0), ALU.mult)
        for j in range(1, 4):
            P.tt("dve", tmpv, sl(j), cwb(j), ALU.mult)
            P.tt("dve", xcv, xcv, tmpv, ALU.add)
        P.tt("dve", xcv, xcv, cbb, ALU.add)
        P.copy("act", S["xcb"], S["xc"])
        go = 0 if (kind == "pre" or psG is not None) else 8
        for cc in range(8):
            P.mm(psGv[:, cc, :], bda[:, cc, :], S["xcb"][:, cc, :], True, True)
        if go:
            for cc in range(8):
                P.mm(psGv[:, go + cc, :], bdx[:, cc, :], S["xcb"][:, cc, :], True, True)
        for cc in range(8):
            P.actf(S["r"][:, cc, :], psGv[:, cc, :], AF.Sigmoid, bias=ba[:, cc:cc + 1])
        if not go:
            for cc in range(8):
                P.mm(psGv[:, cc, :], bdx[:, cc, :], S["xcb"][:, cc, :], True, True)
        for cc in range(8):
            P.actf(S["ig"][:, cc, :], psGv[:, go + cc, :], AF.Sigmoid, bias=bx[:, cc:cc + 1])
        for cc in range(8):
            P.actf(S["a"][:, cc, :], S["r"][:, cc, :], AF.Exp, scale=nsp[:, cc:cc + 1])
        P.tt("dve", S["tmp"], S["a"], S["a"], ALU.mult)
        P.ts("dve", S["tmp"], S["tmp"], -1.0, 1.0, ALU.mult, ALU.add)
        P.actf(S["tmp"], S["tmp"], AF.Sqrt)
        P.tt("dve", S["u"], S["ig"], S["xc"], ALU.mult)
        if kind == "pre":
            P.stt(S["u"], S["u"], pm_col, S["tmp"], ALU.mult, ALU.mult)
        else:
            P.tt("dve", S["u"], S["u"], S["tmp"], ALU.mult)
        if kind == "smp":
            a4 = S["a"].rearrange("p c (b l) -> p c b l", l=8)
            u4 = S["u"].rearrange("p c (b l) -> p c b l", l=8)
            t3 = S["cst"][:, :, 0:16]
            P.tt("dve", t3, a4[:, :, :, 0], S["h0T"], ALU.mult)
            P.tt("dve", u4[:, :, :, 0], u4[:, :, :, 0], t3, ALU.add)
            P.memset("dve", a4[:, :, :, 0], 0.0)
        for cc in range(8):
            if kind == "smp":
                init = 0.0
            elif hprev is not None:
                init = hprev[:, cc, 127:128]
            else:
                init = hstate[:, cc:cc + 1]
            P.scan(hcur[:, cc, :], S["a"][:, cc, :], S["u"][:, cc, :], init)
        if kind != "smp":
            P.copy("dve", hstate, hcur[:, :, 127])
        if kind == "pre":
            return
        P.tt("dve", S["gy"], hcur, S["gy"], ALU.mult)
        P.tt("dve", S["lob"], S["gy"], S["gy"], ALU.mult)
        for cc in range(8):
            P.mm(psS[:, 0:1], S["lob"][:, cc, :], onesb, cc == 0, cc == 7)
        P.copy("dve", ssq_dst, psS[:, 0:1])
        P.tt("dve", mixT_dst, S["gy"], bc(gloT.unsqueeze(2), [128, 8, 128]), ALU.mult)
        pso = psX if psOut is None else psOut
        if outs is not None:
            psC = psS.rearrange("p (c n) -> p c n", n=64)
            if kind == "own":
                P.copy("dve", S["cst"][:, :, 0:3], xpv[:, :, 128:131])
                P.copy("dve", S["cst"][:, :, 3:4], hcur[:, :, 127:128])
                n1, n2 = 3, 1
            else:
                P.copy("dve", S["cst"].rearrange("p c (b j) -> p c b j", j=3), xp4[:, :, :, 8:11])
                P.copy("dve", S["h0T"], hcur.rearrange("p c (b l) -> p c b l", l=8)[:, :, :, 7])
                n1, n2 = 48, 16
            for which, n, dst in ((0, n1, outs[0]), (1, n2, outs[1])):
                for cc in range(8):
                    if kind == "own":
                        src = S["cst"][:, cc, 0:3] if which == 0 else S["cst"][:, cc, 3:4]
                    else:
                        src = S["cst"][:, cc, :] if which == 0 else S["h0T"][:, cc, :]
                    P.tr(pso[0:n, cc * 128:(cc + 1) * 128], src, identf)
                ostv = S["ost"]
                P.copy("act", ostv[0:n, :], pso[0:n, 0:1024])
                P.dma("sp", dst, ostv[0:n, :])

    def alloc_attn_scr():
        sq = SB.alloc("sq", [20, 64], F32)
        aob = SB.alloc("aob", [1024], BF16)
        gqk_t = SB.alloc("gqk", [20, 64], F32)
        P.dma("sp", gqk_t, I["gqk"])
        return dict(qkv=SB.alloc("qkv", [1536], F32), sq=sq, st=SB.alloc("qst", [20], F32),
                    qkn=sq, gqk=gqk_t, qkr=SB.alloc("qkr", [20, 64], F32),
                    t1=SB.alloc("t1", [20, 32], F32), t2=SB.alloc("t2", [20, 32], F32),
                    qb=SB.alloc("qb", [16, 64], BF16), kb=SB.alloc("kb", [4, 64], BF16), QT=SB.alloc("QT", [8, 128], BF16),
                    E=[SB.alloc(f"E{i}", [2, 512], BF16) for i in range(2)], den=SB.alloc("den", [4], F32),
                    ao=SB.alloc("ao", [16, 64], F32), aob=aob, junk=aob,
                    sa=SB.alloc("sa", [1], F32))

    def qkv_tile(hnT, win_qkv, wnames, A, psQ, psT, rope_idx, slot):
        for nb in range(3):
            for dc in range(DC):
                P.mm(psQ[:, nb * 512:(nb + 1) * 512], hnT[:, dc, :], win_qkv[:, dc, nb * 512:(nb + 1) * 512], dc == 0, dc == DC - 1,
                     r=[wnames.at(dc, nb * 512), P._nm(hnT)])
        P.copy("act", A["qkv"], psQ[:, 0:1536], r=[nB0, nB1])
        qk = A["qkv"][:, 0:1280].rearrange("p (h d) -> p h d", d=64)
        P.tt("dve", A["sq"], qk, qk, ALU.mult)
        P.reduce(A["st"], A["sq"], ALU.add)
        P.actf(A["st"], A["st"], AF.Sqrt, bias=epsc, scale=1.0 / 64)
        P.recip(A["st"], A["st"])
        P.tt("dve", A["qkn"], qk, bc(A["st"].unsqueeze(2), [128, 20, 64]), ALU.mult)
        P.tt("dve", A["qkn"], A["qkn"], A["gqk"], ALU.mult)
        cosb = bc(rope[:, rope_idx, 0:32].unsqueeze(1), [128, 20, 32])
        sinb = bc(rope[:, rope_idx, 32:64].unsqueeze(1), [128, 20, 32])
        x1, x2 = A["qkn"][:, :, 0:32], A["qkn"][:, :, 32:64]
        P.tt("dve", A["t1"], x1, cosb, ALU.mult)
        P.tt("dve", A["t2"], x2, sinb, ALU.mult)
        P.tt("dve", A["qkr"][:, :, 0:32], A["t1"], A["t2"], ALU.subtract)
        P.tt("dve", A["t1"], x2, cosb, ALU.mult)
        P.tt("dve", A["t2"], x1, sinb, ALU.mult)
        P.tt("dve", A["qkr"][:, :, 32:64], A["t1"], A["t2"], ALU.add)
        for gp in range(2):
            src = A["qkr"][:, 8 * gp:8 * gp + 8, :].rearrange("p (half r) d -> p half r d", r=4)
            dst = A["qb"][:, 8 * gp:8 * gp + 8, :].rearrange("p (r half) d -> p half r d", half=2)
            P.copy("act", dst, src)
        P.copy("act", A["kb"], A["qkr"][:, 16:20, :])
        P.copy("act", vaug[slot][:, :, 0:64], A["qkv"][:, 1280:1536].rearrange("p (h d) -> p h d", d=64))
        psTb = psT.bitcast(BF16).rearrange("p (c t) -> p c t", t=128)
        for j in range(8):
            P.tr(psTb[:, j, :], A["qb"][:, 2 * j:2 * j + 2, :].rearrange("p h d -> p (h d)"), identb)
        for j in range(2):
            P.tr(psTb[:, 8 + j, :], A["kb"][:, 2 * j:2 * j + 2, :].rearrange("p h d -> p (h d)"), identb)
        P.copy("dve", A["QT"], psTb[:, 0:8, :])
        P.copy("dve", KT[slot], psTb[:, 8:10, :])

    def attn_core(A, kblocks, psL, psO, psT, mixT_dst, rstd_dst, po_banks=None):
        nkb = len(kblocks)
        it = 0
        for g in range(4):
            gp, half = divmod(g, 2)
            rows = slice(64 * half, 64 * half + 64)
            po = psO[g % 2].rearrange("p (r d) -> p r d", d=128)
            for k0 in range(0, nkb, 2):
                ks = kblocks[k0:k0 + 2]
                pl = psL[it % 2].rearrange("p (k n) -> p k n", n=512)
                E = A["E"][it % 2]
                it += 1
                for ki, (kt, va, mk) in enumerate(ks):
                    P.mm(pl[:, ki, :].rearrange("p (r q) -> p r q", q=128), kt[rows, gp, :], A["QT"][rows, 4 * gp:4 * gp + 4, :], True, True)
                n = len(ks)
                P.actf(E[:, 0:n, :], pl[:, 0:n, :], AF.Exp, scale=0.125)
                for ki, (kt, va, mk) in enumerate(ks):
                    Ev = E[:, ki, :].rearrange("p (r q) -> p r q", q=128)
                    P.tt("dve", Ev, Ev, bc(mk.unsqueeze(1), [128, 4, 128]), ALU.mult)
                for r in range(4):
                    dst = po[:, r, 0:65] if po_banks is None else po_banks[r][:, 0:65]
                    for ki, (kt, va, mk) in enumerate(ks):
                        P.mm(dst, E[:, ki, r * 128:(r + 1) * 128], va[:, g, :], (k0 + ki) == 0, (k0 + ki) == nkb - 1)
            if po_banks is None:
                P.tt("dve", A["den"], po[:, :, 64], esink[:, 4 * g:4 * g + 4], ALU.add)
                P.recip(A["den"], A["den"])
                P.tt("dve", A["ao"][:, 4 * g:4 * g + 4, :], po[:, :, 0:64], bc(A["den"].unsqueeze(2), [128, 4, 64]), ALU.mult)
            else:
                for r in range(4):
                    P.tt("dve", A["den"][:, r:r + 1], po_banks[r][:, 64:65], esink[:, 4 * g + r:4 * g + r + 1], ALU.add)
                P.recip(A["den"], A["den"])
                for r in range(4):
                    P.ts("dve", A["ao"][:, 4 * g + r, :], po_banks[r][:, 0:64], A["den"][:, r:r + 1], None, ALU.mult)
        aof = A["ao"].rearrange("p h d -> p (h d)")
        P.actf(A["junk"], aof, AF.Square, accum=A["sa"])
        P.actf(A["sa"], A["sa"], AF.Sqrt, bias=epsc, scale=1.0 / 1024)
        P.recip(rstd_dst, A["sa"])
        P.copy("act", A["aob"], aof)
        psTb = psT.bitcast(BF16).rearrange("p (c t) -> p c t", t=128)
        for cc in range(8):
            P.tr(psTb[:, cc, :], A["aob"][:, cc * 128:(cc + 1) * 128], identb)
        P.tt("dve", mixT_dst, psTb[:, 0:8, :], bc(gaoT.unsqueeze(2), [128, 8, 128]), ALU.mult)

    psT = PS.alloc("psT", [1024], F32)
    psB0 = PS.alloc("psB0", [1024], F32)
    psB1 = PS.alloc("psB1", [1024], F32)
    psO = [PS.alloc("psO0", [512], F32), PS.alloc("psO1", [512], F32)]
    psBig = ps_t[:, 1024:3072]
    nB0, nB1 = P._nm(psB0), P._nm(psB1)
    psL = [psB0, psB1]

    m0 = SB.mark()
    hnTs = [SB.alloc(f"hnT{i}", [16, 128], BF16) for i in range(2)]
    m1 = SB.mark()
    winx = SB.alloc("winx", [16, 1024], BF16)
    wxn = load_w(winx, I["w_in"], 0, 1536, 2560, 16, "winx")
    xt = [SB.alloc(f"xt{i}", [D], F32) for i in range(2)]
    NSs = [alloc_norm_scr() for _ in range(2)]
    LSs = [alloc_lru_scr() for _ in range(2)]
    npre = NPRE if stage != "nopre" else 1
    streams = []
    xpbs = [[LSs[0]["xpb"], LSs[1]["xpb"]], [SB.alloc("xpb_c", [8, 176], F32), SB.alloc("xpb_d", [8, 176], F32)]]
    psGs = [psT, ps_t[:, 3072:4096]]
    for ti in range(NPRE - npre, NPRE):
        par = ti % 2
        pr = (ti // 2) % 2
        P.begin_stream()
        P.dma("sp", xt[par], I["xpre"][ti * 128:(ti + 1) * 128, :])
        norm_T(xt[par], gmixT, hnTs[par], NSs[par], psL[par], "p")
        P.mark_split()
        lru_tile("pre", hnTs[par], winx, wxn, dict(LSs[par], xpb=xpbs[pr][par]), psL[par], psO[par], None, None,
                 pm_col=pmask[:, ti:ti + 1], hprev=(LSs[1 - par]["h"] if ti > NPRE - npre else None), psG=psGs[par])
        emit_conv(13)
        streams.append(P.end_stream())
    if len(streams) > 1:
        P.commit_prefix_pairs(streams)
    else:
        for o in streams[0][0]:
            P._add(*o)
    hnT = hnTs[(NPRE - 1) % 2]
    SB.release(m1)
    wqkv = SB.alloc("wqkv", [16, 1536], BF16)
    wqn = load_w16(wqkv, "w_in", 0, 1536, "wqkv")
    A = alloc_attn_scr()
    qkv_tile(hnT, wqkv, wqn, A, psBig, psT, 0, 0)
    state["kvslot"] = 0
    SB.release(m0)

    def tile_rows(ti):
        if ti < NOWN:
            return I["xp"][ti * 128:(ti + 1) * 128, :], O["yp"][ti * 128:(ti + 1) * 128, :]
        return I["xs"], O["ys"]

    def peer_group(g, tiles, hcT):
        qT = SB.alloc("qT", [16, 384], BF16)
        skT = SB.alloc("skT", [16, 128], BF16)
        P.dma("pool", skT, I["skT"])
        mq = SB.mark()
        wpq = SB.alloc("wpq", [16, 2048], BF16)
        wpn = load_w16(wpq, "w_pq", 0, 2048, f"wpq{g}")
        for m in range(16):
            pb = psL[m % 2]
            for dc in range(DC):
                P.mm(pb[:, 0:384], wpq[:, dc, m * 128:(m + 1) * 128], hcT[:, dc, :], dc == 0, dc == DC - 1,
                     r=[wpn.at(dc, m * 128), P._nm(hcT)])
            P.copy("act" if m % 2 == 0 else "dve", qT[:, m, :], pb[:, 0:384])
        SB.release(mq)
        Wsb = SB.alloc("Wsb", [128, 384], BF16)
        mq = SB.mark()
        s_sb = SB.alloc("s_sb", [16, 128], F32)
        s2w = SB.alloc("s2w", [8, 128], F32)
        sv = SB.alloc("sv", [16, 16], F32)
        si = SB.alloc("si", [8, 16], U32)
        cand = [SB.alloc(f"cand{i}", [16, 16], F32) for i in range(2)]
        cand2 = [SB.alloc(f"cand2{i}", [256], F32) for i in range(2)]
        best = SB.alloc("best", [8, 16], F32)
        eb = SB.alloc("eb", [8, 16], F32)
        zz = SB.alloc("zz", [8], F32)
        nthr = SB.alloc("nthr", [8, 16], F32)
        e1 = SB.alloc("e1", [8, 16], BF16)
        e2 = SB.alloc("e2", [8, 128], BF16)
        e2a = s2w
        si1f = SB.alloc("si1f", [8, 16], F32)
        si1T = SB.alloc("si1T", [128], BF16)
        JB = 8
        Mk = [SB.alloc(f"Mk{i}", [8, 16, JB], BF16) for i in range(2)]
        Gb = [SB.alloc(f"Gb{i}", [8, 16, JB], BF16) for i in range(2)]
        Rw = [SB.alloc(f"Rw{i}", [128, JB], BF16) for i in range(2)]
        RwT = SB.alloc("RwT", [128, 128], BF16)
        O1 = [SB.alloc(f"O1{i}", [8, 128], BF16) for i in range(2)]
        psTb = psT.bitcast(BF16).rearrange("p (c t) -> p c t", t=128)
        for tt in range(3):
            pss = psBig.rearrange("p (m k) -> p m k", k=128)
            for m in range(16):
                P.mm(pss[:, m, :], qT[:, m, tt * 128:(tt + 1) * 128], skT[:, m, :], True, True)
            P.copy("act", s_sb[:, 0:8, :], pss[:, 0:8, :])
            P.copy("dve", s_sb[:, 8:16, :], pss[:, 8:16, :])
            for m0 in (0, 8):
                ms = range(m0, m0 + 8)
                for m in ms:
                    P.max8(sv[:, m, 0:8], s_sb[:, m, :], w=[f"sv#{m}a"], r=[P._nm(s_sb)])
                for m in ms:
                    if m % 2 == 0:
                        P.maxidx(si[:, m // 2, 0:8], sv[:, m, 0:8], s_sb[:, m, :], r=[f"sv#{m}a", P._nm(s_sb)], w=[f"si#{m}a"])
                for m in ms:
                    P.matchrep(s2w[:, m - m0, :], sv[:, m, 0:8], s_sb[:, m, :], -1e30, r=[f"sv#{m}a", P._nm(s_sb)],
                               w=[f"s2w#{m - m0}"])
                for m in ms:
                    P.max8(sv[:, m, 8:16], s2w[:, m - m0, :], r=[f"s2w#{m - m0}"], w=[f"sv#{m}b"])
                for m in ms:
                    if m % 2 == 0:
                        P.maxidx(si[:, m // 2, 8:16], sv[:, m, 8:16], s2w[:, m - m0, :], r=[f"sv#{m}b", f"s2w#{m - m0}"],
                                 w=[f"si#{m}b"])
            svn = [f"sv#{m}{x}" for m in range(16) for x in "ab"]
            sin = [f"si#{m}{x}" for m in range(0, 16, 2) for x in "ab"]
            for h in range(8):
                cd, cd2 = cand[h % 2], cand2[h % 2]
                cdf = cd.rearrange("p a b -> p (a b)")
                P.tt("dve", cd, bc(sv[:, 2 * h, :].unsqueeze(2), [128, 16, 16]),
                     bc(sv[:, 2 * h + 1, :].unsqueeze(1), [128, 16, 16]), ALU.add,
                     r=[f"sv#{2 * h}a", f"sv#{2 * h}b", f"sv#{2 * h + 1}a", f"sv#{2 * h + 1}b"])
                P.max8(best[:, h, 0:8], cdf, w=[f"best#{h}a"])
                P.matchrep(cd2, best[:, h, 0:8], cdf, -1e30, r=[f"best#{h}a", P._nm(cd)])
                P.max8(best[:, h, 8:16], cd2, w=[f"best#{h}b"])
            bn = [f"best#{h}{x}" for h in range(8) for x in "ab"]
            sv4 = sv.rearrange("p (h c) k -> p h c k", c=2)
            sv1, sv2 = sv4[:, :, 0, :], sv4[:, :, 1, :]
            P.tt("dve", eb, best, bc(best[:, :, 0:1], [128, 8, 16]), ALU.subtract, r=bn)
            P.actf(eb, eb, AF.Exp)
            P.reduce(zz, eb, ALU.add)
            P.actf(zz, zz, AF.Ln)
            P.tt("dve", eb, sv1, bc(sv1[:, :, 0:1], [128, 8, 16]), ALU.subtract, r=svn + [P._nm(eb)])
            P.actf(e1, eb, AF.Exp)
            P.tt("dve", zz, zz, sv2[:, :, 0], ALU.add, r=svn + [P._nm(zz)])
            s2 = s_sb.rearrange("p (h c) k -> p h c k", c=2)[:, :, 1, :]
            s2wn = [f"s2w#{k}" for k in range(8)]
            P.tt("dve", e2a, s2, bc(zz.unsqueeze(2), [128, 8, 128]), ALU.subtract, w=s2wn)
            P.actf(e2, e2a, AF.Exp, r=s2wn)
            P.tt("dve", nthr, bc(best[:, :, 15:16], [128, 8, 16]), sv1, ALU.subtract, r=bn + svn)
            P.ts("dve", nthr, nthr, THR_TOL, None, ALU.add)
            P.copy("dve", si1f, si, r=sin)
            pst = psO[0][:, 0:128]
            P.tr(pst, si1f.rearrange("p h a -> p (h a)"), identf)
            P.copy("act", si1T, pst)
            for jb in range(128 // JB):
                par = jb % 2
                mk_, gb_, rw = Mk[par], Gb[par], Rw[par]
                js = slice(jb * JB, (jb + 1) * JB)
                P.tt("dve", mk_, bc(s2[:, :, js].unsqueeze(2), [128, 8, 16, JB]),
                     bc(nthr.unsqueeze(3), [128, 8, 16, JB]), ALU.is_ge)
                P.tt("pool", gb_, bc(e1.unsqueeze(3), [128, 8, 16, JB]),
                     bc(e2[:, :, js].unsqueeze(2), [128, 8, 16, JB]), ALU.mult)
                P.tt("dve", rw.rearrange("p q j -> p (q j)"), mk_.rearrange("p h a j -> p (h a j)"),
                     gb_.rearrange("p h a j -> p (h a j)"), ALU.mult)
                pn = f"psThalf{par}"
                for jj in range(JB):
                    P.tr(psTb[:, par * 8 + jj, :], rw[:, :, jj], identb, w=[pn])
                P.copy("act", RwT[:, js, :], psTb[:, par * 8:par * 8 + JB, :], r=[pn])
            for tb in range(16):
                o1 = O1[tb % 2]
                P.tt("dve", o1, bc(iotab.unsqueeze(1), [128, 8, 128]),
                     bc(si1T[:, tb * 8:(tb + 1) * 8].unsqueeze(2), [128, 8, 128]), ALU.is_equal)
                pw = psL[tb % 2].rearrange("p (t i) -> p t i", i=128)
                for tq in range(8):
                    P.mm(pw[:, tq, :], RwT[:, :, tb * 8 + tq], o1[:, tq, :], True, True)
                t0 = tt * 128 + tb * 8
                P.copy("act", Wsb[:, :, t0:t0 + 8], pw.rearrange("p t i -> p i t"))
        SB.release(mq)
        NR = 4
        Ub = [SB.alloc(f"Ub{i}", [16, 128], BF16) for i in range(NR)]
        NRV = 8
        Vb = [SB.alloc(f"Vb{i}", [1024], BF16) for i in range(NRV)]
        gl = [SB.alloc(f"gl{i}", [384], BF16) for i in range(2)]
        xr1 = [SB.alloc(f"xr1{i}", [1024], F32) for i in range(3)]
        yo = [SB.alloc(f"yo{i}", [1024], F32) for i in range(3)]
        psY = [psT, psB0, psB1]
        for half in range(2):
            c0 = half * 1024

            def stage_a(ic):
                wn = f"W{g}_{ic}"
                vb = Vb[ic % NRV]
                if half == 0:
                    ub = Ub[ic % NR]
                    P.dma("sp", ub, ub16[ic].rearrange("p (a b) -> p a b", b=128), r=[f"ub16#{ic}"])
                    P.dma("sp", vb, vb16[ic * 128:(ic + 1) * 128, 0:1024], r=[f"vb16#{ic}"])
                    pa = psO[ic % 2]
                    for dc in range(DC):
                        P.mm(pa[:, 0:384], ub[:, dc, :], hcT[:, dc, :], dc == 0, dc == DC - 1)
                    P.actf(gl[ic % 2], pa[:, 0:384], AF.Gelu_apprx_tanh)
                    P.tt("dve", Wsb[:, ic, :], gl[ic % 2], Wsb[:, ic, :], ALU.mult, r=[P._nm(gl[ic % 2]), wn], w=[wn])
                else:
                    P.dma("sp", vb, vb16[ic * 128:(ic + 1) * 128, 1024:2048], r=[f"vb16#{ic}"])

            def stage_v(ic):
                wn = f"W{g}_{ic}"
                vb = Vb[ic % NRV]
                for tt in range(3):
                    for nb in range(2):
                        P.mm(psY[tt][:, nb * 512:(nb + 1) * 512], Wsb[:, ic, tt * 128:(tt + 1) * 128],
                             vb[:, nb * 512:(nb + 1) * 512], ic == 0, ic == 127, r=[wn, P._nm(vb)])

            stage_a(0)
            for ic in range(128):
                if ic + 1 < 128:
                    stage_a(ic + 1)
                stage_v(ic)
            for tt, ti in enumerate(tiles):
                _, yout = tile_rows(ti)
                P.dma("sp", xr1[tt], yout[:, c0:c0 + 1024], r=[f"y#{ti}"])
                P.tt("dve", yo[tt], psY[tt], xr1[tt], ALU.add)
                P.dma("sp", yout[:, c0:c0 + 1024], yo[tt], w=[f"y#{ti}#{half}"])

    for g in range(3):
        tiles = [3 * g, 3 * g + 1, 3 * g + 2]
        mG = SB.mark()
        hcT = SB.alloc("hcT", [16, 384], BF16)
        mAB = SB.mark()
        xres = [SB.alloc(f"xres{i}", [D], F32) for i in range(3)]
        mixT = SB.alloc("mixT", [16, 384], BF16)
        hnTg = [SB.alloc(f"hnTg{i}", [16, 128], BF16) for i in range(3)]
        stats = SB.alloc("stats", [3, 2], F32)
        mA1 = SB.mark()
        wqkv = SB.alloc("wqkv", [16, 1536], BF16)
        wqn = load_w16(wqkv, "w_in", 0, 1536, f"wqkv{g}")
        NS = alloc_norm_scr()
        A = alloc_attn_scr()
        for j, ti in enumerate(tiles):
            xin, _ = tile_rows(ti)
            P.dma("sp", xres[j], xin)
            norm_T(xres[j], gmixT, hnTg[j], NS, psT, "a")
            prev = state["kvslot"]
            cur = 1 - prev
            qkv_tile(hnTg[j], wqkv, wqn, A, psBig, psT, 1 + ti, cur)
            mdst = mixT[:, 0:8, j * 128:(j + 1) * 128]
            if ti < NOWN:
                mprev = amask[:, 0, :] if ti == 0 else amask[:, 1, :]
                kbl = [(KT[prev], vaug[prev], mprev), (KT[cur], vaug[cur], amask[:, 2, :])]
                attn_core(A, kbl, psL, psO, psT, mdst, stats[:, j, 0:1])
                if ti == NOWN - 1:
                    P.dma("sp", O["kp"], A["qkr"][:, 16:20, :].rearrange("p h d -> p (h d)"))
                    P.dma("sp", O["vp"], A["qkv"][:, 1280:1536])
            else:
                ckb = SB.alloc("ckb", [16, 256], BF16)
                cvb = ckb
                KTc = SB.alloc("KTc", [16, 2, 128], BF16)
                cva = SB.alloc("cva", [16, 4, 65], BF16)
                smask = SB.alloc("smask", [17, 128], BF16)
                P.dma("pool", ckb, I["ck"].rearrange("b s d -> s b d"))
                P.dma("pool", smask, I["smask"])
                P.dma("sp", O["ks"][:, 0:120, :], I["ck"][:, 8:128, :], w=["ks#bulk"])
                P.dma("sp", O["vs"][:, 0:120, :], I["cv"][:, 8:128, :], w=["vs#bulk"])
                pbb = psBig.bitcast(BF16).rearrange("p (c t) -> p c t", t=128)
                for b in range(16):
                    for gp in range(2):
                        P.tr(pbb[:, b * 2 + gp, :], ckb[:, b, gp * 128:(gp + 1) * 128], identb)
                KTf = KTc.rearrange("p b g s -> p (b g) s")
                P.copy("dve", KTf[:, 0:16, :], pbb[:, 0:16, :])
                P.copy("act", KTf[:, 16:32, :], pbb[:, 16:32, :])
                P.dma("pool", cvb, I["cv"].rearrange("b s d -> s b d"))
                P.memset("dve", cva, 1.0)
                P.copy("act", cva[:, :, :, 0:64], cvb.rearrange("p b (h d) -> p b h d", d=64))
                kbl = [(KTc[:, b, :, :], cva[:, b, :, :], smask[:, b, :]) for b in range(16)]
                kbl.append((KT[cur], vaug[cur], smask[:, 16, :]))
                attn_core(A, kbl, psL, psO, psT, mdst, stats[:, j, 0:1],
                          po_banks=[psO[0], psO[1], psT[:, 0:512], psT[:, 512:1024]])
                for b in range(16):
                    P.dma("sp", O["ks"][b, 120:128, :], A["qkr"][b * 8:(b + 1) * 8, 16:20, :].rearrange("p h d -> p (h d)"),
                          w=[f"ks#{b}"])
                    P.dma("sp", O["vs"][b, 120:128, :], A["qkv"][b * 8:(b + 1) * 8, 1280:1536], w=[f"vs#{b}"])
            state["kvslot"] = cur
            emit_conv(0)
        SB.release(mA1)
        wlru = SB.alloc("wlru", [16, 2048], BF16)
        wln = load_w16(wlru, "w_in", 1536, 3584, f"wlru{g}")
        LS = alloc_lru_scr()
        streams = []
        for j, ti in enumerate(tiles):
            mdst = mixT[:, 8:16, j * 128:(j + 1) * 128]
            if ti >= NOWN:
                sct = SB.alloc("sct", [1024], F32)
                sht = SB.alloc("sht", [1024], F32)
            P.begin_stream()
            if ti < NOWN:
                outs = (O["cp"], O["hp"]) if ti == NOWN - 1 else None
                lru_tile("own", hnTg[j], wlru, wln, LS, psBig, psO[0], mdst, stats[:, j, 1:2], outs=outs,
                         psG=psT, psOut=psT)
            else:
                P.dma("sp", sct[0:48, :], I["sconv"])
                P.dma("sp", sht[0:16, :], I["sh"])
                lru_tile("smp", hnTg[j], wlru, wln, LS, psBig, psO[0], mdst, stats[:, j, 1:2], outs=(O["cs"], O["hs"]),
                         sample_in=(sct[0:48, :], sht[0:16, :]), psG=psT, psOut=psT)
            streams.append(P.end_stream())
        P.commit_spread(streams)
        SB.release(mA1)
        wout = SB.alloc("wout", [16, 2048], BF16)
        won = load_w16(wout, "w_out", 0, 2048, f"wout{g}")
        NS = alloc_norm_scr()
        P.actf(stats[:, :, 1], stats[:, :, 1], AF.Sqrt, bias=epsc, scale=1.0 / 1024)
        P.recip(stats[:, :, 1], stats[:, :, 1])
        for j, ti in enumerate(tiles):
            for nb in range(4):
                pb = psL[nb % 2]
                for half in range(2):
                    for c8 in range(8):
                        cc = half * 8 + c8
                        P.mm(pb[:, half * 512:(half + 1) * 512], mixT[:, cc, j * 128:(j + 1) * 128],
                             wout[:, cc, nb * 512:(nb + 1) * 512], c8 == 0, c8 == 7, r=[won.at(cc, nb * 512), P._nm(mixT)])
                xs_ = xres[j][:, nb * 512:(nb + 1) * 512]
                P.stt(xs_, pb[:, 0:512], stats[:, j, 0:1], xs_, ALU.mult, ALU.add)
                P.stt(xs_, pb[:, 512:1024], stats[:, j, 1:2], xs_, ALU.mult, ALU.add)
            _, yout = tile_rows(ti)
            P.dma("sp", yout, xres[j], w=[f"y#{ti}"])
            emit_conv(0)
            norm_T(xres[j], gffnT, hcT[:, :, j * 128:(j + 1) * 128], NS, psT, "b")
        SB.release(mAB)
        if stage == "mixer":
            SB.release(mG)
            continue
        emit_conv(len(conv_jobs))
        peer_group(g, tiles, hcT)
        SB.release(mG)

    P.emit(es)
    P.simulate()
    es.close()
    print(f"[build] ops={len(P.ops)} sbuf_peak={SB.peak / 1024:.1f} KiB")
    return nc


def _rope_table(pos):
    half = 32
    inv = np.exp(np.float32(-math.log(10000.0)) * np.arange(half, dtype=np.float32) * np.float32(2.0 / 64)).astype(np.float32)
    ang = (pos.astype(np.float32)[:, None] * inv[None, :]).astype(np.float32)
    return np.concatenate([np.cos(ang), np.sin(ang)], axis=-1).astype(np.float32)


def _fm(v, nchunk):
    return np.ascontiguousarray(np.asarray(v, np.float32).reshape(nchunk, 128).T)


def _prep(inp):
    f = lambda a: np.ascontiguousarray(np.asarray(a, dtype=np.float32))
    xp_all, xs_all = f(inp["x_prompt"]), f(inp["x_sample"])
    w_in, w_out, w_pq = f(inp["w_in"][0]), f(inp["w_out"][0]), f(inp["w_peer_q"][0])
    uT = np.ascontiguousarray(f(inp["peer_u"][0]).reshape(128, 128, 16, 128).transpose(0, 3, 2, 1))
    pv = f(inp["peer_v"][0])
    skT = np.ascontiguousarray(f(inp["peer_sub_keys"][0]).transpose(3, 0, 1, 2).reshape(128, 16, 128))
    gq, gk = f(inp["q_norm_g"][0]), f(inp["k_norm_g"][0])
    gqk = np.ascontiguousarray(np.broadcast_to(np.concatenate([np.tile(gq, 16), np.tile(gk, 4)])[None, :], (128, 1280)))
    sinks = np.ascontiguousarray(np.broadcast_to(f(inp["attn_sinks"][0])[None, :], (128, 16)))
    cw = np.ascontiguousarray(f(inp["conv_w"][0]).T.reshape(8, 128, 4).transpose(1, 0, 2))

    def bd(w):
        w = f(w)
        out = np.zeros((128, 8, 128), np.float32)
        for cc in range(8):
            for nl in range(2):
                out[nl * 64:(nl + 1) * 64, cc, nl * 64:(nl + 1) * 64] = w[2 * cc + nl]
        return out
    s_idx = np.arange(128)[:, None]
    q_idx = np.arange(128)[None, :]
    m_prev = (s_idx > q_idx).astype(np.float32)
    m_cur = (s_idx <= q_idx).astype(np.float32)
    smask = np.zeros((128, 17, 128), np.float32)
    tb, tp = q_idx // 8, q_idx % 8
    for b in range(16):
        smask[:, b, :] = ((tb == b) & (s_idx > tp)).astype(np.float32)
    smask[:, 16, :] = ((tb == (s_idx // 8)) & ((s_idx % 8) <= tp)).astype(np.float32)
    common = dict(
        w_in=w_in, w_out=w_out, w_pq=w_pq, uT=uT, pv=pv, skT=skT,
        gmixT=_fm(inp["norm_mix_g"][0], 16), gffnT=_fm(inp["norm_ffn_g"][0], 16),
        gaoT=_fm(inp["attn_out_g"][0], 8), gloT=_fm(inp["lru_out_g"][0], 8),
        gqk=gqk, sinks=sinks, cw=cw, cb=_fm(inp["conv_b"][0], 8),
        ba=_fm(f(inp["b_rec_gate"][0]).reshape(-1), 8), bx=_fm(f(inp["b_in_gate"][0]).reshape(-1), 8),
        lam=_fm(inp["lru_lambda"][0], 8), bda=bd(inp["w_rec_gate"][0]), bdx=bd(inp["w_in_gate"][0]),
        identf=np.eye(128, dtype=np.float32),
        iota=np.ascontiguousarray(np.broadcast_to(np.arange(128, dtype=np.float32)[None, :], (128, 128))),
        smask=smask,
    )
    ck, cv = f(inp["cache_k"][0]).reshape(128, 128, 256), f(inp["cache_v"][0]).reshape(128, 128, 256)
    sconv, sh = f(inp["state_conv"][0]), f(inp["state_h"][0])
    maps = []
    for c in range(NCORES):
        seq, k = divmod(c, 4)
        xpre = np.zeros((NPRE * 128, D), np.float32)
        if k > 0:
            xpre[NPRE * 128 - 1024 * k:] = xp_all[seq, 0:1024 * k]
        pmask = np.zeros((128, NPRE), np.float32)
        pmask[:, NPRE - 8 * k:] = 1.0
        rope = np.zeros((128, 10, 64), np.float32)
        p = np.arange(128)
        rope[:, 0] = _rope_table(np.maximum(k * 1024 - 128 + p, 0))
        for ti in range(8):
            rope[:, 1 + ti] = _rope_table(k * 1024 + ti * 128 + p)
        rope[:, 9] = _rope_table(8192 + (p % 8))
        amask = np.stack([m_prev if k > 0 else np.zeros_like(m_prev), m_prev, m_cur], axis=1)
        m = dict(common)
        m.update(
            xp=np.ascontiguousarray(xp_all[seq, 1024 * k:1024 * (k + 1)]),
            xs=np.ascontiguousarray(xs_all[16 * c:16 * (c + 1)].reshape(128, D)),
            xpre=xpre, pmask=pmask, rope=rope, amask=np.ascontiguousarray(amask),
            ck=np.ascontiguousarray(ck[16 * c:16 * (c + 1)]), cv=np.ascontiguousarray(cv[16 * c:16 * (c + 1)]),
            sconv=np.ascontiguousarray(sconv[16 * c:16 * (c + 1)].reshape(48, 1024)),
            sh=np.ascontiguousarray(sh[16 * c:16 * (c + 1)]),
        )
        maps.append(m)
    return maps


def _assemble(res):
    y_prompt = np.zeros((2, 4096, D), np.float32)
    y_sample = np.zeros((128, 8, D), np.float32)
    k_prompt = np.zeros((1, 2, 128, 4, 64), np.float32)
    v_prompt = np.zeros((1, 2, 128, 4, 64), np.float32)
    conv_prompt = np.zeros((1, 2, 3, 1024), np.float32)
    h_prompt = np.zeros((1, 2, 1024), np.float32)
    k_sample = np.zeros((1, 128, 128, 4, 64), np.float32)
    v_sample = np.zeros((1, 128, 128, 4, 64), np.float32)
    conv_sample = np.zeros((1, 128, 3, 1024), np.float32)
    h_sample = np.zeros((1, 128, 1024), np.float32)
    for c in range(NCORES):
        r = res[c]
        seq, k = divmod(c, 4)
        y_prompt[seq, 1024 * k:1024 * (k + 1)] = r["yp"]
        y_sample[16 * c:16 * (c + 1)] = r["ys"].reshape(16, 8, D)
        if k == 3:
            k_prompt[0, seq] = r["kp"].reshape(128, 4, 64)
            v_prompt[0, seq] = r["vp"].reshape(128, 4, 64)
            conv_prompt[0, seq] = r["cp"]
            h_prompt[0, seq] = r["hp"][0]
        k_sample[0, 16 * c:16 * (c + 1)] = r["ks"].reshape(16, 128, 4, 64)
        v_sample[0, 16 * c:16 * (c + 1)] = r["vs"].reshape(16, 128, 4, 64)
        conv_sample[0, 16 * c:16 * (c + 1)] = r["cs"].reshape(16, 3, 1024)
        h_sample[0, 16 * c:16 * (c + 1)] = r["hs"]
    return (y_prompt, y_sample, k_prompt, v_prompt, conv_prompt, h_prompt, k_sample, v_sample, conv_sample, h_sample)


_STAGE = "full"


def kernel(**inputs):
    maps = _prep(inputs)
    nc = build(_STAGE)
    res = run_bass_kernel_spmd(nc, maps, core_ids=list(range(NCORES)))
    return _assemble(res.results)
```

```python
import math
import sys
from contextlib import ExitStack
import numpy as np
import concourse.bass as bass
import concourse.mybir as mybir
from concourse.bass_utils import run_bass_kernel_spmd

F32 = mybir.dt.float32
BF16 = mybir.dt.bfloat16
U32 = mybir.dt.uint32
AF = mybir.ActivationFunctionType
ALU = mybir.AluOpType
AX = mybir.AxisListType

NCORES = 8
D = 2048
DC = 16
EPS = 1e-6
NPRE = 24
NOWN = 8
THR_TOL = -4e-6


class Prog:
    ENGS = ("pe", "act", "dve", "pool", "sp")
    EPOCH = 50000
    NDMA_SEM = 8

    def __init__(self, nc):
        self.nc = nc
        self.ops = []
        self.last_w = {}
        self.readers = {}
        self.barrier_deps = set()
        self.last_on_eng = {}
        self.dma_rr = {q: 0 for q in ("sp", "act", "pool")}
        self.dma_last = {}
        self.dma_last_bar = {}
        self.dma_cnt = {}

    def _nm(self, ap):
        sp = str(ap.space)
        if sp == "SB":
            return self.sb.lookup(ap)
        if sp == "PSUM":
            return self.ps.lookup(ap)
        return "dram:" + ap.name

    def begin_stream(self):
        self.pending = []
        self.marks = []

    def mark_split(self):
        if getattr(self, "pending", None) is not None:
            self.marks.append(len(self.pending))

    def commit_prefix_pairs(self, streams, lag=0):
        pairs = [streams[i:i + 2] for i in range(0, len(streams), 2)]

        def front(pair):
            out = []
            for k in range(3):
                for ops, m in pair:
                    out.extend(ops[(0 if k == 0 else m[k - 1]):m[k]])
            return out

        def back(pair):
            out = []
            ba = pair[0][0][pair[0][1][2]:]
            bb = pair[1][0][pair[1][1][2]:] if len(pair) > 1 else []
            ia = ib = 0
            while ia < len(ba) or ib < len(bb):
                if ia < len(ba):
                    out.append(ba[ia])
                    ia += 1
                if ib < len(bb) and (ia >= ib + 1 + lag or ia >= len(ba)):
                    out.append(bb[ib])
                    ib += 1
            return out

        def run(ops):
            for o in ops:
                self._add(*o)
        run(front(pairs[0]))
        if len(pairs) > 1:
            run(front(pairs[1]))
        for p in range(len(pairs)):
            bl = back(pairs[p])
            fl = front(pairs[p + 2]) if p + 2 < len(pairs) else []
            nb, nf = len(bl), len(fl)
            jf = 0
            for ib_, o in enumerate(bl):
                self._add(*o)
                tgt = (ib_ + 1) * nf // nb if nb else nf
                while jf < tgt:
                    self._add(*fl[jf])
                    jf += 1
            while jf < nf:
                self._add(*fl[jf])
                jf += 1

    def end_stream(self):
        p, self.pending = self.pending, None
        return p, list(self.marks)

    def commit_spread(self, streams):
        def run(ops):
            for o in ops:
                self._add(*o)
        run(streams[0][0][:streams[0][1][0]])
        for j, (ops, m) in enumerate(streams):
            rl = ops[m[0]:]
            fl = streams[j + 1][0][:streams[j + 1][1][0]] if j + 1 < len(streams) else []
            nb, nf = len(rl), len(fl)
            jf = 0
            for ib_, o in enumerate(rl):
                self._add(*o)
                tgt = (ib_ + 1) * nf // nb if nb else nf
                while jf < tgt:
                    self._add(*fl[jf])
                    jf += 1
            while jf < nf:
                self._add(*fl[jf])
                jf += 1

    def _add(self, eng, fn, r, w, kind, extra=None):
        if getattr(self, "pending", None) is not None and extra != "commit":
            fr = sys._getframe(2)
            self.pending.append((eng, fn, r, w, kind, "commit"))
            return -1
        idx = len(self.ops)
        deps = set(self.barrier_deps)
        for k in r:
            if k in self.last_w:
                deps.add(self.last_w[k])
        for k in w:
            if k in self.last_w:
                deps.add(self.last_w[k])
            for rr in self.readers.get(k, ()):
                deps.add(rr)
        fr = sys._getframe(2)
        lines = []
        while fr is not None and len(lines) < 4:
            lines.append(fr.f_lineno)
            fr = fr.f_back
        self._nobar = (kind == "d_bg")
        if kind == "d_bg":
            kind = "d"
        op = dict(eng=eng, fn=fn, deps=deps, kind=kind, line=lines)
        if kind == "d":
            slot = self.dma_rr[eng]
            self.dma_rr[eng] = (slot + 1) % self.NDMA_SEM
            key = (eng, slot)
            if key in self.dma_last:
                deps.add(self.dma_last[key])
            self.dma_last[key] = idx
            if not getattr(self, "_nobar", False):
                self.dma_last_bar[key] = idx
            self.dma_cnt[key] = self.dma_cnt.get(key, 0) + 1
            op["dsem"] = key
            op["dval"] = 16 * self.dma_cnt[key]
        else:
            self.last_on_eng[eng] = idx
        deps.discard(idx)
        for k in r:
            self.readers.setdefault(k, []).append(idx)
        for k in w:
            self.last_w[k] = idx
            self.readers[k] = []
        self.ops.append(op)
        return idx

    def barrier(self):
        s = set(self.last_on_eng.values())
        s.update(self.dma_last_bar.values())
        self.barrier_deps = s

    def op(self, eng, fn, ins, outs, r=None, w=None):
        rr = list(r) if r is not None else [self._nm(a) for a in ins]
        ww = list(w) if w is not None else [self._nm(a) for a in outs]
        return self._add(eng, fn, rr, ww, "c")

    def dma(self, q, out, in_, r=None, w=None, background=False):
        rr = list(r) if r is not None else [self._nm(in_)]
        ww = list(w) if w is not None else [self._nm(out)]
        return self._add(q, lambda e: e.dma_start(out=out, in_=in_), rr, ww, "d_bg" if background else "d")

    def tt(self, eng, out, in0, in1, op, **kw):
        return self.op(eng, lambda e: e.tensor_tensor(out=out, in0=in0, in1=in1, op=op), [in0, in1], [out], **kw)

    def ts(self, eng, out, in0, s1, s2, op0, op1=None, accum=None, **kw):
        ins = [in0] + [s for s in (s1, s2) if not isinstance(s, (int, float)) and s is not None]
        outs = [out] + ([accum] if accum is not None else [])

        def fn(e):
            kws = {}
            if op1 is not None:
                kws["op1"] = op1
            if accum is not None:
                kws["accum_out"] = accum
            return e.tensor_scalar(out=out, in0=in0, scalar1=s1, scalar2=s2, op0=op0, **kws)
        return self.op(eng, fn, ins, outs, **kw)

    def stt(self, out, in0, scalar, in1, op0, op1, **kw):
        ins = [in0, in1] + ([scalar] if not isinstance(scalar, (int, float)) else [])
        return self.op("dve", lambda e: e.scalar_tensor_tensor(out=out, in0=in0, scalar=scalar, in1=in1, op0=op0, op1=op1),
                       ins, [out], **kw)

    def actf(self, out, in_, func, bias=None, scale=None, accum=None, **kw):
        ins = [in_] + [s for s in (bias, scale) if s is not None and not isinstance(s, (int, float))]
        outs = [out] + ([accum] if accum is not None else [])

        def fn(e):
            kws = {}
            if bias is not None:
                kws["bias"] = bias
            if scale is not None:
                kws["scale"] = scale
            if accum is not None:
                kws["accum_out"] = accum
            return e.activation(out=out, in_=in_, func=func, **kws)
        return self.op("act", fn, ins, outs, **kw)

    def copy(self, eng, out, in_, **kw):
        if eng == "act":
            return self.op("act", lambda e: e.copy(out=out, in_=in_), [in_], [out], **kw)
        return self.op(eng, lambda e: e.tensor_copy(out=out, in_=in_), [in_], [out], **kw)

    def mm(self, out, lhsT, rhs, start, stop, **kw):
        return self.op("pe", lambda e: e.matmul(out, lhsT=lhsT, rhs=rhs, start=start, stop=stop), [lhsT, rhs], [out], **kw)

    def tr(self, out, in_, ident, **kw):
        return self.op("pe", lambda e: e.transpose(out=out, in_=in_, identity=ident), [in_, ident], [out], **kw)

    def scan(self, out, d0, d1, init, **kw):
        ins = [d0, d1] + ([init] if not isinstance(init, (int, float)) else [])
        return self.op("dve", lambda e: e.tensor_tensor_scan(out=out, data0=d0, data1=d1, initial=init, op0=ALU.mult, op1=ALU.add),
                       ins, [out], **kw)

    def max8(self, out, in_, **kw):
        return self.op("dve", lambda e: e.max(out=out, in_=in_), [in_], [out], **kw)

    def maxidx(self, out, in_max, in_values, **kw):
        return self.op("dve", lambda e: e.max_index(out=out, in_max=in_max, in_values=in_values), [in_max, in_values], [out], **kw)

    def matchrep(self, out, in_to_replace, in_values, imm, **kw):
        return self.op("dve", lambda e: e.match_replace(out=out, in_to_replace=in_to_replace, in_values=in_values, imm_value=imm),
                       [in_to_replace, in_values], [out], **kw)

    def recip(self, out, in_, **kw):
        return self.op("dve", lambda e: e.reciprocal(out=out, in_=in_), [in_], [out], **kw)

    def memset(self, eng, ap, val, **kw):
        return self.op(eng, lambda e: e.memset(ap, val), [], [ap], **kw)

    def reduce(self, out, in_, op, **kw):
        return self.op("dve", lambda e: e.tensor_reduce(out=out, in_=in_, axis=AX.X, op=op), [in_], [out], **kw)

    def simulate(self):
        sems = {}
        pos = {e: 0 for e in self.ENGS}
        progress = True
        while progress:
            progress = False
            for e in self.ENGS:
                tr = self.trace[e]
                while pos[e] < len(tr):
                    ev = tr[pos[e]]
                    if ev[0] == "w":
                        if sems.get(ev[1], 0) >= ev[2]:
                            pos[e] += 1
                            progress = True
                        else:
                            break
                    else:
                        sems[ev[1]] = sems.get(ev[1], 0) + ev[2]
                        pos[e] += 1
                        progress = True
        stuck = {e: (pos[e], len(self.trace[e])) for e in self.ENGS if pos[e] < len(self.trace[e])}
        if stuck:
            for e in stuck:
                ev = self.trace[e][pos[e]]
                print(f"[simulate] DEADLOCK: {e} blocked at event {pos[e]}/{len(self.trace[e])}: {ev} cur={sems.get(ev[1], 0)}")
            raise RuntimeError("deadlock in emitted program")
        print("[simulate] no deadlock")

    def emit(self, es):
        nc = self.nc
        ops = self.ops
        needed = set()
        for o in ops:
            for d in o["deps"]:
                needed.add(d)
        cnt = {e: 0 for e in self.ENGS}
        for i, o in enumerate(ops):
            if o["kind"] == "c" and i in needed:
                cnt[o["eng"]] += 1
                o["ticket"] = cnt[o["eng"]]
        csem = {}
        for e in self.ENGS:
            n_ep = (cnt[e] + self.EPOCH - 1) // self.EPOCH
            csem[e] = [es.enter_context(nc.semaphore(f"c_{e}_{k}")) for k in range(max(1, n_ep))]
        dsem = {}
        for key in self.dma_cnt:
            dsem[key] = es.enter_context(nc.semaphore(f"d_{key[0]}_{key[1]}"))
        block = es.enter_context(nc.Block())
        EP = self.EPOCH
        self.trace = {e: [] for e in self.ENGS}

        def emit_engine(eng, e):
            waited = {}

            def wait(semkey, sem, val):
                if waited.get(semkey, 0) >= val:
                    return
                waited[semkey] = val
                e.wait_ge(sem, val)
                self.trace[eng].append(("w", semkey, val))

            for i, o in enumerate(ops):
                if o["eng"] != eng:
                    continue
                for d in sorted(o["deps"]):
                    od = ops[d]
                    if od["kind"] == "d":
                        wait(("d",) + od["dsem"], dsem[od["dsem"]], od["dval"])
                    else:
                        if od["eng"] == "pe" and eng == "pe" and o["kind"] == "c":
                            continue
                        t = od["ticket"]
                        ep = (t - 1) // EP
                        wait(("c", od["eng"], ep), csem[od["eng"]][ep], (t - 1) % EP + 1)
                try:
                    ins = o["fn"](e)
                except Exception:
                    print(f"[emit] failure in op recorded at lines {o['line']} on {eng}")
                    raise
                if o["kind"] == "d":
                    ins.then_inc(dsem[o["dsem"]], 16)
                    self.trace[eng].append(("i", ("d",) + o["dsem"], 16, i))
                elif "ticket" in o:
                    ep = (o["ticket"] - 1) // EP
                    ins.then_inc(csem[eng][ep], 1)
                    self.trace[eng].append(("i", ("c", eng, ep), 1, i))
            if eng == "sp":
                for key, c in self.dma_cnt.items():
                    e.wait_ge(dsem[key], 16 * c)

        @block.sync
        def _(e):
            emit_engine("sp", e)

        @block.scalar
        def _(e):
            emit_engine("act", e)

        @block.vector
        def _(e):
            emit_engine("dve", e)

        @block.gpsimd
        def _(e):
            emit_engine("pool", e)

        @block.tensor
        def _(e):
            emit_engine("pe", e)


class Arena:
    def __init__(self, prog, tensor, nbytes, tag):
        self.P = prog
        self.t = tensor
        self.n = nbytes
        self.tag = tag
        self.top = 0
        self.regions = []
        self.peak = 0
        self.gen = 0

    def alloc(self, name, free_shape, dtype, align=64):
        esz = {F32: 4, BF16: 2, U32: 4}[dtype]
        n = esz
        for s in free_shape:
            n *= s
        start = (self.top + align - 1) // align * align
        end = start + n
        if end > self.n:
            raise RuntimeError(f"{self.tag} arena overflow allocating {name}: {end} > {self.n}")
        self.top = end
        self.peak = max(self.peak, end)
        self.gen += 1
        self.regions.append((start, end, f"{self.tag}:{name}:{self.gen}"))
        ap = self.t[:, start // 4:(end + 3) // 4]
        if dtype != F32:
            ap = ap.bitcast(dtype)[:, 0:n // esz]
        if len(free_shape) > 1:
            names = " ".join(f"d{i}" for i in range(len(free_shape)))
            kw = {f"d{i}": free_shape[i] for i in range(1, len(free_shape))}
            ap = ap.rearrange(f"p ({names}) -> p {names}", **kw)
        return ap

    def mark(self):
        return (self.top, len(self.regions))

    def release(self, m):
        self.top, nreg = m
        del self.regions[nreg:]
        self.P.barrier()

    def lookup(self, ap):
        esz = {F32: 4, BF16: 2, U32: 4}[ap.dtype]
        pitch = ap.ap[0][0]
        off = (ap.offset % pitch) * esz
        for (s, e, nme) in reversed(self.regions):
            if s <= off < e:
                return nme
        raise RuntimeError(f"{self.tag} lookup failed for offset {off}")


def bc(ap, shape):
    return ap.broadcast_to(list(shape))


class WNames:
    def __init__(self, names, mode, blk=512):
        self.names, self.mode, self.blk = names, mode, blk

    def at(self, dc, col):
        return self.names[dc] if self.mode == "dc" else self.names[col // self.blk]


def build(stage="full"):
    nc = bass.Bass("TRN2", target_bir_lowering=False)

    def din(name, shape):
        return nc.dram_tensor(name, list(shape), F32, kind="ExternalInput").ap()

    def dout(name, shape):
        return nc.dram_tensor(name, list(shape), F32, kind="ExternalOutput").ap()

    I = {}
    for name, shape in [
        ("xp", (1024, D)), ("xs", (128, D)), ("xpre", (NPRE * 128, D)), ("pmask", (128, NPRE)),
        ("rope", (128, 10, 64)), ("amask", (128, 3, 128)), ("smask", (128, 17, 128)),
        ("ck", (16, 128, 256)), ("cv", (16, 128, 256)), ("sconv", (48, 1024)), ("sh", (16, 1024)),
        ("w_in", (D, 3584)), ("w_out", (D, D)), ("w_pq", (D, D)), ("uT", (128, 128, 16, 128)),
        ("pv", (16384, D)), ("skT", (128, 16, 128)),
        ("gmixT", (128, 16)), ("gffnT", (128, 16)), ("gaoT", (128, 8)), ("gloT", (128, 8)),
        ("gqk", (128, 1280)), ("sinks", (128, 16)), ("cw", (128, 8, 4)), ("cb", (128, 8)),
        ("ba", (128, 8)), ("bx", (128, 8)), ("lam", (128, 8)), ("bda", (128, 8, 128)), ("bdx", (128, 8, 128)),
        ("identf", (128, 128)), ("iota", (128, 128)),
    ]:
        I[name] = din(name, shape)
    O = {}
    for name, shape in [
        ("yp", (1024, D)), ("ys", (128, D)), ("kp", (128, 256)), ("vp", (128, 256)), ("cp", (3, 1024)),
        ("hp", (1, 1024)), ("ks", (16, 128, 256)), ("vs", (16, 128, 256)), ("cs", (48, 1024)), ("hs", (16, 1024)),
    ]:
        O[name] = dout(name, shape)

    ub16 = nc.dram_tensor("ub16", [128, 128, 2048], BF16, kind="Internal").ap()
    vb16 = nc.dram_tensor("vb16", [16384, 2048], BF16, kind="Internal").ap()
    P = Prog(nc)
    es = ExitStack()
    win16 = nc.dram_tensor("win16", [D, 3584], BF16, kind="Internal").ap()
    wout16 = nc.dram_tensor("wout16", [D, D], BF16, kind="Internal").ap()
    wpq16 = nc.dram_tensor("wpq16", [D, D], BF16, kind="Internal").ap()
    W16 = {"w_in": win16, "w_out": wout16, "w_pq": wpq16}
    conv_jobs = []
    for wn_ in ("w_in", "w_out", "w_pq"):
        for dc in range(DC):
            conv_jobs.append((W16[wn_][dc * 128:(dc + 1) * 128, :], I[wn_][dc * 128:(dc + 1) * 128, :], f"{wn_}16#{dc}"))
    for ic in range(128):
        conv_jobs.append((ub16[ic], I["uT"][ic].rearrange("p a b -> p (a b)"), f"ub16#{ic}"))
        conv_jobs.append((vb16[ic * 128:(ic + 1) * 128, :], I["pv"][ic * 128:(ic + 1) * 128, :], f"vb16#{ic}"))
    conv_jobs.reverse()

    def emit_conv(n):
        for _ in range(n):
            if conv_jobs:
                o_, i_, nm_ = conv_jobs.pop()
                P.dma("pool", o_, i_, r=[], w=[nm_], background=True)
    SB_BYTES = 212480
    sb_t = es.enter_context(nc.sbuf_tensor("sb_arena", [128, SB_BYTES // 4], F32))
    ps_t = es.enter_context(nc.psum_tensor("ps_arena", [128, 4096], F32))
    SB = Arena(P, sb_t, SB_BYTES, "sb")
    PS = Arena(P, ps_t, 16384, "ps")
    P.sb, P.ps = SB, PS

    identf = SB.alloc("identf", [128], F32)
    identb = SB.alloc("identb", [128], BF16)
    iota = SB.alloc("iota", [128], F32)
    gmixT = SB.alloc("gmixT", [16], F32)
    gffnT = SB.alloc("gffnT", [16], F32)
    gaoT = SB.alloc("gaoT", [8], F32)
    gloT = SB.alloc("gloT", [8], F32)
    esink = SB.alloc("esink", [16], F32)
    cw = SB.alloc("cw", [8, 4], F32)
    cb = SB.alloc("cb", [8], F32)
    ba = SB.alloc("ba", [8], F32)
    bx = SB.alloc("bx", [8], F32)
    nsp = SB.alloc("nsp", [8], F32)
    bda = SB.alloc("bda", [8, 128], BF16)
    bdx = SB.alloc("bdx", [8, 128], BF16)
    amask = SB.alloc("amask", [3, 128], BF16)
    rope = SB.alloc("rope", [10, 64], F32)
    pmask = SB.alloc("pmask", [NPRE], F32)
    onesb = SB.alloc("onesb", [1], BF16)
    epsc = SB.alloc("epsc", [1], F32)
    hist = SB.alloc("hist", [8, 3], F32)
    hstate = SB.alloc("hstate", [8], F32)
    KT = [SB.alloc(f"KT{i}", [2, 128], BF16) for i in range(2)]
    vaug = [SB.alloc(f"vaug{i}", [4, 65], BF16) for i in range(2)]

    for nm, t in [("identf", identf), ("iota", iota), ("gmixT", gmixT), ("gffnT", gffnT), ("gaoT", gaoT), ("gloT", gloT),
                  ("sinks", esink), ("cw", cw), ("cb", cb), ("ba", ba), ("bx", bx), ("lam", nsp),
                  ("rope", rope), ("pmask", pmask)]:
        P.dma("sp", t, I[nm])
    for nm, t in [("bda", bda), ("bdx", bdx), ("amask", amask), ("identf", identb)]:
        P.dma("pool", t, I[nm])
    P.memset("dve", onesb, 1.0)
    P.memset("dve", epsc, EPS)
    P.memset("dve", hist, 0.0)
    P.memset("dve", hstate, 0.0)
    for i in range(2):
        P.memset("dve", vaug[i], 1.0)
        P.memset("dve", KT[i], 0.0)
    P.actf(esink, esink, AF.Exp)
    P.actf(nsp, nsp, AF.Exp, scale=-1.0)
    tmp8 = SB.alloc("tmp8", [8], F32)
    P.ts("dve", tmp8, nsp, 1.0, None, ALU.add)
    P.actf(nsp, tmp8, AF.Ln)
    P.ts("dve", nsp, nsp, -8.0, None, ALU.mult)

    state = dict(hslot=1, kvslot=1)

    def load_w(tile_ap, dram, r0, c0, c1, nchunks, tag):
        names = []
        for dc in range(nchunks):
            nm = f"{tag}#{dc}"
            P.dma("pool", tile_ap[:, dc, :], dram[r0 + dc * 128:r0 + (dc + 1) * 128, c0:c1], w=[nm])
            names.append(nm)
        return WNames(names, "dc")

    def load_w16(tile_ap, wname, c0, c1, tag):
        names = []
        src = W16[wname].rearrange("(dc p) n -> p dc n", p=128)
        for k, cb0 in enumerate(range(c0, c1, 512)):
            nm = f"{tag}#c{k}"
            P.dma("sp" if k % 2 == 0 else "act", tile_ap[:, :, cb0 - c0:cb0 - c0 + 512], src[:, :, cb0:cb0 + 512],
                  r=[f"{wname}16#{dc}" for dc in range(DC)], w=[nm])
            names.append(nm)
        return WNames(names, "col")

    def norm_T(xsrc, gT, hT_dst, scr, psT, sfx):
        P.actf(scr["junk"], xsrc, AF.Square, accum=scr["ssq"])
        P.actf(scr["rms"], scr["ssq"], AF.Sqrt, bias=epsc, scale=1.0 / D)
        P.recip(scr["rstd"], scr["rms"])
        P.ts("dve", scr["xn"], xsrc, scr["rstd"], None, ALU.mult)
        psTb = psT.bitcast(BF16).rearrange("p (c t) -> p c t", t=128)
        for dc in range(DC):
            P.tr(psTb[:, dc, :], scr["xn"][:, dc * 128:(dc + 1) * 128], identb)
        P.tt("dve", hT_dst, psTb[:, 0:DC, :], bc(gT.unsqueeze(2), [128, DC, 128]), ALU.mult)

    def alloc_norm_scr():
        xn = SB.alloc("xn", [D], BF16)
        return dict(junk=xn, ssq=SB.alloc("ssq", [1], F32), rms=SB.alloc("rms", [1], F32),
                    rstd=SB.alloc("rstd", [1], F32), xn=xn)

    def alloc_lru_scr():
        return dict(xpb=SB.alloc("xpb", [8, 176], F32), xc=SB.alloc("xc", [8, 128], F32), xcb=SB.alloc("xcb", [8, 128], BF16),
                    tmp=SB.alloc("ltmp", [8, 128], F32), r=SB.alloc("r", [8, 128], F32), ig=SB.alloc("ig", [8, 128], F32),
                    a=SB.alloc("a", [8, 128], F32), u=SB.alloc("u", [8, 128], F32), gy=SB.alloc("gy", [8, 128], F32),
                    lob=SB.alloc("lob", [8, 128], BF16), cst=SB.alloc("cst", [8, 48], F32), h0T=SB.alloc("h0T", [8, 16], F32),
                    ost=SB.alloc("ost", [1024], F32), h=SB.alloc("hcur", [8, 128], F32))

    def lru_tile(kind, hnT, win_lru, wnames, S, psX, psS, mixT_dst, ssq_dst, pm_col=None, outs=None, sample_in=None,
                 hprev=None, psG=None, psOut=None):
        npart = 16 if kind != "pre" else 8
        psXv = psX.rearrange("p (c t) -> p c t", t=128)
        for cc in range(npart):
            for dc in range(DC):
                P.mm(psXv[:, cc, :], win_lru[:, dc, cc * 128:(cc + 1) * 128], hnT[:, dc, :], dc == 0, dc == DC - 1,
                     r=[wnames.at(dc, cc * 128), P._nm(hnT)])
        P.mark_split()
        psGv = psXv if psG is None else psG.rearrange("p (c t) -> p c t", t=128)
        hcur = S["h"]
        if kind != "smp":
            xpv = S["xpb"][:, :, 0:131]
            P.copy("act", xpv[:, :, 3:131], psXv[:, 0:8, :])
            P.copy("dve", xpv[:, :, 0:3], hist)
            P.copy("dve", hist, xpv[:, :, 128:131])

            def sl(j):
                return xpv[:, :, j:j + 128]

            def cwb(j):
                return bc(cw[:, :, j:j + 1], [128, 8, 128])
            cbb = bc(cb.unsqueeze(2), [128, 8, 128])
            xcv, tmpv = S["xc"], S["tmp"]
        else:
            xp4 = S["xpb"].rearrange("p c (b l) -> p c b l", l=11)
            P.copy("act", xp4[:, :, :, 3:11], psXv[:, 0:8, :].rearrange("p c (b l) -> p c b l", l=8))
            sct, sht = sample_in
            psC = psS.rearrange("p (c n) -> p c n", n=64)
            for cc in range(8):
                P.tr(psC[:, cc, 0:48], sct[:, cc * 128:(cc + 1) * 128], identf[0:48, 0:48])
            P.copy("dve", xp4[:, :, :, 0:3], psC[:, :, 0:48].rearrange("p c (b j) -> p c b j", j=3))
            for cc in range(8):
                P.tr(psC[:, cc, 48:64], sht[:, cc * 128:(cc + 1) * 128], identf[0:16, 0:16])
            P.copy("dve", S["h0T"], psC[:, :, 48:64])

            def sl(j):
                return xp4[:, :, :, j:j + 8]

            def cwb(j):
                return bc(cw[:, :, j:j + 1].unsqueeze(3), [128, 8, 16, 8])
            cbb = bc(cb.unsqueeze(2).unsqueeze(3), [128, 8, 16, 8])
            xcv = S["xc"].rearrange("p c (b l) -> p c b l", l=8)
            tmpv = S["tmp"].rearrange("p c (b l) -> p c b l", l=8)
        if kind != "pre":
            P.actf(S["gy"], psXv[:, 8:16, :], AF.Gelu_apprx_tanh)
        P.mark_split()
        P.tt("dve", xcv, sl(0), cwb(0), ALU.mult)
        for j in range(1, 4):
            P.tt("dve", tmpv, sl(j), cwb(j), ALU.mult)
            P.tt("dve", xcv, xcv, tmpv, ALU.add)
        P.tt("dve", xcv, xcv, cbb, ALU.add)
        P.copy("act", S["xcb"], S["xc"])
        go = 0 if (kind == "pre" or psG is not None) else 8
        for cc in range(8):
            P.mm(psGv[:, cc, :], bda[:, cc, :], S["xcb"][:, cc, :], True, True)
        if go:
            for cc in range(8):
                P.mm(psGv[:, go + cc, :], bdx[:, cc, :], S["xcb"][:, cc, :], True, True)
        for cc in range(8):
            P.actf(S["r"][:, cc, :], psGv[:, cc, :], AF.Sigmoid, bias=ba[:, cc:cc + 1])
        if not go:
            for cc in range(8):
                P.mm(psGv[:, cc, :], bdx[:, cc, :], S["xcb"][:, cc, :], True, True)
        for cc in range(8):
            P.actf(S["ig"][:, cc, :], psGv[:, go + cc, :], AF.Sigmoid, bias=bx[:, cc:cc + 1])
        for cc in range(8):
            P.actf(S["a"][:, cc, :], S["r"][:, cc, :], AF.Exp, scale=nsp[:, cc:cc + 1])
        P.tt("dve", S["tmp"], S["a"], S["a"], ALU.mult)
        P.ts("dve", S["tmp"], S["tmp"], -1.0, 1.0, ALU.mult, ALU.add)
        P.actf(S["tmp"], S["tmp"], AF.Sqrt)
        P.tt("dve", S["u"], S["ig"], S["xc"], ALU.mult)
        if kind == "pre":
            P.stt(S["u"], S["u"], pm_col, S["tmp"], ALU.mult, ALU.mult)
        else:
            P.tt("dve", S["u"], S["u"], S["tmp"], ALU.mult)
        if kind == "smp":
            a4 = S["a"].rearrange("p c (b l) -> p c b l", l=8)
            u4 = S["u"].rearrange("p c (b l) -> p c b l", l=8)
            t3 = S["cst"][:, :, 0:16]
            P.tt("dve", t3, a4[:, :, :, 0], S["h0T"], ALU.mult)
            P.tt("dve", u4[:, :, :, 0], u4[:, :, :, 0], t3, ALU.add)
            P.memset("dve", a4[:, :, :, 0], 0.0)
        for cc in range(8):
            if kind == "smp":
                init = 0.0
            elif hprev is not None:
                init = hprev[:, cc, 127:128]
            else:
                init = hstate[:, cc:cc + 1]
            P.scan(hcur[:, cc, :], S["a"][:, cc, :], S["u"][:, cc, :], init)
        if kind != "smp":
            P.copy("dve", hstate, hcur[:, :, 127])
        if kind == "pre":
            return
        P.tt("dve", S["gy"], hcur, S["gy"], ALU.mult)
        P.tt("dve", S["lob"], S["gy"], S["gy"], ALU.mult)
        for cc in range(8):
            P.mm(psS[:, 0:1], S["lob"][:, cc, :], onesb, cc == 0, cc == 7)
        P.copy("dve", ssq_dst, psS[:, 0:1])
        P.tt("dve", mixT_dst, S["gy"], bc(gloT.unsqueeze(2), [128, 8, 128]), ALU.mult)
        pso = psX if psOut is None else psOut
        if outs is not None:
            psC = psS.rearrange("p (c n) -> p c n", n=64)
            if kind == "own":
                P.copy("dve", S["cst"][:, :, 0:3], xpv[:, :, 128:131])
                P.copy("dve", S["cst"][:, :, 3:4], hcur[:, :, 127:128])
                n1, n2 = 3, 1
            else:
                P.copy("dve", S["cst"].rearrange("p c (b j) -> p c b j", j=3), xp4[:, :, :, 8:11])
                P.copy("dve", S["h0T"], hcur.rearrange("p c (b l) -> p c b l", l=8)[:, :, :, 7])
                n1, n2 = 48, 16
            for which, n, dst in ((0, n1, outs[0]), (1, n2, outs[1])):
                for cc in range(8):
                    if kind == "own":
                        src = S["cst"][:, cc, 0:3] if which == 0 else S["cst"][:, cc, 3:4]
                    else:
                        src = S["cst"][:, cc, :] if which == 0 else S["h0T"][:, cc, :]
                    P.tr(pso[0:n, cc * 128:(cc + 1) * 128], src, identf)
                ostv = S["ost"]
                P.copy("act", ostv[0:n, :], pso[0:n, 0:1024])
                P.dma("sp", dst, ostv[0:n, :])

    def alloc_attn_scr():
        sq = SB.alloc("sq", [20, 64], F32)
        aob = SB.alloc("aob", [1024], BF16)
        gqk_t = SB.alloc("gqk", [20, 64], F32)
        P.dma("sp", gqk_t, I["gqk"])
        return dict(qkv=SB.alloc("qkv", [1536], F32), sq=sq, st=SB.alloc("qst", [20], F32),
                    qkn=sq, gqk=gqk_t, qkr=SB.alloc("qkr", [20, 64], F32),
                    t1=SB.alloc("t1", [20, 32], F32), t2=SB.alloc("t2", [20, 32], F32),
                    qb=SB.alloc("qb", [16, 64], BF16), kb=SB.alloc("kb", [4, 64], BF16), QT=SB.alloc("QT", [8, 128], BF16),
                    E=[SB.alloc(f"E{i}", [2, 512], BF16) for i in range(2)], den=SB.alloc("den", [4], F32),
                    ao=SB.alloc("ao", [16, 64], F32), aob=aob, junk=aob,
                    sa=SB.alloc("sa", [1], F32))

    def qkv_tile(hnT, win_qkv, wnames, A, psQ, psT, rope_idx, slot):
        for nb in range(3):
            for dc in range(DC):
                P.mm(psQ[:, nb * 512:(nb + 1) * 512], hnT[:, dc, :], win_qkv[:, dc, nb * 512:(nb + 1) * 512], dc == 0, dc == DC - 1,
                     r=[wnames.at(dc, nb * 512), P._nm(hnT)])
        P.copy("act", A["qkv"], psQ[:, 0:1536], r=[nB0, nB1])
        qk = A["qkv"][:, 0:1280].rearrange("p (h d) -> p h d", d=64)
        P.tt("dve", A["sq"], qk, qk, ALU.mult)
        P.reduce(A["st"], A["sq"], ALU.add)
        P.actf(A["st"], A["st"], AF.Sqrt, bias=epsc, scale=1.0 / 64)
        P.recip(A["st"], A["st"])
        P.tt("dve", A["qkn"], qk, bc(A["st"].unsqueeze(2), [128, 20, 64]), ALU.mult)
        P.tt("dve", A["qkn"], A["qkn"], A["gqk"], ALU.mult)
        cosb = bc(rope[:, rope_idx, 0:32].unsqueeze(1), [128, 20, 32])
        sinb = bc(rope[:, rope_idx, 32:64].unsqueeze(1), [128, 20, 32])
        x1, x2 = A["qkn"][:, :, 0:32], A["qkn"][:, :, 32:64]
        P.tt("dve", A["t1"], x1, cosb, ALU.mult)
        P.tt("dve", A["t2"], x2, sinb, ALU.mult)
        P.tt("dve", A["qkr"][:, :, 0:32], A["t1"], A["t2"], ALU.subtract)
        P.tt("dve", A["t1"], x2, cosb, ALU.mult)
        P.tt("dve", A["t2"], x1, sinb, ALU.mult)
        P.tt("dve", A["qkr"][:, :, 32:64], A["t1"], A["t2"], ALU.add)
        for gp in range(2):
            src = A["qkr"][:, 8 * gp:8 * gp + 8, :].rearrange("p (half r) d -> p half r d", r=4)
            dst = A["qb"][:, 8 * gp:8 * gp + 8, :].rearrange("p (r half) d -> p half r d", half=2)
            P.copy("act", dst, src)
        P.copy("act", A["kb"], A["qkr"][:, 16:20, :])
        P.copy("act", vaug[slot][:, :, 0:64], A["qkv"][:, 1280:1536].rearrange("p (h d) -> p h d", d=64))
        psTb = psT.bitcast(BF16).rearrange("p (c t) -> p c t", t=128)
        for j in range(8):
            P.tr(psTb[:, j, :], A["qb"][:, 2 * j:2 * j + 2, :].rearrange("p h d -> p (h d)"), identb)
        for j in range(2):
            P.tr(psTb[:, 8 + j, :], A["kb"][:, 2 * j:2 * j + 2, :].rearrange("p h d -> p (h d)"), identb)
        P.copy("dve", A["QT"], psTb[:, 0:8, :])
        P.copy("dve", KT[slot], psTb[:, 8:10, :])

    def attn_core(A, kblocks, psL, psO, psT, mixT_dst, rstd_dst, po_banks=None):
        nkb = len(kblocks)
        it = 0
        for g in range(4):
            gp, half = divmod(g, 2)
            rows = slice(64 * half, 64 * half + 64)
            po = psO[g % 2].rearrange("p (r d) -> p r d", d=128)
            for k0 in range(0, nkb, 2):
                ks = kblocks[k0:k0 + 2]
                pl = psL[it % 2].rearrange("p (k n) -> p k n", n=512)
                E = A["E"][it % 2]
                it += 1
                for ki, (kt, va, mk) in enumerate(ks):
                    P.mm(pl[:, ki, :].rearrange("p (r q) -> p r q", q=128), kt[rows, gp, :], A["QT"][rows, 4 * gp:4 * gp + 4, :], True, True)
                n = len(ks)
                P.actf(E[:, 0:n, :], pl[:, 0:n, :], AF.Exp, scale=0.125)
                for ki, (kt, va, mk) in enumerate(ks):
                    Ev = E[:, ki, :].rearrange("p (r q) -> p r q", q=128)
                    P.tt("dve", Ev, Ev, bc(mk.unsqueeze(1), [128, 4, 128]), ALU.mult)
                for r in range(4):
                    dst = po[:, r, 0:65] if po_banks is None else po_banks[r][:, 0:65]
                    for ki, (kt, va, mk) in enumerate(ks):
                        P.mm(dst, E[:, ki, r * 128:(r + 1) * 128], va[:, g, :], (k0 + ki) == 0, (k0 + ki) == nkb - 1)
            if po_banks is None:
                P.tt("dve", A["den"], po[:, :, 64], esink[:, 4 * g:4 * g + 4], ALU.add)
                P.recip(A["den"], A["den"])
                P.tt("dve", A["ao"][:, 4 * g:4 * g + 4, :], po[:, :, 0:64], bc(A["den"].unsqueeze(2), [128, 4, 64]), ALU.mult)
            else:
                for r in range(4):
                    P.tt("dve", A["den"][:, r:r + 1], po_banks[r][:, 64:65], esink[:, 4 * g + r:4 * g + r + 1], ALU.add)
                P.recip(A["den"], A["den"])
                for r in range(4):
                    P.ts("dve", A["ao"][:, 4 * g + r, :], po_banks[r][:, 0:64], A["den"][:, r:r + 1], None, ALU.mult)
        aof = A["ao"].rearrange("p h d -> p (h d)")
        P.actf(A["junk"], aof, AF.Square, accum=A["sa"])
        P.actf(A["sa"], A["sa"], AF.Sqrt, bias=epsc, scale=1.0 / 1024)
        P.recip(rstd_dst, A["sa"])
        P.copy("act", A["aob"], aof)
        psTb = psT.bitcast(BF16).rearrange("p (c t) -> p c t", t=128)
        for cc in range(8):
            P.tr(psTb[:, cc, :], A["aob"][:, cc * 128:(cc + 1) * 128], identb)
        P.tt("dve", mixT_dst, psTb[:, 0:8, :], bc(gaoT.unsqueeze(2), [128, 8, 128]), ALU.mult)

    psT = PS.alloc("psT", [1024], F32)
    psB0 = PS.alloc("psB0", [1024], F32)
    psB1 = PS.alloc("psB1", [1024], F32)
    psO = [PS.alloc("psO0", [512], F32), PS.alloc("psO1", [512], F32)]
    psBig = ps_t[:, 1024:3072]
    nB0, nB1 = P._nm(psB0), P._nm(psB1)
    psL = [psB0, psB1]

    m0 = SB.mark()
    hnTs = [SB.alloc(f"hnT{i}", [16, 128], BF16) for i in range(2)]
    m1 = SB.mark()
    winx = SB.alloc("winx", [16, 1024], BF16)
    wxn = load_w(winx, I["w_in"], 0, 1536, 2560, 16, "winx")
    xt = [SB.alloc(f"xt{i}", [D], F32) for i in range(2)]
    NSs = [alloc_norm_scr() for _ in range(2)]
    LSs = [alloc_lru_scr() for _ in range(2)]
    npre = NPRE if stage != "nopre" else 1
    streams = []
    xpbs = [[LSs[0]["xpb"], LSs[1]["xpb"]], [SB.alloc("xpb_c", [8, 176], F32), SB.alloc("xpb_d", [8, 176], F32)]]
    psGs = [psT, ps_t[:, 3072:4096]]
    for ti in range(NPRE - npre, NPRE):
        par = ti % 2
        pr = (ti // 2) % 2
        P.begin_stream()
        P.dma("sp", xt[par], I["xpre"][ti * 128:(ti + 1) * 128, :])
        norm_T(xt[par], gmixT, hnTs[par], NSs[par], psL[par], "p")
        P.mark_split()
        lru_tile("pre", hnTs[par], winx, wxn, dict(LSs[par], xpb=xpbs[pr][par]), psL[par], psO[par], None, None,
                 pm_col=pmask[:, ti:ti + 1], hprev=(LSs[1 - par]["h"] if ti > NPRE - npre else None), psG=psGs[par])
        emit_conv(13)
        streams.append(P.end_stream())
    if len(streams) > 1:
        P.commit_prefix_pairs(streams)
    else:
        for o in streams[0][0]:
            P._add(*o)
    hnT = hnTs[(NPRE - 1) % 2]
    SB.release(m1)
    wqkv = SB.alloc("wqkv", [16, 1536], BF16)
    wqn = load_w16(wqkv, "w_in", 0, 1536, "wqkv")
    A = alloc_attn_scr()
    qkv_tile(hnT, wqkv, wqn, A, psBig, psT, 0, 0)
    state["kvslot"] = 0
    SB.release(m0)

    def tile_rows(ti):
        if ti < NOWN:
            return I["xp"][ti * 128:(ti + 1) * 128, :], O["yp"][ti * 128:(ti + 1) * 128, :]
        return I["xs"], O["ys"]

    def peer_group(g, tiles, hcT):
        qT = SB.alloc("qT", [16, 384], BF16)
        skT = SB.alloc("skT", [16, 128], BF16)
        P.dma("pool", skT, I["skT"])
        mq = SB.mark()
        wpq = SB.alloc("wpq", [16, 2048], BF16)
        wpn = load_w16(wpq, "w_pq", 0, 2048, f"wpq{g}")
        for m in range(16):
            pb = psL[m % 2]
            for dc in range(DC):
                P.mm(pb[:, 0:384], wpq[:, dc, m * 128:(m + 1) * 128], hcT[:, dc, :], dc == 0, dc == DC - 1,
                     r=[wpn.at(dc, m * 128), P._nm(hcT)])
            P.copy("act" if m % 2 == 0 else "dve", qT[:, m, :], pb[:, 0:384])
        SB.release(mq)
        Wsb = SB.alloc("Wsb", [128, 384], BF16)
        mq = SB.mark()
        s_sb = SB.alloc("s_sb", [16, 128], F32)
        s2w = SB.alloc("s2w", [8, 128], F32)
        sv = SB.alloc("sv", [16, 16], F32)
        si = SB.alloc("si", [8, 16], U32)
        cand = [SB.alloc(f"cand{i}", [16, 16], F32) for i in range(2)]
        cand2 = [SB.alloc(f"cand2{i}", [256], F32) for i in range(2)]
        best = SB.alloc("best", [8, 16], F32)
        eb = SB.alloc("eb", [8, 16], F32)
        zz = SB.alloc("zz", [8], F32)
        nthr = SB.alloc("nthr", [8, 16], F32)
        e1 = SB.alloc("e1", [8, 16], BF16)
        e2 = SB.alloc("e2", [8, 128], BF16)
        e2a = s2w
        si1f = SB.alloc("si1f", [8, 16], F32)
        si1T = SB.alloc("si1T", [128], F32)
        JB = 8
        Mk = [SB.alloc(f"Mk{i}", [8, 16, JB], BF16) for i in range(2)]
        Gb = [SB.alloc(f"Gb{i}", [8, 16, JB], BF16) for i in range(2)]
        Rw = [SB.alloc(f"Rw{i}", [128, JB], BF16) for i in range(2)]
        RwT = SB.alloc("RwT", [128, 128], BF16)
        O1 = [SB.alloc(f"O1{i}", [8, 128], BF16) for i in range(2)]
        psTb = psT.bitcast(BF16).rearrange("p (c t) -> p c t", t=128)
        for tt in range(3):
            pss = psBig.rearrange("p (m k) -> p m k", k=128)
            for m in range(16):
                P.mm(pss[:, m, :], qT[:, m, tt * 128:(tt + 1) * 128], skT[:, m, :], True, True)
            P.copy("act", s_sb[:, 0:8, :], pss[:, 0:8, :])
            P.copy("dve", s_sb[:, 8:16, :], pss[:, 8:16, :])
            for m0 in (0, 8):
                ms = range(m0, m0 + 8)
                for m in ms:
                    P.max8(sv[:, m, 0:8], s_sb[:, m, :], w=[f"sv#{m}a"], r=[P._nm(s_sb)])
                for m in ms:
                    if m % 2 == 0:
                        P.maxidx(si[:, m // 2, 0:8], sv[:, m, 0:8], s_sb[:, m, :], r=[f"sv#{m}a", P._nm(s_sb)], w=[f"si#{m}a"])
                for m in ms:
                    P.matchrep(s2w[:, m - m0, :], sv[:, m, 0:8], s_sb[:, m, :], -1e30, r=[f"sv#{m}a", P._nm(s_sb)],
                               w=[f"s2w#{m - m0}"])
                for m in ms:
                    P.max8(sv[:, m, 8:16], s2w[:, m - m0, :], r=[f"s2w#{m - m0}"], w=[f"sv#{m}b"])
                for m in ms:
                    if m % 2 == 0:
                        P.maxidx(si[:, m // 2, 8:16], sv[:, m, 8:16], s2w[:, m - m0, :], r=[f"sv#{m}b", f"s2w#{m - m0}"],
                                 w=[f"si#{m}b"])
            svn = [f"sv#{m}{x}" for m in range(16) for x in "ab"]
            sin = [f"si#{m}{x}" for m in range(0, 16, 2) for x in "ab"]
            for h in range(8):
                cd, cd2 = cand[h % 2], cand2[h % 2]
                cdf = cd.rearrange("p a b -> p (a b)")
                P.tt("dve", cd, bc(sv[:, 2 * h, :].unsqueeze(2), [128, 16, 16]),
                     bc(sv[:, 2 * h + 1, :].unsqueeze(1), [128, 16, 16]), ALU.add,
                     r=[f"sv#{2 * h}a", f"sv#{2 * h}b", f"sv#{2 * h + 1}a", f"sv#{2 * h + 1}b"])
                P.max8(best[:, h, 0:8], cdf, w=[f"best#{h}a"])
                P.matchrep(cd2, best[:, h, 0:8], cdf, -1e30, r=[f"best#{h}a", P._nm(cd)])
                P.max8(best[:, h, 8:16], cd2, w=[f"best#{h}b"])
            bn = [f"best#{h}{x}" for h in range(8) for x in "ab"]
            sv4 = sv.rearrange("p (h c) k -> p h c k", c=2)
            sv1, sv2 = sv4[:, :, 0, :], sv4[:, :, 1, :]
            P.tt("dve", eb, best, bc(best[:, :, 0:1], [128, 8, 16]), ALU.subtract, r=bn)
            P.actf(eb, eb, AF.Exp)
            P.reduce(zz, eb, ALU.add)
            P.actf(zz, zz, AF.Ln)
            P.tt("dve", eb, sv1, bc(sv1[:, :, 0:1], [128, 8, 16]), ALU.subtract, r=svn + [P._nm(eb)])
            P.actf(e1, eb, AF.Exp)
            P.tt("dve", zz, zz, sv2[:, :, 0], ALU.add, r=svn + [P._nm(zz)])
            s2 = s_sb.rearrange("p (h c) k -> p h c k", c=2)[:, :, 1, :]
            s2wn = [f"s2w#{k}" for k in range(8)]
            P.tt("dve", e2a, s2, bc(zz.unsqueeze(2), [128, 8, 128]), ALU.subtract, w=s2wn)
            P.actf(e2, e2a, AF.Exp, r=s2wn)
            P.tt("dve", nthr, bc(best[:, :, 15:16], [128, 8, 16]), sv1, ALU.subtract, r=bn + svn)
            P.ts("dve", nthr, nthr, THR_TOL, None, ALU.add)
            P.copy("dve", si1f, si, r=sin)
            pst = psO[0][:, 0:128]
            P.tr(pst, si1f.rearrange("p h a -> p (h a)"), identf)
            P.copy("act", si1T, pst)
            for jb in range(128 // JB):
                par = jb % 2
                mk_, gb_, rw = Mk[par], Gb[par], Rw[par]
                js = slice(jb * JB, (jb + 1) * JB)
                P.tt("dve", mk_, bc(s2[:, :, js].unsqueeze(2), [128, 8, 16, JB]),
                     bc(nthr.unsqueeze(3), [128, 8, 16, JB]), ALU.is_ge)
                P.tt("pool", gb_, bc(e1.unsqueeze(3), [128, 8, 16, JB]),
                     bc(e2[:, :, js].unsqueeze(2), [128, 8, 16, JB]), ALU.mult)
                P.tt("dve", rw.rearrange("p q j -> p (q j)"), mk_.rearrange("p h a j -> p (h a j)"),
                     gb_.rearrange("p h a j -> p (h a j)"), ALU.mult)
                pn = f"psThalf{par}"
                for jj in range(JB):
                    P.tr(psTb[:, par * 8 + jj, :], rw[:, :, jj], identb, w=[pn])
                P.copy("act", RwT[:, js, :], psTb[:, par * 8:par * 8 + JB, :], r=[pn])
            for tb in range(16):
                o1 = O1[tb % 2]
                P.tt("dve", o1, bc(iota.unsqueeze(1), [128, 8, 128]),
                     bc(si1T[:, tb * 8:(tb + 1) * 8].unsqueeze(2), [128, 8, 128]), ALU.is_equal)
                pw = psL[tb % 2].rearrange("p (t i) -> p t i", i=128)
                for tq in range(8):
                    P.mm(pw[:, tq, :], RwT[:, :, tb * 8 + tq], o1[:, tq, :], True, True)
                t0 = tt * 128 + tb * 8
                P.copy("act", Wsb[:, :, t0:t0 + 8], pw.rearrange("p t i -> p i t"))
        SB.release(mq)
        NR = 4
        Ub = [SB.alloc(f"Ub{i}", [16, 128], BF16) for i in range(NR)]
        NRV = 8
        Vb = [SB.alloc(f"Vb{i}", [1024], BF16) for i in range(NRV)]
        gl = [SB.alloc(f"gl{i}", [384], BF16) for i in range(2)]
        xr1 = [SB.alloc(f"xr1{i}", [1024], F32) for i in range(3)]
        yo = [SB.alloc(f"yo{i}", [1024], F32) for i in range(3)]
        psY = [psT, psB0, psB1]
        for half in range(2):
            c0 = half * 1024

            def stage_a(ic):
                wn = f"W{g}_{ic}"
                vb = Vb[ic % NRV]
                if half == 0:
                    ub = Ub[ic % NR]
                    P.dma("sp", ub, ub16[ic].rearrange("p (a b) -> p a b", b=128), r=[f"ub16#{ic}"])
                    P.dma("sp", vb, vb16[ic * 128:(ic + 1) * 128, 0:1024], r=[f"vb16#{ic}"])
                    pa = psO[ic % 2]
                    for dc in range(DC):
                        P.mm(pa[:, 0:384], ub[:, dc, :], hcT[:, dc, :], dc == 0, dc == DC - 1)
                    P.actf(gl[ic % 2], pa[:, 0:384], AF.Gelu_apprx_tanh)
                    P.tt("dve", Wsb[:, ic, :], gl[ic % 2], Wsb[:, ic, :], ALU.mult, r=[P._nm(gl[ic % 2]), wn], w=[wn])
                else:
                    P.dma("sp", vb, vb16[ic * 128:(ic + 1) * 128, 1024:2048], r=[f"vb16#{ic}"])

            def stage_v(ic):
                wn = f"W{g}_{ic}"
                vb = Vb[ic % NRV]
                for tt in range(3):
                    for nb in range(2):
                        P.mm(psY[tt][:, nb * 512:(nb + 1) * 512], Wsb[:, ic, tt * 128:(tt + 1) * 128],
                             vb[:, nb * 512:(nb + 1) * 512], ic == 0, ic == 127, r=[wn, P._nm(vb)])

            stage_a(0)
            for ic in range(128):
                if ic + 1 < 128:
                    stage_a(ic + 1)
                stage_v(ic)
            for tt, ti in enumerate(tiles):
                _, yout = tile_rows(ti)
                P.dma("sp", xr1[tt], yout[:, c0:c0 + 1024], r=[f"y#{ti}"])
                P.tt("dve", yo[tt], psY[tt], xr1[tt], ALU.add)
                P.dma("sp", yout[:, c0:c0 + 1024], yo[tt], w=[f"y#{ti}#{half}"])

    for g in range(3):
        tiles = [3 * g, 3 * g + 1, 3 * g + 2]
        mG = SB.mark()
        hcT = SB.alloc("hcT", [16, 384], BF16)
        mAB = SB.mark()
        xres = [SB.alloc(f"xres{i}", [D], F32) for i in range(3)]
        mixT = SB.alloc("mixT", [16, 384], BF16)
        hnTg = [SB.alloc(f"hnTg{i}", [16, 128], BF16) for i in range(3)]
        stats = SB.alloc("stats", [3, 2], F32)
        mA1 = SB.mark()
        wqkv = SB.alloc("wqkv", [16, 1536], BF16)
        wqn = load_w16(wqkv, "w_in", 0, 1536, f"wqkv{g}")
        NS = alloc_norm_scr()
        A = alloc_attn_scr()
        for j, ti in enumerate(tiles):
            xin, _ = tile_rows(ti)
            P.dma("sp", xres[j], xin)
            norm_T(xres[j], gmixT, hnTg[j], NS, psT, "a")
            prev = state["kvslot"]
            cur = 1 - prev
            qkv_tile(hnTg[j], wqkv, wqn, A, psBig, psT, 1 + ti, cur)
            mdst = mixT[:, 0:8, j * 128:(j + 1) * 128]
            if ti < NOWN:
                mprev = amask[:, 0, :] if ti == 0 else amask[:, 1, :]
                kbl = [(KT[prev], vaug[prev], mprev), (KT[cur], vaug[cur], amask[:, 2, :])]
                attn_core(A, kbl, psL, psO, psT, mdst, stats[:, j, 0:1])
                if ti == NOWN - 1:
                    P.dma("sp", O["kp"], A["qkr"][:, 16:20, :].rearrange("p h d -> p (h d)"))
                    P.dma("sp", O["vp"], A["qkv"][:, 1280:1536])
            else:
                ckb = SB.alloc("ckb", [16, 256], BF16)
                cvb = ckb
                KTc = SB.alloc("KTc", [16, 2, 128], BF16)
                cva = SB.alloc("cva", [16, 4, 65], BF16)
                smask = SB.alloc("smask", [17, 128], BF16)
                P.dma("pool", ckb, I["ck"].rearrange("b s d -> s b d"))
                P.dma("pool", smask, I["smask"])
                P.dma("sp", O["ks"][:, 0:120, :], I["ck"][:, 8:128, :], w=["ks#bulk"])
                P.dma("sp", O["vs"][:, 0:120, :], I["cv"][:, 8:128, :], w=["vs#bulk"])
                pbb = psBig.bitcast(BF16).rearrange("p (c t) -> p c t", t=128)
                for b in range(16):
                    for gp in range(2):
                        P.tr(pbb[:, b * 2 + gp, :], ckb[:, b, gp * 128:(gp + 1) * 128], identb)
                KTf = KTc.rearrange("p b g s -> p (b g) s")
                P.copy("dve", KTf[:, 0:16, :], pbb[:, 0:16, :])
                P.copy("act", KTf[:, 16:32, :], pbb[:, 16:32, :])
                P.dma("pool", cvb, I["cv"].rearrange("b s d -> s b d"))
                P.memset("dve", cva, 1.0)
                P.copy("act", cva[:, :, :, 0:64], cvb.rearrange("p b (h d) -> p b h d", d=64))
                kbl = [(KTc[:, b, :, :], cva[:, b, :, :], smask[:, b, :]) for b in range(16)]
                kbl.append((KT[cur], vaug[cur], smask[:, 16, :]))
                attn_core(A, kbl, psL, psO, psT, mdst, stats[:, j, 0:1],
                          po_banks=[psO[0], psO[1], psT[:, 0:512], psT[:, 512:1024]])
                for b in range(16):
                    P.dma("sp", O["ks"][b, 120:128, :], A["qkr"][b * 8:(b + 1) * 8, 16:20, :].rearrange("p h d -> p (h d)"),
                          w=[f"ks#{b}"])
                    P.dma("sp", O["vs"][b, 120:128, :], A["qkv"][b * 8:(b + 1) * 8, 1280:1536], w=[f"vs#{b}"])
            state["kvslot"] = cur
            emit_conv(0)
        SB.release(mA1)
        wlru = SB.alloc("wlru", [16, 2048], BF16)
        wln = load_w16(wlru, "w_in", 1536, 3584, f"wlru{g}")
        LS = alloc_lru_scr()
        streams = []
        for j, ti in enumerate(tiles):
            mdst = mixT[:, 8:16, j * 128:(j + 1) * 128]
            if ti >= NOWN:
                sct = SB.alloc("sct", [1024], F32)
                sht = SB.alloc("sht", [1024], F32)
            P.begin_stream()
            if ti < NOWN:
                outs = (O["cp"], O["hp"]) if ti == NOWN - 1 else None
                lru_tile("own", hnTg[j], wlru, wln, LS, psBig, psO[0], mdst, stats[:, j, 1:2], outs=outs,
                         psG=psT, psOut=psT)
            else:
                P.dma("sp", sct[0:48, :], I["sconv"])
                P.dma("sp", sht[0:16, :], I["sh"])
                lru_tile("smp", hnTg[j], wlru, wln, LS, psBig, psO[0], mdst, stats[:, j, 1:2], outs=(O["cs"], O["hs"]),
                         sample_in=(sct[0:48, :], sht[0:16, :]), psG=psT, psOut=psT)
            streams.append(P.end_stream())
        P.commit_spread(streams)
        SB.release(mA1)
        wout = SB.alloc("wout", [16, 2048], BF16)
        won = load_w16(wout, "w_out", 0, 2048, f"wout{g}")
        NS = alloc_norm_scr()
        P.actf(stats[:, :, 1], stats[:, :, 1], AF.Sqrt, bias=epsc, scale=1.0 / 1024)
        P.recip(stats[:, :, 1], stats[:, :, 1])
        for j, ti in enumerate(tiles):
            for nb in range(4):
                pb = psL[nb % 2]
                for half in range(2):
                    for c8 in range(8):
                        cc = half * 8 + c8
                        P.mm(pb[:, half * 512:(half + 1) * 512], mixT[:, cc, j * 128:(j + 1) * 128],
                             wout[:, cc, nb * 512:(nb + 1) * 512], c8 == 0, c8 == 7, r=[won.at(cc, nb * 512), P._nm(mixT)])
                xs_ = xres[j][:, nb * 512:(nb + 1) * 512]
                P.stt(xs_, pb[:, 0:512], stats[:, j, 0:1], xs_, ALU.mult, ALU.add)
                P.stt(xs_, pb[:, 512:1024], stats[:, j, 1:2], xs_, ALU.mult, ALU.add)
            _, yout = tile_rows(ti)
            P.dma("sp", yout, xres[j], w=[f"y#{ti}"])
            emit_conv(0)
            norm_T(xres[j], gffnT, hcT[:, :, j * 128:(j + 1) * 128], NS, psT, "b")
        SB.release(mAB)
        if stage == "mixer":
            SB.release(mG)
            continue
        emit_conv(len(conv_jobs))
        peer_group(g, tiles, hcT)
        SB.release(mG)

    P.emit(es)
    P.simulate()
    es.close()
    print(f"[build] ops={len(P.ops)} sbuf_peak={SB.peak / 1024:.1f} KiB")
    return nc


def _rope_table(pos):
    half = 32
    inv = np.exp(np.float32(-math.log(10000.0)) * np.arange(half, dtype=np.float32) * np.float32(2.0 / 64)).astype(np.float32)
    ang = (pos.astype(np.float32)[:, None] * inv[None, :]).astype(np.float32)
    return np.concatenate([np.cos(ang), np.sin(ang)], axis=-1).astype(np.float32)


def _fm(v, nchunk):
    return np.ascontiguousarray(np.asarray(v, np.float32).reshape(nchunk, 128).T)


def _prep(inp):
    f = lambda a: np.ascontiguousarray(np.asarray(a, dtype=np.float32))
    xp_all, xs_all = f(inp["x_prompt"]), f(inp["x_sample"])
    w_in, w_out, w_pq = f(inp["w_in"][0]), f(inp["w_out"][0]), f(inp["w_peer_q"][0])
    uT = np.ascontiguousarray(f(inp["peer_u"][0]).reshape(128, 128, 16, 128).transpose(0, 3, 2, 1))
    pv = f(inp["peer_v"][0])
    skT = np.ascontiguousarray(f(inp["peer_sub_keys"][0]).transpose(3, 0, 1, 2).reshape(128, 16, 128))
    gq, gk = f(inp["q_norm_g"][0]), f(inp["k_norm_g"][0])
    gqk = np.ascontiguousarray(np.broadcast_to(np.concatenate([np.tile(gq, 16), np.tile(gk, 4)])[None, :], (128, 1280)))
    sinks = np.ascontiguousarray(np.broadcast_to(f(inp["attn_sinks"][0])[None, :], (128, 16)))
    cw = np.ascontiguousarray(f(inp["conv_w"][0]).T.reshape(8, 128, 4).transpose(1, 0, 2))

    def bd(w):
        w = f(w)
        out = np.zeros((128, 8, 128), np.float32)
        for cc in range(8):
            for nl in range(2):
                out[nl * 64:(nl + 1) * 64, cc, nl * 64:(nl + 1) * 64] = w[2 * cc + nl]
        return out
    s_idx = np.arange(128)[:, None]
    q_idx = np.arange(128)[None, :]
    m_prev = (s_idx > q_idx).astype(np.float32)
    m_cur = (s_idx <= q_idx).astype(np.float32)
    smask = np.zeros((128, 17, 128), np.float32)
    tb, tp = q_idx // 8, q_idx % 8
    for b in range(16):
        smask[:, b, :] = ((tb == b) & (s_idx > tp)).astype(np.float32)
    smask[:, 16, :] = ((tb == (s_idx // 8)) & ((s_idx % 8) <= tp)).astype(np.float32)
    common = dict(
        w_in=w_in, w_out=w_out, w_pq=w_pq, uT=uT, pv=pv, skT=skT,
        gmixT=_fm(inp["norm_mix_g"][0], 16), gffnT=_fm(inp["norm_ffn_g"][0], 16),
        gaoT=_fm(inp["attn_out_g"][0], 8), gloT=_fm(inp["lru_out_g"][0], 8),
        gqk=gqk, sinks=sinks, cw=cw, cb=_fm(inp["conv_b"][0], 8),
        ba=_fm(f(inp["b_rec_gate"][0]).reshape(-1), 8), bx=_fm(f(inp["b_in_gate"][0]).reshape(-1), 8),
        lam=_fm(inp["lru_lambda"][0], 8), bda=bd(inp["w_rec_gate"][0]), bdx=bd(inp["w_in_gate"][0]),
        identf=np.eye(128, dtype=np.float32),
        iota=np.ascontiguousarray(np.broadcast_to(np.arange(128, dtype=np.float32)[None, :], (128, 128))),
        smask=smask,
    )
    ck, cv = f(inp["cache_k"][0]).reshape(128, 128, 256), f(inp["cache_v"][0]).reshape(128, 128, 256)
    sconv, sh = f(inp["state_conv"][0]), f(inp["state_h"][0])
    maps = []
    for c in range(NCORES):
        seq, k = divmod(c, 4)
        xpre = np.zeros((NPRE * 128, D), np.float32)
        if k > 0:
            xpre[NPRE * 128 - 1024 * k:] = xp_all[seq, 0:1024 * k]
        pmask = np.zeros((128, NPRE), np.float32)
        pmask[:, NPRE - 8 * k:] = 1.0
        rope = np.zeros((128, 10, 64), np.float32)
        p = np.arange(128)
        rope[:, 0] = _rope_table(np.maximum(k * 1024 - 128 + p, 0))
        for ti in range(8):
            rope[:, 1 + ti] = _rope_table(k * 1024 + ti * 128 + p)
        rope[:, 9] = _rope_table(8192 + (p % 8))
        amask = np.stack([m_prev if k > 0 else np.zeros_like(m_prev), m_prev, m_cur], axis=1)
        m = dict(common)
        m.update(
            xp=np.ascontiguousarray(xp_all[seq, 1024 * k:1024 * (k + 1)]),
            xs=np.ascontiguousarray(xs_all[16 * c:16 * (c + 1)].reshape(128, D)),
            xpre=xpre, pmask=pmask, rope=rope, amask=np.ascontiguousarray(amask),
            ck=np.ascontiguousarray(ck[16 * c:16 * (c + 1)]), cv=np.ascontiguousarray(cv[16 * c:16 * (c + 1)]),
            sconv=np.ascontiguousarray(sconv[16 * c:16 * (c + 1)].reshape(48, 1024)),
            sh=np.ascontiguousarray(sh[16 * c:16 * (c + 1)]),
        )
        maps.append(m)
    return maps


def _assemble(res):
    y_prompt = np.zeros((2, 4096, D), np.float32)
    y_sample = np.zeros((128, 8, D), np.float32)
    k_prompt = np.zeros((1, 2, 128, 4, 64), np.float32)
    v_prompt = np.zeros((1, 2, 128, 4, 64), np.float32)
    conv_prompt = np.zeros((1, 2, 3, 1024), np.float32)
    h_prompt = np.zeros((1, 2, 1024), np.float32)
    k_sample = np.zeros((1, 128, 128, 4, 64), np.float32)
    v_sample = np.zeros((1, 128, 128, 4, 64), np.float32)
    conv_sample = np.zeros((1, 128, 3, 1024), np.float32)
    h_sample = np.zeros((1, 128, 1024), np.float32)
    for c in range(NCORES):
        r = res[c]
        seq, k = divmod(c, 4)
        y_prompt[seq, 1024 * k:1024 * (k + 1)] = r["yp"]
        y_sample[16 * c:16 * (c + 1)] = r["ys"].reshape(16, 8, D)
        if k == 3:
            k_prompt[0, seq] = r["kp"].reshape(128, 4, 64)
            v_prompt[0, seq] = r["vp"].reshape(128, 4, 64)
            conv_prompt[0, seq] = r["cp"]
            h_prompt[0, seq] = r["hp"][0]
        k_sample[0, 16 * c:16 * (c + 1)] = r["ks"].reshape(16, 128, 4, 64)
        v_sample[0, 16 * c:16 * (c + 1)] = r["vs"].reshape(16, 128, 4, 64)
        conv_sample[0, 16 * c:16 * (c + 1)] = r["cs"].reshape(16, 3, 1024)
        h_sample[0, 16 * c:16 * (c + 1)] = r["hs"]
    return (y_prompt, y_sample, k_prompt, v_prompt, conv_prompt, h_prompt, k_sample, v_sample, conv_sample, h_sample)


_STAGE = "full"


def kernel(**inputs):
    maps = _prep(inputs)
    nc = build(_STAGE)
    res = run_bass_kernel_spmd(nc, maps, core_ids=list(range(NCORES)))
    return _assemble(res.results)
```

```python
import math
import sys
from contextlib import ExitStack
import numpy as np
import concourse.bass as bass
import concourse.mybir as mybir
from concourse.bass_utils import run_bass_kernel_spmd

F32 = mybir.dt.float32
BF16 = mybir.dt.bfloat16
U32 = mybir.dt.uint32
AF = mybir.ActivationFunctionType
ALU = mybir.AluOpType
AX = mybir.AxisListType

NCORES = 8
D = 2048
DC = 16
EPS = 1e-6
NPRE = 24
NOWN = 8
THR_TOL = -4e-6


class Prog:
    ENGS = ("pe", "act", "dve", "pool", "sp")
    EPOCH = 50000
    NDMA_SEM = 8

    def __init__(self, nc):
        self.nc = nc
        self.ops = []
        self.last_w = {}
        self.readers = {}
        self.barrier_deps = set()
        self.last_on_eng = {}
        self.dma_rr = {q: 0 for q in ("sp", "act", "pool")}
        self.dma_last = {}
        self.dma_last_bar = {}
        self.dma_cnt = {}

    def _nm(self, ap):
        sp = str(ap.space)
        if sp == "SB":
            return self.sb.lookup(ap)
        if sp == "PSUM":
            return self.ps.lookup(ap)
        return "dram:" + ap.name

    def begin_stream(self):
        self.pending = []
        self.marks = []

    def mark_split(self):
        if getattr(self, "pending", None) is not None:
            self.marks.append(len(self.pending))

    def commit_prefix_pairs(self, streams, lag=0):
        pairs = [streams[i:i + 2] for i in range(0, len(streams), 2)]

        def front(pair):
            out = []
            for k in range(3):
                for ops, m in pair:
                    out.extend(ops[(0 if k == 0 else m[k - 1]):m[k]])
            return out

        def back(pair):
            out = []
            ba = pair[0][0][pair[0][1][2]:]
            bb = pair[1][0][pair[1][1][2]:] if len(pair) > 1 else []
            ia = ib = 0
            while ia < len(ba) or ib < len(bb):
                if ia < len(ba):
                    out.append(ba[ia])
                    ia += 1
                if ib < len(bb) and (ia >= ib + 1 + lag or ia >= len(ba)):
                    out.append(bb[ib])
                    ib += 1
            return out

        def run(ops):
            for o in ops:
                self._add(*o)
        run(front(pairs[0]))
        if len(pairs) > 1:
            run(front(pairs[1]))
        for p in range(len(pairs)):
            bl = back(pairs[p])
            fl = front(pairs[p + 2]) if p + 2 < len(pairs) else []
            if fl:
                n0 = sum(m[0] for _, m in pairs[p + 2])
                head = fl[:n0]
                dmas = [o for o in head if o[0] == "sp"]
                acts = [o for o in head if o[0] == "act"]
                per_tile = len(acts) // len(pairs[p + 2])
                grouped = [acts[t * per_tile + k] for k in range(per_tile) for t in range(len(pairs[p + 2]))]
                lead = dmas + grouped
                lead_ids = set(id(o) for o in lead)
                run(lead)
                fl = [o for o in fl if id(o) not in lead_ids]
            nb, nf = len(bl), len(fl)
            jf = 0
            for ib_, o in enumerate(bl):
                self._add(*o)
                tgt = (ib_ + 1) * nf // nb if nb else nf
                while jf < tgt:
                    self._add(*fl[jf])
                    jf += 1
            while jf < nf:
                self._add(*fl[jf])
                jf += 1

    def end_stream(self):
        p, self.pending = self.pending, None
        return p, list(self.marks)

    def commit_spread(self, streams):
        def run(ops):
            for o in ops:
                self._add(*o)
        run(streams[0][0][:streams[0][1][0]])
        for j, (ops, m) in enumerate(streams):
            rl = ops[m[0]:]
            fl = streams[j + 1][0][:streams[j + 1][1][0]] if j + 1 < len(streams) else []
            nb, nf = len(rl), len(fl)
            jf = 0
            for ib_, o in enumerate(rl):
                self._add(*o)
                tgt = (ib_ + 1) * nf // nb if nb else nf
                while jf < tgt:
                    self._add(*fl[jf])
                    jf += 1
            while jf < nf:
                self._add(*fl[jf])
                jf += 1

    def _add(self, eng, fn, r, w, kind, extra=None):
        if getattr(self, "pending", None) is not None and extra != "commit":
            fr = sys._getframe(2)
            self.pending.append((eng, fn, r, w, kind, "commit"))
            return -1
        idx = len(self.ops)
        deps = set(self.barrier_deps)
        for k in r:
            if k in self.last_w:
                deps.add(self.last_w[k])
        for k in w:
            if k in self.last_w:
                deps.add(self.last_w[k])
            for rr in self.readers.get(k, ()):
                deps.add(rr)
        fr = sys._getframe(2)
        lines = []
        while fr is not None and len(lines) < 4:
            lines.append(fr.f_lineno)
            fr = fr.f_back
        self._nobar = (kind == "d_bg")
        if kind == "d_bg":
            kind = "d"
        op = dict(eng=eng, fn=fn, deps=deps, kind=kind, line=lines)
        if kind == "d":
            slot = self.dma_rr[eng]
            self.dma_rr[eng] = (slot + 1) % self.NDMA_SEM
            key = (eng, slot)
            if key in self.dma_last:
                deps.add(self.dma_last[key])
            self.dma_last[key] = idx
            if not getattr(self, "_nobar", False):
                self.dma_last_bar[key] = idx
            self.dma_cnt[key] = self.dma_cnt.get(key, 0) + 1
            op["dsem"] = key
            op["dval"] = 16 * self.dma_cnt[key]
        else:
            self.last_on_eng[eng] = idx
        deps.discard(idx)
        for k in r:
            self.readers.setdefault(k, []).append(idx)
        for k in w:
            self.last_w[k] = idx
            self.readers[k] = []
        self.ops.append(op)
        return idx

    def barrier(self):
        s = set(self.last_on_eng.values())
        s.update(self.dma_last_bar.values())
        self.barrier_deps = s

    def op(self, eng, fn, ins, outs, r=None, w=None):
        rr = list(r) if r is not None else [self._nm(a) for a in ins]
        ww = list(w) if w is not None else [self._nm(a) for a in outs]
        return self._add(eng, fn, rr, ww, "c")

    def dma(self, q, out, in_, r=None, w=None, background=False):
        rr = list(r) if r is not None else [self._nm(in_)]
        ww = list(w) if w is not None else [self._nm(out)]
        return self._add(q, lambda e: e.dma_start(out=out, in_=in_), rr, ww, "d_bg" if background else "d")

    def tt(self, eng, out, in0, in1, op, **kw):
        return self.op(eng, lambda e: e.tensor_tensor(out=out, in0=in0, in1=in1, op=op), [in0, in1], [out], **kw)

    def ts(self, eng, out, in0, s1, s2, op0, op1=None, accum=None, **kw):
        ins = [in0] + [s for s in (s1, s2) if not isinstance(s, (int, float)) and s is not None]
        outs = [out] + ([accum] if accum is not None else [])

        def fn(e):
            kws = {}
            if op1 is not None:
                kws["op1"] = op1
            if accum is not None:
                kws["accum_out"] = accum
            return e.tensor_scalar(out=out, in0=in0, scalar1=s1, scalar2=s2, op0=op0, **kws)
        return self.op(eng, fn, ins, outs, **kw)

    def stt(self, out, in0, scalar, in1, op0, op1, **kw):
        ins = [in0, in1] + ([scalar] if not isinstance(scalar, (int, float)) else [])
        return self.op("dve", lambda e: e.scalar_tensor_tensor(out=out, in0=in0, scalar=scalar, in1=in1, op0=op0, op1=op1),
                       ins, [out], **kw)

    def actf(self, out, in_, func, bias=None, scale=None, accum=None, **kw):
        ins = [in_] + [s for s in (bias, scale) if s is not None and not isinstance(s, (int, float))]
        outs = [out] + ([accum] if accum is not None else [])

        def fn(e):
            kws = {}
            if bias is not None:
                kws["bias"] = bias
            if scale is not None:
                kws["scale"] = scale
            if accum is not None:
                kws["accum_out"] = accum
            return e.activation(out=out, in_=in_, func=func, **kws)
        return self.op("act", fn, ins, outs, **kw)

    def copy(self, eng, out, in_, **kw):
        if eng == "act":
            return self.op("act", lambda e: e.copy(out=out, in_=in_), [in_], [out], **kw)
        return self.op(eng, lambda e: e.tensor_copy(out=out, in_=in_), [in_], [out], **kw)

    def mm(self, out, lhsT, rhs, start, stop, **kw):
        return self.op("pe", lambda e: e.matmul(out, lhsT=lhsT, rhs=rhs, start=start, stop=stop), [lhsT, rhs], [out], **kw)

    def tr(self, out, in_, ident, **kw):
        return self.op("pe", lambda e: e.transpose(out=out, in_=in_, identity=ident), [in_, ident], [out], **kw)

    def scan(self, out, d0, d1, init, **kw):
        ins = [d0, d1] + ([init] if not isinstance(init, (int, float)) else [])
        return self.op("dve", lambda e: e.tensor_tensor_scan(out=out, data0=d0, data1=d1, initial=init, op0=ALU.mult, op1=ALU.add),
                       ins, [out], **kw)

    def max8(self, out, in_, **kw):
        return self.op("dve", lambda e: e.max(out=out, in_=in_), [in_], [out], **kw)

    def maxidx(self, out, in_max, in_values, **kw):
        return self.op("dve", lambda e: e.max_index(out=out, in_max=in_max, in_values=in_values), [in_max, in_values], [out], **kw)

    def matchrep(self, out, in_to_replace, in_values, imm, **kw):
        return self.op("dve", lambda e: e.match_replace(out=out, in_to_replace=in_to_replace, in_values=in_values, imm_value=imm),
                       [in_to_replace, in_values], [out], **kw)

    def recip(self, out, in_, **kw):
        return self.op("dve", lambda e: e.reciprocal(out=out, in_=in_), [in_], [out], **kw)

    def memset(self, eng, ap, val, **kw):
        return self.op(eng, lambda e: e.memset(ap, val), [], [ap], **kw)

    def reduce(self, out, in_, op, **kw):
        return self.op("dve", lambda e: e.tensor_reduce(out=out, in_=in_, axis=AX.X, op=op), [in_], [out], **kw)

    def simulate(self):
        sems = {}
        pos = {e: 0 for e in self.ENGS}
        progress = True
        while progress:
            progress = False
            for e in self.ENGS:
                tr = self.trace[e]
                while pos[e] < len(tr):
                    ev = tr[pos[e]]
                    if ev[0] == "w":
                        if sems.get(ev[1], 0) >= ev[2]:
                            pos[e] += 1
                            progress = True
                        else:
                            break
                    else:
                        sems[ev[1]] = sems.get(ev[1], 0) + ev[2]
                        pos[e] += 1
                        progress = True
        stuck = {e: (pos[e], len(self.trace[e])) for e in self.ENGS if pos[e] < len(self.trace[e])}
        if stuck:
            for e in stuck:
                ev = self.trace[e][pos[e]]
                print(f"[simulate] DEADLOCK: {e} blocked at event {pos[e]}/{len(self.trace[e])}: {ev} cur={sems.get(ev[1], 0)}")
            raise RuntimeError("deadlock in emitted program")
        print("[simulate] no deadlock")

    def emit(self, es):
        nc = self.nc
        ops = self.ops
        needed = set()
        for o in ops:
            for d in o["deps"]:
                needed.add(d)
        cnt = {e: 0 for e in self.ENGS}
        for i, o in enumerate(ops):
            if o["kind"] == "c" and i in needed:
                cnt[o["eng"]] += 1
                o["ticket"] = cnt[o["eng"]]
        csem = {}
        for e in self.ENGS:
            n_ep = (cnt[e] + self.EPOCH - 1) // self.EPOCH
            csem[e] = [es.enter_context(nc.semaphore(f"c_{e}_{k}")) for k in range(max(1, n_ep))]
        dsem = {}
        for key in self.dma_cnt:
            dsem[key] = es.enter_context(nc.semaphore(f"d_{key[0]}_{key[1]}"))
        block = es.enter_context(nc.Block())
        EP = self.EPOCH
        self.trace = {e: [] for e in self.ENGS}

        def emit_engine(eng, e):
            waited = {}

            def wait(semkey, sem, val):
                if waited.get(semkey, 0) >= val:
                    return
                waited[semkey] = val
                e.wait_ge(sem, val)
                self.trace[eng].append(("w", semkey, val))

            for i, o in enumerate(ops):
                if o["eng"] != eng:
                    continue
                for d in sorted(o["deps"]):
                    od = ops[d]
                    if od["kind"] == "d":
                        wait(("d",) + od["dsem"], dsem[od["dsem"]], od["dval"])
                    else:
                        if od["eng"] == "pe" and eng == "pe" and o["kind"] == "c":
                            continue
                        t = od["ticket"]
                        ep = (t - 1) // EP
                        wait(("c", od["eng"], ep), csem[od["eng"]][ep], (t - 1) % EP + 1)
                try:
                    ins = o["fn"](e)
                except Exception:
                    print(f"[emit] failure in op recorded at lines {o['line']} on {eng}")
                    raise
                if o["kind"] == "d":
                    ins.then_inc(dsem[o["dsem"]], 16)
                    self.trace[eng].append(("i", ("d",) + o["dsem"], 16, i))
                elif "ticket" in o:
                    ep = (o["ticket"] - 1) // EP
                    ins.then_inc(csem[eng][ep], 1)
                    self.trace[eng].append(("i", ("c", eng, ep), 1, i))
            if eng == "sp":
                for key, c in self.dma_cnt.items():
                    e.wait_ge(dsem[key], 16 * c)

        @block.sync
        def _(e):
            emit_engine("sp", e)

        @block.scalar
        def _(e):
            emit_engine("act", e)

        @block.vector
        def _(e):
            emit_engine("dve", e)

        @block.gpsimd
        def _(e):
            emit_engine("pool", e)

        @block.tensor
        def _(e):
            emit_engine("pe", e)


class Arena:
    def __init__(self, prog, tensor, nbytes, tag):
        self.P = prog
        self.t = tensor
        self.n = nbytes
        self.tag = tag
        self.top = 0
        self.regions = []
        self.peak = 0
        self.gen = 0

    def alloc(self, name, free_shape, dtype, align=64):
        esz = {F32: 4, BF16: 2, U32: 4}[dtype]
        n = esz
        for s in free_shape:
            n *= s
        start = (self.top + align - 1) // align * align
        end = start + n
        if end > self.n:
            raise RuntimeError(f"{self.tag} arena overflow allocating {name}: {end} > {self.n}")
        self.top = end
        self.peak = max(self.peak, end)
        self.gen += 1
        self.regions.append((start, end, f"{self.tag}:{name}:{self.gen}"))
        ap = self.t[:, start // 4:(end + 3) // 4]
        if dtype != F32:
            ap = ap.bitcast(dtype)[:, 0:n // esz]
        if len(free_shape) > 1:
            names = " ".join(f"d{i}" for i in range(len(free_shape)))
            kw = {f"d{i}": free_shape[i] for i in range(1, len(free_shape))}
            ap = ap.rearrange(f"p ({names}) -> p {names}", **kw)
        return ap

    def mark(self):
        return (self.top, len(self.regions))

    def release(self, m):
        self.top, nreg = m
        del self.regions[nreg:]
        self.P.barrier()

    def lookup(self, ap):
        esz = {F32: 4, BF16: 2, U32: 4}[ap.dtype]
        pitch = ap.ap[0][0]
        off = (ap.offset % pitch) * esz
        for (s, e, nme) in reversed(self.regions):
            if s <= off < e:
                return nme
        raise RuntimeError(f"{self.tag} lookup failed for offset {off}")


def bc(ap, shape):
    return ap.broadcast_to(list(shape))


class WNames:
    def __init__(self, names, mode, blk=512):
        self.names, self.mode, self.blk = names, mode, blk

    def at(self, dc, col):
        return self.names[dc] if self.mode == "dc" else self.names[col // self.blk]


def build(stage="full"):
    nc = bass.Bass("TRN2", target_bir_lowering=False)

    def din(name, shape):
        return nc.dram_tensor(name, list(shape), F32, kind="ExternalInput").ap()

    def dout(name, shape):
        return nc.dram_tensor(name, list(shape), F32, kind="ExternalOutput").ap()

    I = {}
    for name, shape in [
        ("xp", (1024, D)), ("xs", (128, D)), ("xpre", (NPRE * 128, D)), ("pmask", (128, NPRE)),
        ("rope", (128, 10, 64)), ("amask", (128, 3, 128)), ("smask", (128, 17, 128)),
        ("ck", (16, 128, 256)), ("cv", (16, 128, 256)), ("sconv", (48, 1024)), ("sh", (16, 1024)),
        ("w_in", (D, 3584)), ("w_out", (D, D)), ("w_pq", (D, D)), ("uT", (128, 128, 16, 128)),
        ("pv", (16384, D)), ("skT", (128, 16, 128)),
        ("gmixT", (128, 16)), ("gffnT", (128, 16)), ("gaoT", (128, 8)), ("gloT", (128, 8)),
        ("gqk", (128, 1280)), ("sinks", (128, 16)), ("cw", (128, 8, 4)), ("cb", (128, 8)),
        ("ba", (128, 8)), ("bx", (128, 8)), ("lam", (128, 8)), ("bda", (128, 8, 128)), ("bdx", (128, 8, 128)),
        ("identf", (128, 128)), ("iota", (128, 128)),
    ]:
        I[name] = din(name, shape)
    O = {}
    for name, shape in [
        ("yp", (1024, D)), ("ys", (128, D)), ("kp", (128, 256)), ("vp", (128, 256)), ("cp", (3, 1024)),
        ("hp", (1, 1024)), ("ks", (16, 128, 256)), ("vs", (16, 128, 256)), ("cs", (48, 1024)), ("hs", (16, 1024)),
    ]:
        O[name] = dout(name, shape)

    ub16 = nc.dram_tensor("ub16", [128, 128, 2048], BF16, kind="Internal").ap()
    vb16 = nc.dram_tensor("vb16", [16384, 2048], BF16, kind="Internal").ap()
    P = Prog(nc)
    es = ExitStack()
    win16 = nc.dram_tensor("win16", [D, 3584], BF16, kind="Internal").ap()
    wout16 = nc.dram_tensor("wout16", [D, D], BF16, kind="Internal").ap()
    wpq16 = nc.dram_tensor("wpq16", [D, D], BF16, kind="Internal").ap()
    W16 = {"w_in": win16, "w_out": wout16, "w_pq": wpq16}
    conv_jobs = []
    for wn_ in ("w_in", "w_out", "w_pq"):
        for dc in range(DC):
            conv_jobs.append((W16[wn_][dc * 128:(dc + 1) * 128, :], I[wn_][dc * 128:(dc + 1) * 128, :], f"{wn_}16#{dc}"))
    for ic in range(128):
        conv_jobs.append((ub16[ic], I["uT"][ic].rearrange("p a b -> p (a b)"), f"ub16#{ic}"))
        conv_jobs.append((vb16[ic * 128:(ic + 1) * 128, :], I["pv"][ic * 128:(ic + 1) * 128, :], f"vb16#{ic}"))
    conv_jobs.reverse()

    def emit_conv(n):
        for _ in range(n):
            if conv_jobs:
                o_, i_, nm_ = conv_jobs.pop()
                P.dma("pool", o_, i_, r=[], w=[nm_], background=True)
    SB_BYTES = 212480
    sb_t = es.enter_context(nc.sbuf_tensor("sb_arena", [128, SB_BYTES // 4], F32))
    ps_t = es.enter_context(nc.psum_tensor("ps_arena", [128, 4096], F32))
    SB = Arena(P, sb_t, SB_BYTES, "sb")
    PS = Arena(P, ps_t, 16384, "ps")
    P.sb, P.ps = SB, PS

    identf = SB.alloc("identf", [128], F32)
    identb = SB.alloc("identb", [128], BF16)
    iota = SB.alloc("iota", [128], F32)
    gmixT = SB.alloc("gmixT", [16], F32)
    gffnT = SB.alloc("gffnT", [16], F32)
    gaoT = SB.alloc("gaoT", [8], F32)
    gloT = SB.alloc("gloT", [8], F32)
    esink = SB.alloc("esink", [16], F32)
    cw = SB.alloc("cw", [8, 4], F32)
    cb = SB.alloc("cb", [8], F32)
    ba = SB.alloc("ba", [8], F32)
    bx = SB.alloc("bx", [8], F32)
    nsp = SB.alloc("nsp", [8], F32)
    bda = SB.alloc("bda", [8, 128], BF16)
    bdx = SB.alloc("bdx", [8, 128], BF16)
    amask = SB.alloc("amask", [3, 128], BF16)
    rope = SB.alloc("rope", [10, 64], F32)
    pmask = SB.alloc("pmask", [NPRE], F32)
    onesb = SB.alloc("onesb", [1], BF16)
    epsc = SB.alloc("epsc", [1], F32)
    hist = SB.alloc("hist", [8, 3], F32)
    hstate = SB.alloc("hstate", [8], F32)
    KT = [SB.alloc(f"KT{i}", [2, 128], BF16) for i in range(2)]
    vaug = [SB.alloc(f"vaug{i}", [4, 65], BF16) for i in range(2)]

    for nm, t in [("identf", identf), ("iota", iota), ("gmixT", gmixT), ("gffnT", gffnT), ("gaoT", gaoT), ("gloT", gloT),
                  ("sinks", esink), ("cw", cw), ("cb", cb), ("ba", ba), ("bx", bx), ("lam", nsp),
                  ("rope", rope), ("pmask", pmask)]:
        P.dma("sp", t, I[nm])
    for nm, t in [("bda", bda), ("bdx", bdx), ("amask", amask), ("identf", identb)]:
        P.dma("pool", t, I[nm])
    P.memset("dve", onesb, 1.0)
    P.memset("dve", epsc, EPS)
    P.memset("dve", hist, 0.0)
    P.memset("dve", hstate, 0.0)
    for i in range(2):
        P.memset("dve", vaug[i], 1.0)
        P.memset("dve", KT[i], 0.0)
    P.actf(esink, esink, AF.Exp)
    P.actf(nsp, nsp, AF.Exp, scale=-1.0)
    tmp8 = SB.alloc("tmp8", [8], F32)
    P.ts("dve", tmp8, nsp, 1.0, None, ALU.add)
    P.actf(nsp, tmp8, AF.Ln)
    P.ts("dve", nsp, nsp, -8.0, None, ALU.mult)

    state = dict(hslot=1, kvslot=1)

    def load_w(tile_ap, dram, r0, c0, c1, nchunks, tag):
        names = []
        for dc in range(nchunks):
            nm = f"{tag}#{dc}"
            P.dma("pool", tile_ap[:, dc, :], dram[r0 + dc * 128:r0 + (dc + 1) * 128, c0:c1], w=[nm])
            names.append(nm)
        return WNames(names, "dc")

    def load_w16(tile_ap, wname, c0, c1, tag):
        names = []
        src = W16[wname].rearrange("(dc p) n -> p dc n", p=128)
        for k, cb0 in enumerate(range(c0, c1, 512)):
            nm = f"{tag}#c{k}"
            P.dma("sp" if k % 2 == 0 else "act", tile_ap[:, :, cb0 - c0:cb0 - c0 + 512], src[:, :, cb0:cb0 + 512],
                  r=[f"{wname}16#{dc}" for dc in range(DC)], w=[nm])
            names.append(nm)
        return WNames(names, "col")

    def norm_T(xsrc, gT, hT_dst, scr, psT, sfx):
        P.actf(scr["junk"], xsrc, AF.Square, accum=scr["ssq"])
        P.actf(scr["rms"], scr["ssq"], AF.Sqrt, bias=epsc, scale=1.0 / D)
        P.recip(scr["rstd"], scr["rms"])
        P.ts("dve", scr["xn"], xsrc, scr["rstd"], None, ALU.mult)
        psTb = psT.bitcast(BF16).rearrange("p (c t) -> p c t", t=128)
        for dc in range(DC):
            P.tr(psTb[:, dc, :], scr["xn"][:, dc * 128:(dc + 1) * 128], identb)
        P.tt("dve", hT_dst, psTb[:, 0:DC, :], bc(gT.unsqueeze(2), [128, DC, 128]), ALU.mult)

    def alloc_norm_scr():
        xn = SB.alloc("xn", [D], BF16)
        return dict(junk=xn, ssq=SB.alloc("ssq", [1], F32), rms=SB.alloc("rms", [1], F32),
                    rstd=SB.alloc("rstd", [1], F32), xn=xn)

    def alloc_lru_scr():
        return dict(xpb=SB.alloc("xpb", [8, 176], F32), xc=SB.alloc("xc", [8, 128], F32), xcb=SB.alloc("xcb", [8, 128], BF16),
                    tmp=SB.alloc("ltmp", [8, 128], F32), r=SB.alloc("r", [8, 128], F32), ig=SB.alloc("ig", [8, 128], F32),
                    a=SB.alloc("a", [8, 128], F32), u=SB.alloc("u", [8, 128], F32), gy=SB.alloc("gy", [8, 128], F32),
                    lob=SB.alloc("lob", [8, 128], BF16), cst=SB.alloc("cst", [8, 48], F32), h0T=SB.alloc("h0T", [8, 16], F32),
                    ost=SB.alloc("ost", [1024], F32), h=SB.alloc("hcur", [8, 128], F32))

    def lru_tile(kind, hnT, win_lru, wnames, S, psX, psS, mixT_dst, ssq_dst, pm_col=None, outs=None, sample_in=None,
                 hprev=None, psG=None, psOut=None):
        npart = 16 if kind != "pre" else 8
        psXv = psX.rearrange("p (c t) -> p c t", t=128)
        for cc in range(npart):
            for dc in range(DC):
                P.mm(psXv[:, cc, :], win_lru[:, dc, cc * 128:(cc + 1) * 128], hnT[:, dc, :], dc == 0, dc == DC - 1,
                     r=[wnames.at(dc, cc * 128), P._nm(hnT)])
        P.mark_split()
        psGv = psXv if psG is None else psG.rearrange("p (c t) -> p c t", t=128)
        hcur = S["h"]
        if kind != "smp":
            xpv = S["xpb"][:, :, 0:131]
            P.copy("act", xpv[:, :, 3:131], psXv[:, 0:8, :])
            P.copy("dve", xpv[:, :, 0:3], hist)
            P.copy("dve", hist, xpv[:, :, 128:131])

            def sl(j):
                return xpv[:, :, j:j + 128]

            def cwb(j):
                return bc(cw[:, :, j:j + 1], [128, 8, 128])
            cbb = bc(cb.unsqueeze(2), [128, 8, 128])
            xcv, tmpv = S["xc"], S["tmp"]
        else:
            xp4 = S["xpb"].rearrange("p c (b l) -> p c b l", l=11)
            P.copy("act", xp4[:, :, :, 3:11], psXv[:, 0:8, :].rearrange("p c (b l) -> p c b l", l=8))
            sct, sht = sample_in
            psC = psS.rearrange("p (c n) -> p c n", n=64)
            for cc in range(8):
                P.tr(psC[:, cc, 0:48], sct[:, cc * 128:(cc + 1) * 128], identf[0:48, 0:48])
            P.copy("dve", xp4[:, :, :, 0:3], psC[:, :, 0:48].rearrange("p c (b j) -> p c b j", j=3))
            for cc in range(8):
                P.tr(psC[:, cc, 48:64], sht[:, cc * 128:(cc + 1) * 128], identf[0:16, 0:16])
            P.copy("dve", S["h0T"], psC[:, :, 48:64])

            def sl(j):
                return xp4[:, :, :, j:j + 8]

            def cwb(j):
                return bc(cw[:, :, j:j + 1].unsqueeze(3), [128, 8, 16, 8])
            cbb = bc(cb.unsqueeze(2).unsqueeze(3), [128, 8, 16, 8])
            xcv = S["xc"].rearrange("p c (b l) -> p c b l", l=8)
            tmpv = S["tmp"].rearrange("p c (b l) -> p c b l", l=8)
        if kind != "pre":
            P.actf(S["gy"], psXv[:, 8:16, :], AF.Gelu_apprx_tanh)
        P.mark_split()
        P.tt("dve", xcv, sl(0), cwb(0), ALU.mult)
        for j in range(1, 4):
            P.tt("dve", tmpv, sl(j), cwb(j), ALU.mult)
            P.tt("dve", xcv, xcv, tmpv, ALU.add)
        P.tt("dve", xcv, xcv, cbb, ALU.add)
        P.copy("act", S["xcb"], S["xc"])
        go = 0 if (kind == "pre" or psG is not None) else 8
        for cc in range(8):
            P.mm(psGv[:, cc, :], bda[:, cc, :], S["xcb"][:, cc, :], True, True)
        if go:
            for cc in range(8):
                P.mm(psGv[:, go + cc, :], bdx[:, cc, :], S["xcb"][:, cc, :], True, True)
        for cc in range(8):
            P.actf(S["r"][:, cc, :], psGv[:, cc, :], AF.Sigmoid, bias=ba[:, cc:cc + 1])
        if not go:
            for cc in range(8):
                P.mm(psGv[:, cc, :], bdx[:, cc, :], S["xcb"][:, cc, :], True, True)
        for cc in range(8):
            P.actf(S["ig"][:, cc, :], psGv[:, go + cc, :], AF.Sigmoid, bias=bx[:, cc:cc + 1])
        for cc in range(8):
            P.actf(S["a"][:, cc, :], S["r"][:, cc, :], AF.Exp, scale=nsp[:, cc:cc + 1])
        P.tt("dve", S["tmp"], S["a"], S["a"], ALU.mult)
        P.ts("dve", S["tmp"], S["tmp"], -1.0, 1.0, ALU.mult, ALU.add)
        P.actf(S["tmp"], S["tmp"], AF.Sqrt)
        P.tt("dve", S["u"], S["ig"], S["xc"], ALU.mult)
        if kind == "pre":
            P.stt(S["u"], S["u"], pm_col, S["tmp"], ALU.mult, ALU.mult)
        else:
            P.tt("dve", S["u"], S["u"], S["tmp"], ALU.mult)
        if kind == "smp":
            a4 = S["a"].rearrange("p c (b l) -> p c b l", l=8)
            u4 = S["u"].rearrange("p c (b l) -> p c b l", l=8)
            t3 = S["cst"][:, :, 0:16]
            P.tt("dve", t3, a4[:, :, :, 0], S["h0T"], ALU.mult)
            P.tt("dve", u4[:, :, :, 0], u4[:, :, :, 0], t3, ALU.add)
            P.memset("dve", a4[:, :, :, 0], 0.0)
        for cc in range(8):
            if kind == "smp":
                init = 0.0
            elif hprev is not None:
                init = hprev[:, cc, 127:128]
            else:
                init = hstate[:, cc:cc + 1]
            P.scan(hcur[:, cc, :], S["a"][:, cc, :], S["u"][:, cc, :], init)
        if kind != "smp":
            P.copy("dve", hstate, hcur[:, :, 127])
        if kind == "pre":
            return
        P.tt("dve", S["gy"], hcur, S["gy"], ALU.mult)
        P.tt("dve", S["lob"], S["gy"], S["gy"], ALU.mult)
        for cc in range(8):
            P.mm(psS[:, 0:1], S["lob"][:, cc, :], onesb, cc == 0, cc == 7)
        P.copy("dve", ssq_dst, psS[:, 0:1])
        P.tt("dve", mixT_dst, S["gy"], bc(gloT.unsqueeze(2), [128, 8, 128]), ALU.mult)
        pso = psX if psOut is None else psOut
        if outs is not None:
            psC = psS.rearrange("p (c n) -> p c n", n=64)
            if kind == "own":
                P.copy("dve", S["cst"][:, :, 0:3], xpv[:, :, 128:131])
                P.copy("dve", S["cst"][:, :, 3:4], hcur[:, :, 127:128])
                n1, n2 = 3, 1
            else:
                P.copy("dve", S["cst"].rearrange("p c (b j) -> p c b j", j=3), xp4[:, :, :, 8:11])
                P.copy("dve", S["h0T"], hcur.rearrange("p c (b l) -> p c b l", l=8)[:, :, :, 7])
                n1, n2 = 48, 16
            for which, n, dst in ((0, n1, outs[0]), (1, n2, outs[1])):
                for cc in range(8):
                    if kind == "own":
                        src = S["cst"][:, cc, 0:3] if which == 0 else S["cst"][:, cc, 3:4]
                    else:
                        src = S["cst"][:, cc, :] if which == 0 else S["h0T"][:, cc, :]
                    P.tr(pso[0:n, cc * 128:(cc + 1) * 128], src, identf)
                ostv = S["ost"]
                P.copy("act", ostv[0:n, :], pso[0:n, 0:1024])
                P.dma("sp", dst, ostv[0:n, :])

    def alloc_attn_scr():
        sq = SB.alloc("sq", [20, 64], F32)
        aob = SB.alloc("aob", [1024], BF16)
        gqk_t = SB.alloc("gqk", [20, 64], F32)
        P.dma("sp", gqk_t, I["gqk"])
        return dict(qkv=SB.alloc("qkv", [1536], F32), sq=sq, st=SB.alloc("qst", [20], F32),
                    qkn=sq, gqk=gqk_t, qkr=SB.alloc("qkr", [20, 64], F32),
                    t1=SB.alloc("t1", [20, 32], F32), t2=SB.alloc("t2", [20, 32], F32),
                    qb=SB.alloc("qb", [16, 64], BF16), kb=SB.alloc("kb", [4, 64], BF16), QT=SB.alloc("QT", [8, 128], BF16),
                    E=[SB.alloc(f"E{i}", [2, 512], BF16) for i in range(2)], den=SB.alloc("den", [4], F32),
                    ao=SB.alloc("ao", [16, 64], F32), aob=aob, junk=aob,
                    sa=SB.alloc("sa", [1], F32))

    def qkv_tile(hnT, win_qkv, wnames, A, psQ, psT, rope_idx, slot):
        for nb in range(3):
            for dc in range(DC):
                P.mm(psQ[:, nb * 512:(nb + 1) * 512], hnT[:, dc, :], win_qkv[:, dc, nb * 512:(nb + 1) * 512], dc == 0, dc == DC - 1,
                     r=[wnames.at(dc, nb * 512), P._nm(hnT)])
        P.copy("act", A["qkv"], psQ[:, 0:1536], r=[nB0, nB1])
        qk = A["qkv"][:, 0:1280].rearrange("p (h d) -> p h d", d=64)
        P.tt("dve", A["sq"], qk, qk, ALU.mult)
        P.reduce(A["st"], A["sq"], ALU.add)
        P.actf(A["st"], A["st"], AF.Sqrt, bias=epsc, scale=1.0 / 64)
        P.recip(A["st"], A["st"])
        P.tt("dve", A["qkn"], qk, bc(A["st"].unsqueeze(2), [128, 20, 64]), ALU.mult)
        P.tt("dve", A["qkn"], A["qkn"], A["gqk"], ALU.mult)
        cosb = bc(rope[:, rope_idx, 0:32].unsqueeze(1), [128, 20, 32])
        sinb = bc(rope[:, rope_idx, 32:64].unsqueeze(1), [128, 20, 32])
        x1, x2 = A["qkn"][:, :, 0:32], A["qkn"][:, :, 32:64]
        P.tt("dve", A["t1"], x1, cosb, ALU.mult)
        P.tt("dve", A["t2"], x2, sinb, ALU.mult)
        P.tt("dve", A["qkr"][:, :, 0:32], A["t1"], A["t2"], ALU.subtract)
        P.tt("dve", A["t1"], x2, cosb, ALU.mult)
        P.tt("dve", A["t2"], x1, sinb, ALU.mult)
        P.tt("dve", A["qkr"][:, :, 32:64], A["t1"], A["t2"], ALU.add)
        for gp in range(2):
            src = A["qkr"][:, 8 * gp:8 * gp + 8, :].rearrange("p (half r) d -> p half r d", r=4)
            dst = A["qb"][:, 8 * gp:8 * gp + 8, :].rearrange("p (r half) d -> p half r d", half=2)
            P.copy("act", dst, src)
        P.copy("act", A["kb"], A["qkr"][:, 16:20, :])
        P.copy("act", vaug[slot][:, :, 0:64], A["qkv"][:, 1280:1536].rearrange("p (h d) -> p h d", d=64))
        psTb = psT.bitcast(BF16).rearrange("p (c t) -> p c t", t=128)
        for j in range(8):
            P.tr(psTb[:, j, :], A["qb"][:, 2 * j:2 * j + 2, :].rearrange("p h d -> p (h d)"), identb)
        for j in range(2):
            P.tr(psTb[:, 8 + j, :], A["kb"][:, 2 * j:2 * j + 2, :].rearrange("p h d -> p (h d)"), identb)
        P.copy("dve", A["QT"], psTb[:, 0:8, :])
        P.copy("dve", KT[slot], psTb[:, 8:10, :])

    def attn_core(A, kblocks, psL, psO, psT, mixT_dst, rstd_dst, po_banks=None):
        nkb = len(kblocks)
        it = 0
        for g in range(4):
            gp, half = divmod(g, 2)
            rows = slice(64 * half, 64 * half + 64)
            po = psO[g % 2].rearrange("p (r d) -> p r d", d=128)
            for k0 in range(0, nkb, 2):
                ks = kblocks[k0:k0 + 2]
                pl = psL[it % 2].rearrange("p (k n) -> p k n", n=512)
                E = A["E"][it % 2]
                it += 1
                for ki, (kt, va, mk) in enumerate(ks):
                    P.mm(pl[:, ki, :].rearrange("p (r q) -> p r q", q=128), kt[rows, gp, :], A["QT"][rows, 4 * gp:4 * gp + 4, :], True, True)
                n = len(ks)
                P.actf(E[:, 0:n, :], pl[:, 0:n, :], AF.Exp, scale=0.125)
                for ki, (kt, va, mk) in enumerate(ks):
                    Ev = E[:, ki, :].rearrange("p (r q) -> p r q", q=128)
                    P.tt("dve", Ev, Ev, bc(mk.unsqueeze(1), [128, 4, 128]), ALU.mult)
                for r in range(4):
                    dst = po[:, r, 0:65] if po_banks is None else po_banks[r][:, 0:65]
                    for ki, (kt, va, mk) in enumerate(ks):
                        P.mm(dst, E[:, ki, r * 128:(r + 1) * 128], va[:, g, :], (k0 + ki) == 0, (k0 + ki) == nkb - 1)
            if po_banks is None:
                P.tt("dve", A["den"], po[:, :, 64], esink[:, 4 * g:4 * g + 4], ALU.add)
                P.recip(A["den"], A["den"])
                P.tt("dve", A["ao"][:, 4 * g:4 * g + 4, :], po[:, :, 0:64], bc(A["den"].unsqueeze(2), [128, 4, 64]), ALU.mult)
            else:
                for r in range(4):
                    P.tt("dve", A["den"][:, r:r + 1], po_banks[r][:, 64:65], esink[:, 4 * g + r:4 * g + r + 1], ALU.add)
                P.recip(A["den"], A["den"])
                for r in range(4):
                    P.ts("dve", A["ao"][:, 4 * g + r, :], po_banks[r][:, 0:64], A["den"][:, r:r + 1], None, ALU.mult)
        aof = A["ao"].rearrange("p h d -> p (h d)")
        P.actf(A["junk"], aof, AF.Square, accum=A["sa"])
        P.actf(A["sa"], A["sa"], AF.Sqrt, bias=epsc, scale=1.0 / 1024)
        P.recip(rstd_dst, A["sa"])
        P.copy("act", A["aob"], aof)
        psTb = psT.bitcast(BF16).rearrange("p (c t) -> p c t", t=128)
        for cc in range(8):
            P.tr(psTb[:, cc, :], A["aob"][:, cc * 128:(cc + 1) * 128], identb)
        P.tt("dve", mixT_dst, psTb[:, 0:8, :], bc(gaoT.unsqueeze(2), [128, 8, 128]), ALU.mult)

    psT = PS.alloc("psT", [1024], F32)
    psB0 = PS.alloc("psB0", [1024], F32)
    psB1 = PS.alloc("psB1", [1024], F32)
    psO = [PS.alloc("psO0", [512], F32), PS.alloc("psO1", [512], F32)]
    psBig = ps_t[:, 1024:3072]
    nB0, nB1 = P._nm(psB0), P._nm(psB1)
    psL = [psB0, psB1]

    m0 = SB.mark()
    hnTs = [SB.alloc(f"hnT{i}", [16, 128], BF16) for i in range(2)]
    m1 = SB.mark()
    winx = SB.alloc("winx", [16, 1024], BF16)
    wxn = load_w(winx, I["w_in"], 0, 1536, 2560, 16, "winx")
    xt = [SB.alloc(f"xt{i}", [D], F32) for i in range(2)]
    NSs = [alloc_norm_scr() for _ in range(2)]
    LSs = [alloc_lru_scr() for _ in range(2)]
    npre = NPRE if stage != "nopre" else 1
    streams = []
    xpbs = [[LSs[0]["xpb"], LSs[1]["xpb"]], [SB.alloc("xpb_c", [8, 176], F32), SB.alloc("xpb_d", [8, 176], F32)]]
    psGs = [psT, ps_t[:, 3072:4096]]
    for ti in range(NPRE - npre, NPRE):
        par = ti % 2
        pr = (ti // 2) % 2
        P.begin_stream()
        P.dma("sp", xt[par], I["xpre"][ti * 128:(ti + 1) * 128, :])
        norm_T(xt[par], gmixT, hnTs[par], NSs[par], psL[par], "p")
        P.mark_split()
        lru_tile("pre", hnTs[par], winx, wxn, dict(LSs[par], xpb=xpbs[pr][par]), psL[par], psO[par], None, None,
                 pm_col=pmask[:, ti:ti + 1], hprev=(LSs[1 - par]["h"] if ti > NPRE - npre else None), psG=psGs[par])
        emit_conv(13)
        streams.append(P.end_stream())
    if len(streams) > 1:
        P.commit_prefix_pairs(streams)
    else:
        for o in streams[0][0]:
            P._add(*o)
    hnT = hnTs[(NPRE - 1) % 2]
    SB.release(m1)
    wqkv = SB.alloc("wqkv", [16, 1536], BF16)
    wqn = load_w16(wqkv, "w_in", 0, 1536, "wqkv")
    A = alloc_attn_scr()
    qkv_tile(hnT, wqkv, wqn, A, psBig, psT, 0, 0)
    state["kvslot"] = 0
    SB.release(m0)

    def tile_rows(ti):
        if ti < NOWN:
            return I["xp"][ti * 128:(ti + 1) * 128, :], O["yp"][ti * 128:(ti + 1) * 128, :]
        return I["xs"], O["ys"]

    def peer_group(g, tiles, hcT):
        qT = SB.alloc("qT", [16, 384], BF16)
        skT = SB.alloc("skT", [16, 128], BF16)
        P.dma("pool", skT, I["skT"])
        mq = SB.mark()
        wpq = SB.alloc("wpq", [16, 2048], BF16)
        wpn = load_w16(wpq, "w_pq", 0, 2048, f"wpq{g}")
        for m in range(16):
            pb = psL[m % 2]
            for dc in range(DC):
                P.mm(pb[:, 0:384], wpq[:, dc, m * 128:(m + 1) * 128], hcT[:, dc, :], dc == 0, dc == DC - 1,
                     r=[wpn.at(dc, m * 128), P._nm(hcT)])
            P.copy("act" if m % 2 == 0 else "dve", qT[:, m, :], pb[:, 0:384])
        SB.release(mq)
        Wsb = SB.alloc("Wsb", [128, 384], BF16)
        mq = SB.mark()
        s_sb = SB.alloc("s_sb", [16, 128], F32)
        s2w = SB.alloc("s2w", [8, 128], F32)
        sv = SB.alloc("sv", [16, 16], F32)
        si = SB.alloc("si", [8, 16], U32)
        cand = [SB.alloc(f"cand{i}", [16, 16], F32) for i in range(2)]
        cand2 = [SB.alloc(f"cand2{i}", [256], F32) for i in range(2)]
        best = SB.alloc("best", [8, 16], F32)
        eb = SB.alloc("eb", [8, 16], F32)
        zz = SB.alloc("zz", [8], F32)
        nthr = SB.alloc("nthr", [8, 16], F32)
        e1 = SB.alloc("e1", [8, 16], BF16)
        e2 = SB.alloc("e2", [8, 128], BF16)
        e2a = s2w
        si1f = SB.alloc("si1f", [8, 16], F32)
        si1T = SB.alloc("si1T", [128], F32)
        JB = 8
        Mk = [SB.alloc(f"Mk{i}", [8, 16, JB], BF16) for i in range(2)]
        Gb = [SB.alloc(f"Gb{i}", [8, 16, JB], BF16) for i in range(2)]
        Rw = [SB.alloc(f"Rw{i}", [128, JB], BF16) for i in range(2)]
        RwT = SB.alloc("RwT", [128, 128], BF16)
        O1 = [SB.alloc(f"O1{i}", [8, 128], BF16) for i in range(2)]
        psTb = psT.bitcast(BF16).rearrange("p (c t) -> p c t", t=128)
        for tt in range(3):
            pss = psBig.rearrange("p (m k) -> p m k", k=128)
            for m in range(16):
                P.mm(pss[:, m, :], qT[:, m, tt * 128:(tt + 1) * 128], skT[:, m, :], True, True)
            P.copy("act", s_sb[:, 0:8, :], pss[:, 0:8, :])
            P.copy("dve", s_sb[:, 8:16, :], pss[:, 8:16, :])
            for m0 in (0, 8):
                ms = range(m0, m0 + 8)
                for m in ms:
                    P.max8(sv[:, m, 0:8], s_sb[:, m, :], w=[f"sv#{m}a"], r=[P._nm(s_sb)])
                for m in ms:
                    if m % 2 == 0:
                        P.maxidx(si[:, m // 2, 0:8], sv[:, m, 0:8], s_sb[:, m, :], r=[f"sv#{m}a", P._nm(s_sb)], w=[f"si#{m}a"])
                for m in ms:
                    P.matchrep(s2w[:, m - m0, :], sv[:, m, 0:8], s_sb[:, m, :], -1e30, r=[f"sv#{m}a", P._nm(s_sb)],
                               w=[f"s2w#{m - m0}"])
                for m in ms:
                    P.max8(sv[:, m, 8:16], s2w[:, m - m0, :], r=[f"s2w#{m - m0}"], w=[f"sv#{m}b"])
                for m in ms:
                    if m % 2 == 0:
                        P.maxidx(si[:, m // 2, 8:16], sv[:, m, 8:16], s2w[:, m - m0, :], r=[f"sv#{m}b", f"s2w#{m - m0}"],
                                 w=[f"si#{m}b"])
            svn = [f"sv#{m}{x}" for m in range(16) for x in "ab"]
            sin = [f"si#{m}{x}" for m in range(0, 16, 2) for x in "ab"]
            for h in range(8):
                cd, cd2 = cand[h % 2], cand2[h % 2]
                cdf = cd.rearrange("p a b -> p (a b)")
                P.tt("dve", cd, bc(sv[:, 2 * h, :].unsqueeze(2), [128, 16, 16]),
                     bc(sv[:, 2 * h + 1, :].unsqueeze(1), [128, 16, 16]), ALU.add,
                     r=[f"sv#{2 * h}a", f"sv#{2 * h}b", f"sv#{2 * h + 1}a", f"sv#{2 * h + 1}b"])
                P.max8(best[:, h, 0:8], cdf, w=[f"best#{h}a"])
                P.matchrep(cd2, best[:, h, 0:8], cdf, -1e30, r=[f"best#{h}a", P._nm(cd)])
                P.max8(best[:, h, 8:16], cd2, w=[f"best#{h}b"])
            bn = [f"best#{h}{x}" for h in range(8) for x in "ab"]
            sv4 = sv.rearrange("p (h c) k -> p h c k", c=2)
            sv1, sv2 = sv4[:, :, 0, :], sv4[:, :, 1, :]
            P.tt("dve", eb, best, bc(best[:, :, 0:1], [128, 8, 16]), ALU.subtract, r=bn)
            P.actf(eb, eb, AF.Exp)
            P.reduce(zz, eb, ALU.add)
            P.actf(zz, zz, AF.Ln)
            P.tt("dve", eb, sv1, bc(sv1[:, :, 0:1], [128, 8, 16]), ALU.subtract, r=svn + [P._nm(eb)])
            P.actf(e1, eb, AF.Exp)
            P.tt("dve", zz, zz, sv2[:, :, 0], ALU.add, r=svn + [P._nm(zz)])
            s2 = s_sb.rearrange("p (h c) k -> p h c k", c=2)[:, :, 1, :]
            s2wn = [f"s2w#{k}" for k in range(8)]
            P.tt("dve", e2a, s2, bc(zz.unsqueeze(2), [128, 8, 128]), ALU.subtract, w=s2wn)
            P.actf(e2, e2a, AF.Exp, r=s2wn)
            P.tt("dve", nthr, bc(best[:, :, 15:16], [128, 8, 16]), sv1, ALU.subtract, r=bn + svn)
            P.ts("dve", nthr, nthr, THR_TOL, None, ALU.add)
            P.copy("dve", si1f, si, r=sin)
            pst = psO[0][:, 0:128]
            P.tr(pst, si1f.rearrange("p h a -> p (h a)"), identf)
            P.copy("act", si1T, pst)
            for jb in range(128 // JB):
                par = jb % 2
                mk_, gb_, rw = Mk[par], Gb[par], Rw[par]
                js = slice(jb * JB, (jb + 1) * JB)
                P.tt("dve", mk_, bc(s2[:, :, js].unsqueeze(2), [128, 8, 16, JB]),
                     bc(nthr.unsqueeze(3), [128, 8, 16, JB]), ALU.is_ge)
                P.tt("pool", gb_, bc(e1.unsqueeze(3), [128, 8, 16, JB]),
                     bc(e2[:, :, js].unsqueeze(2), [128, 8, 16, JB]), ALU.mult)
                P.tt("dve", rw.rearrange("p q j -> p (q j)"), mk_.rearrange("p h a j -> p (h a j)"),
                     gb_.rearrange("p h a j -> p (h a j)"), ALU.mult)
                pn = f"psThalf{par}"
                for jj in range(JB):
                    P.tr(psTb[:, par * 8 + jj, :], rw[:, :, jj], identb, w=[pn])
                P.copy("act", RwT[:, js, :], psTb[:, par * 8:par * 8 + JB, :], r=[pn])
            for tb in range(16):
                o1 = O1[tb % 2]
                P.tt("dve", o1, bc(iota.unsqueeze(1), [128, 8, 128]),
                     bc(si1T[:, tb * 8:(tb + 1) * 8].unsqueeze(2), [128, 8, 128]), ALU.is_equal)
                pw = psL[tb % 2].rearrange("p (t i) -> p t i", i=128)
                for tq in range(8):
                    P.mm(pw[:, tq, :], RwT[:, :, tb * 8 + tq], o1[:, tq, :], True, True)
                t0 = tt * 128 + tb * 8
                P.copy("act", Wsb[:, :, t0:t0 + 8], pw.rearrange("p t i -> p i t"))
        SB.release(mq)
        NR = 4
        Ub = [SB.alloc(f"Ub{i}", [16, 128], BF16) for i in range(NR)]
        NRV = 8
        Vb = [SB.alloc(f"Vb{i}", [1024], BF16) for i in range(NRV)]
        gl = [SB.alloc(f"gl{i}", [384], BF16) for i in range(2)]
        xr1 = [SB.alloc(f"xr1{i}", [1024], F32) for i in range(3)]
        yo = [SB.alloc(f"yo{i}", [1024], F32) for i in range(3)]
        psY = [psT, psB0, psB1]
        for half in range(2):
            c0 = half * 1024

            def stage_a(ic):
                wn = f"W{g}_{ic}"
                vb = Vb[ic % NRV]
                if half == 0:
                    ub = Ub[ic % NR]
                    P.dma("sp", ub, ub16[ic].rearrange("p (a b) -> p a b", b=128), r=[f"ub16#{ic}"])
                    P.dma("sp", vb, vb16[ic * 128:(ic + 1) * 128, 0:1024], r=[f"vb16#{ic}"])
                    pa = psO[ic % 2]
                    for dc in range(DC):
                        P.mm(pa[:, 0:384], ub[:, dc, :], hcT[:, dc, :], dc == 0, dc == DC - 1)
                    P.actf(gl[ic % 2], pa[:, 0:384], AF.Gelu_apprx_tanh)
                    P.tt("dve", Wsb[:, ic, :], gl[ic % 2], Wsb[:, ic, :], ALU.mult, r=[P._nm(gl[ic % 2]), wn], w=[wn])
                else:
                    P.dma("sp", vb, vb16[ic * 128:(ic + 1) * 128, 1024:2048], r=[f"vb16#{ic}"])

            def stage_v(ic):
                wn = f"W{g}_{ic}"
                vb = Vb[ic % NRV]
                for tt in range(3):
                    for nb in range(2):
                        P.mm(psY[tt][:, nb * 512:(nb + 1) * 512], Wsb[:, ic, tt * 128:(tt + 1) * 128],
                             vb[:, nb * 512:(nb + 1) * 512], ic == 0, ic == 127, r=[wn, P._nm(vb)])

            stage_a(0)
            for ic in range(128):
                if ic + 1 < 128:
                    stage_a(ic + 1)
                stage_v(ic)
            for tt, ti in enumerate(tiles):
                _, yout = tile_rows(ti)
                P.dma("sp", xr1[tt], yout[:, c0:c0 + 1024], r=[f"y#{ti}"])
                P.tt("dve", yo[tt], psY[tt], xr1[tt], ALU.add)
                P.dma("sp", yout[:, c0:c0 + 1024], yo[tt], w=[f"y#{ti}#{half}"])

    for g in range(3):
        tiles = [3 * g, 3 * g + 1, 3 * g + 2]
        mG = SB.mark()
        hcT = SB.alloc("hcT", [16, 384], BF16)
        mAB = SB.mark()
        xres = [SB.alloc(f"xres{i}", [D], F32) for i in range(3)]
        mixT = SB.alloc("mixT", [16, 384], BF16)
        hnTg = [SB.alloc(f"hnTg{i}", [16, 128], BF16) for i in range(3)]
        stats = SB.alloc("stats", [3, 2], F32)
        mA1 = SB.mark()
        wqkv = SB.alloc("wqkv", [16, 1536], BF16)
        wqn = load_w16(wqkv, "w_in", 0, 1536, f"wqkv{g}")
        NS = alloc_norm_scr()
        A = alloc_attn_scr()
        for j, ti in enumerate(tiles):
            xin, _ = tile_rows(ti)
            P.dma("sp", xres[j], xin)
            norm_T(xres[j], gmixT, hnTg[j], NS, psT, "a")
            prev = state["kvslot"]
            cur = 1 - prev
            qkv_tile(hnTg[j], wqkv, wqn, A, psBig, psT, 1 + ti, cur)
            mdst = mixT[:, 0:8, j * 128:(j + 1) * 128]
            if ti < NOWN:
                mprev = amask[:, 0, :] if ti == 0 else amask[:, 1, :]
                kbl = [(KT[prev], vaug[prev], mprev), (KT[cur], vaug[cur], amask[:, 2, :])]
                attn_core(A, kbl, psL, psO, psT, mdst, stats[:, j, 0:1])
                if ti == NOWN - 1:
                    P.dma("sp", O["kp"], A["qkr"][:, 16:20, :].rearrange("p h d -> p (h d)"))
                    P.dma("sp", O["vp"], A["qkv"][:, 1280:1536])
            else:
                ckb = SB.alloc("ckb", [16, 256], BF16)
                cvb = ckb
                KTc = SB.alloc("KTc", [16, 2, 128], BF16)
                cva = SB.alloc("cva", [16, 4, 65], BF16)
                smask = SB.alloc("smask", [17, 128], BF16)
                P.dma("pool", ckb, I["ck"].rearrange("b s d -> s b d"))
                P.dma("pool", smask, I["smask"])
                P.dma("sp", O["ks"][:, 0:120, :], I["ck"][:, 8:128, :], w=["ks#bulk"])
                P.dma("sp", O["vs"][:, 0:120, :], I["cv"][:, 8:128, :], w=["vs#bulk"])
                pbb = psBig.bitcast(BF16).rearrange("p (c t) -> p c t", t=128)
                for b in range(16):
                    for gp in range(2):
                        P.tr(pbb[:, b * 2 + gp, :], ckb[:, b, gp * 128:(gp + 1) * 128], identb)
                KTf = KTc.rearrange("p b g s -> p (b g) s")
                P.copy("dve", KTf[:, 0:16, :], pbb[:, 0:16, :])
                P.copy("act", KTf[:, 16:32, :], pbb[:, 16:32, :])
                P.dma("pool", cvb, I["cv"].rearrange("b s d -> s b d"))
                P.memset("dve", cva, 1.0)
                P.copy("act", cva[:, :, :, 0:64], cvb.rearrange("p b (h d) -> p b h d", d=64))
                kbl = [(KTc[:, b, :, :], cva[:, b, :, :], smask[:, b, :]) for b in range(16)]
                kbl.append((KT[cur], vaug[cur], smask[:, 16, :]))
                attn_core(A, kbl, psL, psO, psT, mdst, stats[:, j, 0:1],
                          po_banks=[psO[0], psO[1], psT[:, 0:512], psT[:, 512:1024]])
                for b in range(16):
                    P.dma("sp", O["ks"][b, 120:128, :], A["qkr"][b * 8:(b + 1) * 8, 16:20, :].rearrange("p h d -> p (h d)"),
                          w=[f"ks#{b}"])
                    P.dma("sp", O["vs"][b, 120:128, :], A["qkv"][b * 8:(b + 1) * 8, 1280:1536], w=[f"vs#{b}"])
            state["kvslot"] = cur
            emit_conv(0)
        SB.release(mA1)
        wlru = SB.alloc("wlru", [16, 2048], BF16)
        wln = load_w16(wlru, "w_in", 1536, 3584, f"wlru{g}")
        LS = alloc_lru_scr()
        streams = []
        for j, ti in enumerate(tiles):
            mdst = mixT[:, 8:16, j * 128:(j + 1) * 128]
            if ti >= NOWN:
                sct = SB.alloc("sct", [1024], F32)
                sht = SB.alloc("sht", [1024], F32)
            P.begin_stream()
            if ti < NOWN:
                outs = (O["cp"], O["hp"]) if ti == NOWN - 1 else None
                lru_tile("own", hnTg[j], wlru, wln, LS, psBig, psO[0], mdst, stats[:, j, 1:2], outs=outs,
                         psG=psT, psOut=psT)
            else:
                P.dma("sp", sct[0:48, :], I["sconv"])
                P.dma("sp", sht[0:16, :], I["sh"])
                lru_tile("smp", hnTg[j], wlru, wln, LS, psBig, psO[0], mdst, stats[:, j, 1:2], outs=(O["cs"], O["hs"]),
                         sample_in=(sct[0:48, :], sht[0:16, :]), psG=psT, psOut=psT)
            streams.append(P.end_stream())
        P.commit_spread(streams)
        SB.release(mA1)
        wout = SB.alloc("wout", [16, 2048], BF16)
        won = load_w16(wout, "w_out", 0, 2048, f"wout{g}")
        NS = alloc_norm_scr()
        P.actf(stats[:, :, 1], stats[:, :, 1], AF.Sqrt, bias=epsc, scale=1.0 / 1024)
        P.recip(stats[:, :, 1], stats[:, :, 1])
        for j, ti in enumerate(tiles):
            for nb in range(4):
                pb = psL[nb % 2]
                for half in range(2):
                    for c8 in range(8):
                        cc = half * 8 + c8
                        P.mm(pb[:, half * 512:(half + 1) * 512], mixT[:, cc, j * 128:(j + 1) * 128],
                             wout[:, cc, nb * 512:(nb + 1) * 512], c8 == 0, c8 == 7, r=[won.at(cc, nb * 512), P._nm(mixT)])
                xs_ = xres[j][:, nb * 512:(nb + 1) * 512]
                P.stt(xs_, pb[:, 0:512], stats[:, j, 0:1], xs_, ALU.mult, ALU.add)
                P.stt(xs_, pb[:, 512:1024], stats[:, j, 1:2], xs_, ALU.mult, ALU.add)
            _, yout = tile_rows(ti)
            P.dma("sp", yout, xres[j], w=[f"y#{ti}"])
            emit_conv(0)
            norm_T(xres[j], gffnT, hcT[:, :, j * 128:(j + 1) * 128], NS, psT, "b")
        SB.release(mAB)
        if stage == "mixer":
            SB.release(mG)
            continue
        emit_conv(len(conv_jobs))
        peer_group(g, tiles, hcT)
        SB.release(mG)

    P.emit(es)
    P.simulate()
    es.close()
    print(f"[build] ops={len(P.ops)} sbuf_peak={SB.peak / 1024:.1f} KiB")
    return nc


def _rope_table(pos):
    half = 32
    inv = np.exp(np.float32(-math.log(10000.0)) * np.arange(half, dtype=np.float32) * np.float32(2.0 / 64)).astype(np.float32)
    ang = (pos.astype(np.float32)[:, None] * inv[None, :]).astype(np.float32)
    return np.concatenate([np.cos(ang), np.sin(ang)], axis=-1).astype(np.float32)


def _fm(v, nchunk):
    return np.ascontiguousarray(np.asarray(v, np.float32).reshape(nchunk, 128).T)


def _prep(inp):
    f = lambda a: np.ascontiguousarray(np.asarray(a, dtype=np.float32))
    xp_all, xs_all = f(inp["x_prompt"]), f(inp["x_sample"])
    w_in, w_out, w_pq = f(inp["w_in"][0]), f(inp["w_out"][0]), f(inp["w_peer_q"][0])
    uT = np.ascontiguousarray(f(inp["peer_u"][0]).reshape(128, 128, 16, 128).transpose(0, 3, 2, 1))
    pv = f(inp["peer_v"][0])
    skT = np.ascontiguousarray(f(inp["peer_sub_keys"][0]).transpose(3, 0, 1, 2).reshape(128, 16, 128))
    gq, gk = f(inp["q_norm_g"][0]), f(inp["k_norm_g"][0])
    gqk = np.ascontiguousarray(np.broadcast_to(np.concatenate([np.tile(gq, 16), np.tile(gk, 4)])[None, :], (128, 1280)))
    sinks = np.ascontiguousarray(np.broadcast_to(f(inp["attn_sinks"][0])[None, :], (128, 16)))
    cw = np.ascontiguousarray(f(inp["conv_w"][0]).T.reshape(8, 128, 4).transpose(1, 0, 2))

    def bd(w):
        w = f(w)
        out = np.zeros((128, 8, 128), np.float32)
        for cc in range(8):
            for nl in range(2):
                out[nl * 64:(nl + 1) * 64, cc, nl * 64:(nl + 1) * 64] = w[2 * cc + nl]
        return out
    s_idx = np.arange(128)[:, None]
    q_idx = np.arange(128)[None, :]
    m_prev = (s_idx > q_idx).astype(np.float32)
    m_cur = (s_idx <= q_idx).astype(np.float32)
    smask = np.zeros((128, 17, 128), np.float32)
    tb, tp = q_idx // 8, q_idx % 8
    for b in range(16):
        smask[:, b, :] = ((tb == b) & (s_idx > tp)).astype(np.float32)
    smask[:, 16, :] = ((tb == (s_idx // 8)) & ((s_idx % 8) <= tp)).astype(np.float32)
    common = dict(
        w_in=w_in, w_out=w_out, w_pq=w_pq, uT=uT, pv=pv, skT=skT,
        gmixT=_fm(inp["norm_mix_g"][0], 16), gffnT=_fm(inp["norm_ffn_g"][0], 16),
        gaoT=_fm(inp["attn_out_g"][0], 8), gloT=_fm(inp["lru_out_g"][0], 8),
        gqk=gqk, sinks=sinks, cw=cw, cb=_fm(inp["conv_b"][0], 8),
        ba=_fm(f(inp["b_rec_gate"][0]).reshape(-1), 8), bx=_fm(f(inp["b_in_gate"][0]).reshape(-1), 8),
        lam=_fm(inp["lru_lambda"][0], 8), bda=bd(inp["w_rec_gate"][0]), bdx=bd(inp["w_in_gate"][0]),
        identf=np.eye(128, dtype=np.float32),
        iota=np.ascontiguousarray(np.broadcast_to(np.arange(128, dtype=np.float32)[None, :], (128, 128))),
        smask=smask,
    )
    ck, cv = f(inp["cache_k"][0]).reshape(128, 128, 256), f(inp["cache_v"][0]).reshape(128, 128, 256)
    sconv, sh = f(inp["state_conv"][0]), f(inp["state_h"][0])
    maps = []
    for c in range(NCORES):
        seq, k = divmod(c, 4)
        xpre = np.zeros((NPRE * 128, D), np.float32)
        if k > 0:
            xpre[NPRE * 128 - 1024 * k:] = xp_all[seq, 0:1024 * k]
        pmask = np.zeros((128, NPRE), np.float32)
        pmask[:, NPRE - 8 * k:] = 1.0
        rope = np.zeros((128, 10, 64), np.float32)
        p = np.arange(128)
        rope[:, 0] = _rope_table(np.maximum(k * 1024 - 128 + p, 0))
        for ti in range(8):
            rope[:, 1 + ti] = _rope_table(k * 1024 + ti * 128 + p)
        rope[:, 9] = _rope_table(8192 + (p % 8))
        amask = np.stack([m_prev if k > 0 else np.zeros_like(m_prev), m_prev, m_cur], axis=1)
        m = dict(common)
        m.update(
            xp=np.ascontiguousarray(xp_all[seq, 1024 * k:1024 * (k + 1)]),
            xs=np.ascontiguousarray(xs_all[16 * c:16 * (c + 1)].reshape(128, D)),
            xpre=xpre, pmask=pmask, rope=rope, amask=np.ascontiguousarray(amask),
            ck=np.ascontiguousarray(ck[16 * c:16 * (c + 1)]), cv=np.ascontiguousarray(cv[16 * c:16 * (c + 1)]),
            sconv=np.ascontiguousarray(sconv[16 * c:16 * (c + 1)].reshape(48, 1024)),
            sh=np.ascontiguousarray(sh[16 * c:16 * (c + 1)]),
        )
        maps.append(m)
    return maps


def _assemble(res):
    y_prompt = np.zeros((2, 4096, D), np.float32)
    y_sample = np.zeros((128, 8, D), np.float32)
    k_prompt = np.zeros((1, 2, 128, 4, 64), np.float32)
    v_prompt = np.zeros((1, 2, 128, 4, 64), np.float32)
    conv_prompt = np.zeros((1, 2, 3, 1024), np.float32)
    h_prompt = np.zeros((1, 2, 1024), np.float32)
    k_sample = np.zeros((1, 128, 128, 4, 64), np.float32)
    v_sample = np.zeros((1, 128, 128, 4, 64), np.float32)
    conv_sample = np.zeros((1, 128, 3, 1024), np.float32)
    h_sample = np.zeros((1, 128, 1024), np.float32)
    for c in range(NCORES):
        r = res[c]
        seq, k = divmod(c, 4)
        y_prompt[seq, 1024 * k:1024 * (k + 1)] = r["yp"]
        y_sample[16 * c:16 * (c + 1)] = r["ys"].reshape(16, 8, D)
        if k == 3:
            k_prompt[0, seq] = r["kp"].reshape(128, 4, 64)
            v_prompt[0, seq] = r["vp"].reshape(128, 4, 64)
            conv_prompt[0, seq] = r["cp"]
            h_prompt[0, seq] = r["hp"][0]
        k_sample[0, 16 * c:16 * (c + 1)] = r["ks"].reshape(16, 128, 4, 64)
        v_sample[0, 16 * c:16 * (c + 1)] = r["vs"].reshape(16, 128, 4, 64)
        conv_sample[0, 16 * c:16 * (c + 1)] = r["cs"].reshape(16, 3, 1024)
        h_sample[0, 16 * c:16 * (c + 1)] = r["hs"]
    return (y_prompt, y_sample, k_prompt, v_prompt, conv_prompt, h_prompt, k_sample, v_sample, conv_sample, h_sample)


_STAGE = "full"


def kernel(**inputs):
    maps = _prep(inputs)
    nc = build(_STAGE)
    res = run_bass_kernel_spmd(nc, maps, core_ids=list(range(NCORES)))
    return _assemble(res.results)
```
